# Optimizing a Trainium2 kernel written in Bass

```python
import jax, jax.numpy as jnp
from jax import lax
import numpy as np

D_MODEL = 1024
BATCH = 1
SEQ = 16384
DEPTH = 1

HEAD_DIM = 64
N_FOX_HEADS = 8
N_DIL_HEADS = 8
FOX_WIDTH = N_FOX_HEADS * HEAD_DIM
DIL_WIDTH = N_DIL_HEADS * HEAD_DIM
MIX_WIDTH = FOX_WIDTH + DIL_WIDTH
DILATED_PATTERNS = ((128, 1), (512, 4), (2048, 16))
Q_BLOCK = 128
N_MEM = 256
N_MEM_HEADS = 4
MEM_WIDTH = N_MEM_HEADS * HEAD_DIM
N_BUCKETS = 32
MAX_DISTANCE = 2048
D_FF = -(-8 * D_MODEL // (3 * 256)) * 256
RMS_EPS = 1e-6
IN_SIZES = (FOX_WIDTH, FOX_WIDTH, FOX_WIDTH, N_FOX_HEADS, DIL_WIDTH, DIL_WIDTH, DIL_WIDTH)
IN_COLS = sum(IN_SIZES)

kernel_name = "hybrid_fox_dilated_memxattn_swiglu"


def rmsnorm(x, g):
    xf = x.astype(jnp.float32)
    y = xf * lax.rsqrt(jnp.mean(xf * xf, axis=-1, keepdims=True) + RMS_EPS)
    return (y * g.astype(jnp.float32)).astype(x.dtype)


def split_heads(t, n):
    B, S, _ = t.shape
    return t.reshape(B, S, n, HEAD_DIM).transpose(0, 2, 1, 3)


def merge_heads(t):
    B, H, S, Dh = t.shape
    return t.transpose(0, 2, 1, 3).reshape(B, S, H * Dh)


def t5_causal_bucket(dist):
    max_exact = N_BUCKETS // 2
    d = np.maximum(dist, 1).astype(np.float32)
    large = max_exact + (np.log(d / max_exact) / np.log(MAX_DISTANCE / max_exact)
                         * (N_BUCKETS - max_exact)).astype(np.int32)
    large = np.minimum(large, N_BUCKETS - 1)
    return np.where(dist < max_exact, dist, large).astype(np.int32)


def forgetting_attention(q, k, v, log_f):
    B, H, S, Dh = q.shape
    nb = S // Q_BLOCK
    c = jnp.cumsum(log_f, axis=-1)
    qb = q.reshape(B, H, nb, Q_BLOCK, Dh).transpose(2, 0, 1, 3, 4)
    cb = c.reshape(B, H, nb, Q_BLOCK).transpose(2, 0, 1, 3)
    key_pos = jnp.arange(S)
    scale = Dh ** -0.5

    def block(args):
        qi, ci, i = args
        s = jnp.einsum('bhqd,bhkd->bhqk', qi, k).astype(jnp.float32) * scale
        s = s + ci[..., :, None] - c[..., None, :]
        q_pos = i * Q_BLOCK + jnp.arange(Q_BLOCK)
        s = jnp.where(key_pos[None, :] <= q_pos[:, None], s, -jnp.inf)
        p = jax.nn.softmax(s, axis=-1)
        return jnp.einsum('bhqk,bhkd->bhqd', p.astype(v.dtype), v)

    o = lax.map(block, (qb, cb, jnp.arange(nb)))
    return o.transpose(1, 2, 0, 3, 4).reshape(B, H, S, Dh)


def dilated_branch(q, k, v, rel_bias, window, dilation):
    B, H, S, Dh = q.shape
    w = window // dilation
    span = w * dilation
    S_pad = -(-S // span) * span
    L = S_pad // dilation
    nb = L // w

    def split(t):
        t = jnp.pad(t, ((0, 0), (0, 0), (0, S_pad - S), (0, 0)))
        t = t.reshape(B, H, L, dilation, Dh).transpose(0, 1, 3, 2, 4)
        return t.reshape(B, H, dilation, nb, w, Dh)

    def with_prev(t):
        prev = jnp.pad(t[:, :, :, :-1], ((0, 0), (0, 0), (0, 0), (1, 0), (0, 0), (0, 0)))
        return jnp.concatenate([prev, t], axis=4)

    qs = split(q)
    kk = with_prev(split(k))
    vv = with_prev(split(v))
    s = jnp.einsum('bhrnqd,bhrnkd->bhrnqk', qs, kk).astype(jnp.float32) * (Dh ** -0.5)

    qi = np.arange(w)[:, None]
    kj = np.arange(2 * w)[None, :]
    sub_dist = qi + w - kj
    band = (sub_dist >= 0) & (sub_dist <= w)
    bucket = t5_causal_bucket(np.clip(sub_dist, 0, w) * dilation)
    bias = rel_bias.astype(jnp.float32)[bucket]
    s = s + jnp.transpose(bias, (2, 0, 1))[None, :, None, None]
    not_first = (np.arange(nb)[:, None, None] > 0) | (kj[None] >= w)
    mask = jnp.asarray(band[None] & not_first)
    s = jnp.where(mask, s, -jnp.inf)

    m = jnp.max(s, axis=-1, keepdims=True)
    e = jnp.exp(s - m)
    l = jnp.sum(e, axis=-1, keepdims=True)
    o = jnp.einsum('bhrnqk,bhrnkd->bhrnqd', (e / l).astype(v.dtype), vv)
    lse = (m + jnp.log(l))[..., 0]

    def merge(t):
        t = t.reshape(B, H, dilation, L, *t.shape[5:])
        t = jnp.moveaxis(t, 2, 3)
        t = t.reshape(B, H, S_pad, *t.shape[4:])
        return t[:, :, :S]

    return merge(o), merge(lse)


def dilated_attention(q, k, v, rel_bias):
    outs, lses = [], []
    for window, dilation in DILATED_PATTERNS:
        o, lse = dilated_branch(q, k, v, rel_bias, window, dilation)
        outs.append(o)
        lses.append(lse)
    alpha = jax.nn.softmax(jnp.stack(lses, axis=0), axis=0)
    return jnp.einsum('pbhs,pbhsd->bhsd', alpha.astype(v.dtype), jnp.stack(outs, axis=0))


def memory_cross_attention(h, hm, w_xq, w_xk, w_xv, w_xo):
    q = split_heads(h @ w_xq, N_MEM_HEADS)
    k = split_heads(hm @ w_xk, N_MEM_HEADS)
    v = split_heads(hm @ w_xv, N_MEM_HEADS)
    s = jnp.einsum('bhqd,bhkd->bhqk', q, k).astype(jnp.float32) * (HEAD_DIM ** -0.5)
    p = jax.nn.softmax(s, axis=-1)
    o = jnp.einsum('bhqk,bhkd->bhqd', p.astype(v.dtype), v)
    return merge_heads(o) @ w_xo


def setup_inputs(seed: int = 0) -> dict:
    key = jax.random.key(seed)
    ks = jax.random.split(key, 24)
    nrm = jax.random.normal

    def w(k, shape, fan_in):
        return nrm(k, shape, jnp.float32) * fan_in ** -0.5

    def gain(k):
        return 1.0 + 0.1 * nrm(k, (DEPTH, D_MODEL), jnp.float32)

    return {
        "x": nrm(ks[0], (BATCH, SEQ, D_MODEL), jnp.float32),
        "mem": nrm(ks[1], (BATCH, N_MEM, D_MODEL), jnp.float32),
        "g_mix_pre": gain(ks[2]),
        "w_in": w(ks[3], (DEPTH, D_MODEL, IN_COLS), D_MODEL),
        "b_f": 2.0 + 0.5 * nrm(ks[4], (DEPTH, N_FOX_HEADS), jnp.float32),
        "rel_bias": 0.5 * nrm(ks[5], (N_BUCKETS, N_DIL_HEADS), jnp.float32),
        "w_out": w(ks[6], (DEPTH, MIX_WIDTH, D_MODEL), MIX_WIDTH),
        "g_mix_post": gain(ks[7]),
        "g_xattn_pre": gain(ks[8]),
        "g_mem": gain(ks[9]),
        "w_xq": w(ks[10], (DEPTH, D_MODEL, MEM_WIDTH), D_MODEL),
        "w_xk": w(ks[11], (DEPTH, D_MODEL, MEM_WIDTH), D_MODEL),
        "w_xv": w(ks[12], (DEPTH, D_MODEL, MEM_WIDTH), D_MODEL),
        "w_xo": w(ks[13], (DEPTH, MEM_WIDTH, D_MODEL), MEM_WIDTH),
        "g_xattn_post": gain(ks[14]),
        "g_ffn_pre": gain(ks[15]),
        "w_gate": w(ks[16], (DEPTH, D_MODEL, D_FF), D_MODEL),
        "w_up": w(ks[17], (DEPTH, D_MODEL, D_FF), D_MODEL),
        "w_down": w(ks[18], (DEPTH, D_FF, D_MODEL), D_FF),
        "g_ffn_post": gain(ks[19]),
    }


def reference(x, mem, g_mix_pre, w_in, b_f, rel_bias, w_out, g_mix_post,
              g_xattn_pre, g_mem, w_xq, w_xk, w_xv, w_xo, g_xattn_post,
              g_ffn_pre, w_gate, w_up, w_down, g_ffn_post):
    split_points = [int(p) for p in np.cumsum(IN_SIZES)[:-1]]
    for layer in range(DEPTH):
        h = rmsnorm(x, g_mix_pre[layer])
        proj = h @ w_in[layer]
        fq, fk, fv, fgate, dq, dk, dv = jnp.split(proj, split_points, axis=-1)
        log_f = jax.nn.log_sigmoid((fgate + b_f[layer]).astype(jnp.float32))
        log_f = log_f.transpose(0, 2, 1)
        o_fox = forgetting_attention(split_heads(fq, N_FOX_HEADS), split_heads(fk, N_FOX_HEADS),
                                     split_heads(fv, N_FOX_HEADS), log_f)
        o_dil = dilated_attention(split_heads(dq, N_DIL_HEADS), split_heads(dk, N_DIL_HEADS),
                                  split_heads(dv, N_DIL_HEADS), rel_bias)
        o = merge_heads(jnp.concatenate([o_fox, o_dil], axis=1))
        x = x + rmsnorm(o @ w_out[layer], g_mix_post[layer])

        h = rmsnorm(x, g_xattn_pre[layer])
        hm = rmsnorm(mem, g_mem[layer])
        y = memory_cross_attention(h, hm, w_xq[layer], w_xk[layer], w_xv[layer], w_xo[layer])
        x = x + rmsnorm(y, g_xattn_post[layer])

        h = rmsnorm(x, g_ffn_pre[layer])
        y = (jax.nn.silu(h @ w_gate[layer]) * (h @ w_up[layer])) @ w_down[layer]
        x = x + rmsnorm(y, g_ffn_post[layer])
    return x
```

```python
import numpy as np
from contextlib import ExitStack
import concourse.bass as bass
import concourse.mybir as mybir
from concourse.bass_utils import run_bass_kernel_spmd

F32 = mybir.dt.float32
BF16 = mybir.dt.bfloat16
AF = mybir.ActivationFunctionType
ALU = mybir.AluOpType

COMPUTE = ("pe", "act", "dve", "pool")
EPOCH = 16000
NEG = -30000.0
S = 16384
OWN = 2048
NCORE = 8
EPS = 1e-6
DFF = 2816
NJ = DFF // 128


class Buf:
    __slots__ = ("name", "last_writer", "readers", "dma_count", "sem", "inc_amt")

    def __init__(self, name):
        self.name = name
        self.last_writer = None
        self.readers = []
        self.dma_count = 0
        self.sem = None
        self.inc_amt = 16


class Op:
    __slots__ = ("idx", "eng", "fn", "deps", "is_dma", "key", "needs_signal", "sigval", "bar")

    def __init__(self, idx, eng, fn, is_dma, key):
        self.idx = idx
        self.eng = eng
        self.fn = fn
        self.deps = []
        self.is_dma = is_dma
        self.key = key
        self.needs_signal = False
        self.sigval = None
        self.bar = None


class MK:
    def __init__(self, nc):
        self.nc = nc
        self.ops = []
        self.streams = {"pe": [], "act": [], "dve": [], "pool": [], "sp": []}
        self.dma_keys = []
        self.pending_bar = {}

    def buf(self, name):
        return Buf(name)

    def barrier(self):
        last = [self.streams[e][-1] for e in COMPUTE if self.streams[e]]
        last = [o for o in last if not o.is_dma]
        lastc = []
        for e in COMPUTE:
            for o in reversed(self.streams[e]):
                if not o.is_dma:
                    lastc.append(o)
                    break
        for o in lastc:
            o.needs_signal = True
        dstate = [(k, k.dma_count) for k in self.dma_keys if k.dma_count > 0]
        for e in self.streams:
            self.pending_bar[e] = (lastc, dstate)

    def op(self, eng, fn, reads=(), writes=(), dma=False):
        key = None
        if dma:
            for h in writes:
                if not h.name.startswith("dram:"):
                    key = h
                    break
            if key is None:
                for h in reads:
                    if not h.name.startswith("dram:"):
                        key = h
                        break
            if key is None:
                key = reads[0] if reads else writes[0]
            if key not in self.dma_keys:
                self.dma_keys.append(key)
        o = Op(len(self.ops), eng, fn, dma, key)
        if eng in self.pending_bar:
            o.bar = self.pending_bar.pop(eng)
        deps = {}

        def add_dep(d):
            if d is None or d is o:
                return
            if (not d.is_dma) and d.eng == "pe" and eng == "pe" and not dma:
                return
            if d.is_dma:
                deps[d.idx] = (d, d.key.dma_count)
            else:
                deps[d.idx] = (d, None)
            d.needs_signal = True

        for h in reads:
            add_dep(h.last_writer)
        for h in writes:
            add_dep(h.last_writer)
            for r in h.readers:
                add_dep(r)
        if dma:
            key.dma_count += 1
        for h in writes:
            h.last_writer = o
            h.readers = []
        for h in reads:
            if h.last_writer is not o:
                h.readers.append(o)
        o.deps = list(deps.values())
        self.ops.append(o)
        self.streams[eng].append(o)
        return o

    def pe(self, fn, reads=(), writes=()):
        return self.op("pe", fn, reads, writes)

    def act(self, fn, reads=(), writes=()):
        return self.op("act", fn, reads, writes)

    def dve(self, fn, reads=(), writes=()):
        return self.op("dve", fn, reads, writes)

    def pool(self, fn, reads=(), writes=()):
        return self.op("pool", fn, reads, writes)

    def dma(self, fn, reads=(), writes=(), eng="sp"):
        return self.op(eng, fn, reads, writes, dma=True)

    def emit(self, final_ops=()):
        nc = self.nc
        nep = {}
        for e in COMPUTE:
            c = 0
            for o in self.streams[e]:
                if o.is_dma:
                    continue
                if o.needs_signal:
                    c += 1
                    o.sigval = ((c - 1) // EPOCH, (c - 1) % EPOCH + 1)
            nep[e] = max(c - 1, 0) // EPOCH + 1
        with ExitStack() as st:
            esem = {e: [st.enter_context(nc.semaphore(f"s_{e}{j}")) for j in range(nep[e])]
                    for e in COMPUTE}
            for i, k in enumerate(self.dma_keys):
                k.sem = st.enter_context(nc.semaphore(f"d_{i}"))
            block = st.enter_context(nc.Block())
            engobj = {"pe": "tensor", "act": "scalar", "dve": "vector", "pool": "gpsimd",
                      "sp": "sync"}
            fin = [(o.key, o.key.inc_amt * o.key.dma_count) for o in final_ops]

            def make_body(ename):
                stream = self.streams[ename]

                def body(eng):
                    known = {}

                    def wait(sem, val):
                        kid = id(sem)
                        if known.get(kid, 0) >= val:
                            return
                        known[kid] = val
                        eng.wait_ge(sem, val)

                    for o in stream:
                        if o.bar is not None:
                            lastc, dstate = o.bar
                            for d in lastc:
                                wait(esem[d.eng][d.sigval[0]], d.sigval[1])
                            for (k, cnt) in dstate:
                                wait(k.sem, k.inc_amt * cnt)
                        for (d, dv) in o.deps:
                            if d.is_dma:
                                wait(d.key.sem, d.key.inc_amt * dv)
                            else:
                                wait(esem[d.eng][d.sigval[0]], d.sigval[1])
                        ins = o.fn(eng)
                        if o.is_dma:
                            ins.then_inc(o.key.sem, o.key.inc_amt)
                        elif o.needs_signal:
                            ins.then_inc(esem[o.eng][o.sigval[0]], 1)
                    if ename == "sp":
                        for (k, v) in fin:
                            eng.wait_ge(k.sem, v)
                return body

            for ename in ("sp", "pe", "act", "dve", "pool"):
                if not self.streams[ename] and not (ename == "sp" and fin):
                    continue
                getattr(block, engobj[ename])(make_body(ename))


class Arena:
    def __init__(self, t, nfloats):
        self.t = t
        self.n = nfloats
        self.off = 0

    def alloc(self, nelem, dt=F32):
        nb = nelem * (4 if dt == F32 else 2)
        nf = ((nb + 3) // 4 + 15) // 16 * 16
        ap = self.t[:, self.off:self.off + nf]
        self.off += nf
        assert self.off <= self.n, ("SBUF arena overflow", self.off, self.n)
        if dt != F32:
            ap = ap.bitcast(dt)
        return ap[:, 0:nelem]


def build_program(debug=False):
    nc = bass.Bass("TRN2", target_bir_lowering=False)
    mk = MK(nc)

    def din(name, shape, dt=F32):
        return nc.dram_tensor(name, shape, dt, kind="ExternalInput").ap()

    xT = din("xT", [1024, S])
    x_own = din("x_own", [OWN, 1024])
    memx = din("memx", [256, 1024])
    keymask_d = din("keymask", [128, 128])
    halomask_d = din("halomask", [128, 1])
    w_a = din("w_a", [1024, 3080])
    gpre_d = din("gpre", [4, 128, 8])
    bf_d = din("bf", [8, 1])
    dbiasc_d = din("dbiasc", [8, 3, 128, 512])
    dbiasp_d = din("dbiasp", [8, 3, 128, 128])
    negm_d = din("negm", [4, 128, 512])
    ident_d = din("ident", [128, 128])
    msel_d = din("msel", [128, 3])
    w_out_d = din("w_out", [1024, 1024])
    w_xq_d = din("w_xq", [1024, 256])
    w_xk_d = din("w_xk", [1024, 256])
    w_xv_d = din("w_xv", [1024, 256])
    w_xo_d = din("w_xo", [256, 1024])
    w_gate_d = din("w_gate", [1024, DFF])
    w_up_d = din("w_up", [1024, DFF])
    w_down_d = din("w_down", [DFF, 1024])
    gpost_d = din("gpost", [3, 128, 1024])
    out_d = nc.dram_tensor("out", [OWN, 1024], F32, kind="ExternalOutput").ap()

    skind = "ExternalOutput" if debug else "Internal"

    def dscr(name, shape, dt):
        if debug:
            return nc.dram_tensor(name, shape, dt, kind="ExternalOutput").ap()
        return nc.dram_tensor(name, shape, dt).ap()

    KT_s = dscr("KT_s", [8, 64, S], BF16)
    V_s = dscr("V_s", [8, 128, 128 * 128], BF16)
    QT_s = dscr("QT_s", [8, 64, OWN], BF16)
    C_s = dscr("C_s", [8, OWN], F32)
    DQ_s = dscr("DQ_s", [8, 64, OWN], BF16)
    DK_s = dscr("DK_s", [8, 64, 2 * OWN], BF16)
    DV_s = dscr("DV_s", [8, 64, 2 * OWN], BF16)
    Wout_b = dscr("Wout_b", [1024, 1024], BF16)
    Wxq_b = dscr("Wxq_b", [1024, 256], BF16)
    Wxk_b = dscr("Wxk_b", [1024, 256], BF16)
    Wxv_b = dscr("Wxv_b", [1024, 256], BF16)
    Wxo_b = dscr("Wxo_b", [256, 1024], BF16)
    Wg_b = dscr("Wg_b", [NJ, 128, 1024], BF16)
    Wu_b = dscr("Wu_b", [NJ, 128, 1024], BF16)
    Wd_b = dscr("Wd_b", [NJ, 128, 1024], BF16)
    if debug:
        OT_dbg = nc.dram_tensor("OT_dbg", [128, 16 * 1024], BF16, kind="ExternalOutput").ap()
        NC_dbg = nc.dram_tensor("NC_dbg", [128, 1024], F32, kind="ExternalOutput").ap()

    dbufs = {}

    def DB(name, idx=0):
        k = (name, idx)
        if k not in dbufs:
            dbufs[k] = mk.buf(f"dram:{name}{idx}")
        return dbufs[k]

    with ExitStack() as st:
        ARENA_F = 49 * 1024
        arena_t = st.enter_context(nc.sbuf_tensor("arena", [128, ARENA_F], F32))
        ar = Arena(arena_t, ARENA_F)
        PSALL = st.enter_context(nc.psum_tensor("psall", [128, 4096], F32))
        PSB = [PSALL[:, 512 * i:512 * (i + 1)] for i in range(8)]

        def B(name):
            return mk.buf(name)

        OTall = ar.alloc(8 * OWN, BF16).rearrange("p (k n) -> p k n", n=OWN)
        bOT = [B(f"OTall{k}") for k in range(8)]
        IDF = ar.alloc(128, F32)
        IDB = ar.alloc(128, BF16)
        ONESB = ar.alloc(128, BF16)
        MSEL = ar.alloc(3, F32)
        KMASK = ar.alloc(128, F32)
        HMASK = ar.alloc(1, F32)
        GPRE = ar.alloc(32, F32).rearrange("p (g k) -> p g k", k=8)
        NBF = ar.alloc(1, F32)
        ONES8 = ar.alloc(512, F32)
        bC = B("consts")

        mk.dma(lambda e: e.dma_start(out=IDF, in_=ident_d), writes=[bC])
        mk.dma(lambda e: e.dma_start(out=MSEL, in_=msel_d), writes=[bC])
        mk.dma(lambda e: e.dma_start(out=KMASK, in_=keymask_d), writes=[bC])
        mk.dma(lambda e: e.dma_start(out=HMASK, in_=halomask_d), writes=[bC])
        mk.dma(lambda e: e.dma_start(out=GPRE, in_=gpre_d.rearrange("g p k -> p g k")), writes=[bC])
        mk.dma(lambda e: e.dma_start(out=NBF[0:8, :], in_=bf_d), writes=[bC])
        bC2 = B("consts2")
        mk.dve(lambda e: e.tensor_copy(out=IDB, in_=IDF), reads=[bC], writes=[bC2])
        mk.dve(lambda e: e.memset(ONESB, 1.0), writes=[bC2])
        mk.dve(lambda e: e.memset(ONES8, 1.0), writes=[bC2])
        mk.dve(lambda e: e.tensor_scalar(out=NBF[0:8, :], in0=NBF[0:8, :], scalar1=-1.0, scalar2=None,
                                         op0=ALU.mult), reads=[bC], writes=[bC, bC2])
        markD = ar.off
        negc_all = ar.alloc(128 * 8, F32).rearrange("p (t h) -> p t h", h=8)
        bNC = [B(f"negc{b}") for b in range(32)]
        persist_mark = ar.off

        WA = ar.alloc(8 * 3080, BF16).rearrange("p (k c) -> p k c", c=3080)
        bWAc = [B(f"WA{i}") for i in range(7)]
        XTraw = [ar.alloc(8 * 512, F32) for _ in range(2)]
        XT = [a.rearrange("p (k n) -> p k n", n=512) for a in XTraw]
        bXT = [B("XT0"), B("XT1")]
        w_a_v = w_a.rearrange("(k p) c -> p k c", p=128)

        XSQ = ar.alloc(8 * 512, BF16).rearrange("p (k n) -> p k n", n=512)
        bXSQ = B("XSQ")
        HT = [ar.alloc(8 * 512, BF16).rearrange("p (k n) -> p k n", n=512) for _ in range(2)]
        bHT = [B("HT0"), B("HT1")]
        LN = ar.alloc(512, F32)
        bLN = B("LN")
        RSTD = ar.alloc(512, F32)
        bRSTD = B("RSTD")
        EST = [ar.alloc(512, BF16) for _ in range(3)]
        bEST = [B(f"EST{i}") for i in range(3)]
        VT = [ar.alloc(512, BF16) for _ in range(2)]
        bVT = [B("VT0"), B("VT1")]
        VST = ar.alloc(8 * 8 * 128, BF16).rearrange("p (h t d) -> p h t d", h=8, t=8)
        bVST = B("VST")
        GE = ar.alloc(512, F32)
        bGE = B("GE")
        GSP = ar.alloc(512, F32)
        bGSP = B("GSP")
        CS = [ar.alloc(512, F32) for _ in range(2)]
        bCS = [B("CS0"), B("CS1")]
        NCS = ar.alloc(512, F32)
        bNCS = B("NCS")


        SS_ps, bSS = PSB[0], B("SS_ps")
        PR_ps = [PSB[1], PSB[2], PSB[3]]
        bPR = [B("PR0"), B("PR1"), B("PR2")]
        TPb_ps = PSB[4].bitcast(BF16)
        bTPb = B("TPb")
        TPc_ps, bTPc = PSB[5], B("TPc")

        xT_v = xT.rearrange("(k p) n -> p k n", p=128)
        KT_v = KT_s.rearrange("h d n -> (h d) n")
        QT_v = QT_s.rearrange("h d n -> (h d) n")
        DQ_v = DQ_s.rearrange("h d n -> (h d) n")
        DK_v = DK_s.rearrange("h d n -> (h d) n")
        DV_v = DV_s.rearrange("h d n -> (h d) n")
        V_v = V_s.rearrange("h p n -> p h n")
        state = {"pr": 0, "est": 0}

        def project(hs, col0, M, rd_extra=()):
            k = state["pr"] % 3
            state["pr"] += 1
            for kc in range(8):
                mk.pe(lambda e, kc=kc, k=k: e.matmul(PR_ps[k][0:M, :], WA[:, kc, col0:col0 + M], HT[hs][:, kc, :],
                                                       start=(kc == 0), stop=(kc == 7)),
                      reads=bWAc[col0 // 440:(col0 + M - 1) // 440 + 1] + [bHT[hs]], writes=[bPR[k]])
            return k

        def evac_store(k, dst_ap, dst_buf, scale=1.0):
            s = state["est"] % 3
            state["est"] += 1
            mk.act(lambda e: e.activation(out=EST[s], in_=PR_ps[k], func=AF.Copy, scale=scale),
                   reads=[bPR[k]], writes=[bEST[s]])
            mk.dma(lambda e: e.dma_start(out=dst_ap, in_=EST[s]), reads=[bEST[s]], writes=[dst_buf], eng="act")

        NBLK = S // 512

        def xload(b):
            xs = b % 2
            mk.dma(lambda e: e.dma_start(out=XT[xs], in_=xT_v[:, :, 512 * b:512 * (b + 1)]), writes=[bXT[xs]])

        def stats(b):
            xs = b % 2
            hs = b % 2
            mk.pool(lambda e: e.tensor_tensor(out=XSQ, in0=XT[xs], in1=XT[xs], op=ALU.mult),
                    reads=[bXT[xs]], writes=[bXSQ])
            for kc in range(8):
                mk.pe(lambda e, kc=kc: e.matmul(SS_ps, ONESB, XSQ[:, kc, :], start=(kc == 0), stop=(kc == 7)),
                      reads=[bXSQ, bC2], writes=[bSS])
            mk.act(lambda e: e.activation(out=LN, in_=SS_ps, func=AF.Ln, bias=EPS, scale=1.0 / 1024),
                   reads=[bSS], writes=[bLN])
            mk.act(lambda e: e.activation(out=RSTD, in_=LN, func=AF.Exp, scale=-0.5), reads=[bLN], writes=[bRSTD])
            for kc in range(8):
                fn = (lambda e, kc=kc: e.tensor_tensor(out=HT[hs][:, kc, :], in0=XT[xs][:, kc, :], in1=RSTD, op=ALU.mult))
                if kc % 4 == 3:
                    mk.pool(fn, reads=[bXT[xs], bRSTD], writes=[bHT[hs]])
                else:
                    mk.dve(fn, reads=[bXT[xs], bRSTD], writes=[bHT[hs]])

        def cs_transposes(bb):
            cs = bb % 2
            for j in range(4):
                mk.pe(lambda e, j=j: e.transpose(TPc_ps[:, 8 * j:8 * j + 8], CS[cs][0:8, 128 * j:128 * (j + 1)],
                                                 IDF[0:8, 0:8]), reads=[bCS[cs], bC], writes=[bTPc])
            for j in range(4):
                kt = 4 * bb + j
                mk.dve(lambda e, j=j, kt=kt: e.tensor_scalar(out=negc_all[:, kt, :], in0=TPc_ps[:, 8 * j:8 * j + 8],
                                                             scalar1=KMASK[:, kt:kt + 1], scalar2=None, op0=ALU.add),
                       reads=[bTPc, bC], writes=[bNC[bb]])

        mk.pool(lambda e: e.memset(VST.rearrange("p h t d -> p (h t d)"), 1.0), writes=[bVST])
        xload(0)
        xload(1)
        stats(0)
        OTf = OTall.rearrange("p k n -> p (k n)").bitcast(F32)
        WST = [OTf[:, 4096 * i:4096 * i + 8 * 440].rearrange("p (k c) -> p k c", c=440) for i in range(2)]
        bWST = [B("WST0"), B("WST1")]
        for i in range(7):
            s = i % 2
            mk.dma(lambda e, i=i, s=s: e.dma_start(out=WST[s], in_=w_a_v[:, :, 440 * i:440 * (i + 1)]),
                   writes=[bWST[s]])
            for kc in range(8):
                if kc % 2 == 0:
                    mk.dve(lambda e, i=i, s=s, kc=kc: e.tensor_scalar(out=WA[:, kc, 440 * i:440 * (i + 1)], in0=WST[s][:, kc, :],
                                                                      scalar1=GPRE[:, 0, kc:kc + 1], scalar2=None, op0=ALU.mult),
                           reads=[bWST[s], bC], writes=[bWAc[i]])
                else:
                    mk.act(lambda e, i=i, s=s, kc=kc: e.activation(out=WA[:, kc, 440 * i:440 * (i + 1)], in_=WST[s][:, kc, :],
                                                                   func=AF.Copy, scale=GPRE[:, 0, kc:kc + 1]),
                           reads=[bWST[s], bC], writes=[bWAc[i]])
        for b in range(NBLK):
            xs = b % 2
            hs = b % 2
            if b + 1 < NBLK:
                stats(b + 1)
            if b + 2 < NBLK:
                xload(b + 2)
            if b >= 1:
                cs_transposes(b - 1)
            for i in range(4):
                k = project(hs, 128 * i, 128)
                evac_store(k, KT_v[128 * i:128 * (i + 1), 512 * b:512 * (b + 1)], DB("KT", b))
            def v_transposes(i, vs):
                for j in range(4):
                    mk.pe(lambda e, j=j, vs=vs: e.transpose(TPb_ps[:, 128 * j:128 * (j + 1)],
                                                            VT[vs][:, 128 * j:128 * (j + 1)], IDB),
                          reads=[bVT[vs], bC2], writes=[bTPb])
                t0 = 4 * (b % 2)
                mk.dve(lambda e, i=i, t0=t0: e.tensor_copy(
                    out=VST[:, 2 * i:2 * i + 2, t0:t0 + 4, 0:64],
                    in_=TPb_ps[:, 0:512].rearrange("p (t h d) -> p h t d", t=4, h=2)),
                    reads=[bTPb], writes=[bVST])
            for i in range(4):
                k = project(hs, 512 + 128 * i, 128)
                vs = i % 2
                mk.act(lambda e, k=k, vs=vs: e.activation(out=VT[vs], in_=PR_ps[k], func=AF.Copy),
                       reads=[bPR[k]], writes=[bVT[vs]])
                if i >= 1:
                    v_transposes(i - 1, (i - 1) % 2)
            k = project(hs, 3072, 8)
            v_transposes(3, 1)
            if b % 2 == 1:
                sb_ = b // 2
                mk.dma(lambda e, sb_=sb_: e.dma_start(out=V_v[:, :, 1024 * sb_:1024 * (sb_ + 1)],
                                                      in_=VST.rearrange("p h t d -> p h (t d)")),
                       reads=[bVST], writes=[DB("V", sb_)], eng="act")
            mk.act(lambda e, k=k: e.activation(out=GE[0:8, :], in_=PR_ps[k][0:8, :], func=AF.Exp,
                                               bias=NBF[0:8, :], scale=-1.0), reads=[bPR[k], bC2], writes=[bGE])
            mk.act(lambda e: e.activation(out=GSP[0:8, :], in_=GE[0:8, :], func=AF.Ln, bias=1.0, scale=1.0),
                   reads=[bGE], writes=[bGSP])
            cs = b % 2
            if b == 0:
                mk.dve(lambda e: e.tensor_tensor_scan(out=CS[0][0:8, :], data0=ONES8[0:8, :], data1=GSP[0:8, :],
                                                      initial=0.0, op0=ALU.mult, op1=ALU.add),
                       reads=[bGSP, bC2], writes=[bCS[0]])
            else:
                mk.dve(lambda e, cs=cs: e.tensor_tensor_scan(out=CS[cs][0:8, :], data0=ONES8[0:8, :],
                                                             data1=GSP[0:8, :], initial=CS[1 - cs][0:8, 511:512],
                                                             op0=ALU.mult, op1=ALU.add),
                       reads=[bGSP, bC2, bCS[1 - cs]], writes=[bCS[cs]])
            if b >= 28:
                ob = b - 28
                mk.dve(lambda e, cs=cs: e.tensor_scalar(out=NCS[0:8, :], in0=CS[cs][0:8, :], scalar1=-1.0, scalar2=None,
                                                        op0=ALU.mult), reads=[bCS[cs]], writes=[bNCS])
                mk.dma(lambda e, ob=ob: e.dma_start(out=C_s[:, 512 * ob:512 * (ob + 1)], in_=NCS[0:8, :]),
                       reads=[bNCS], writes=[DB("C", ob)], eng="act")
                for i in range(4):
                    k = project(hs, 1024 + 128 * i, 128)
                    evac_store(k, QT_v[128 * i:128 * (i + 1), 512 * ob:512 * (ob + 1)], DB("QT", ob), scale=0.125)
                for i in range(4):
                    k = project(hs, 1536 + 128 * i, 128)
                    evac_store(k, DQ_v[128 * i:128 * (i + 1), 512 * ob:512 * (ob + 1)], DB("DQ", ob), scale=0.125)
            if b >= 24:
                wb = b - 24
                for i in range(4):
                    k = project(hs, 2048 + 128 * i, 128)
                    evac_store(k, DK_v[128 * i:128 * (i + 1), 512 * wb:512 * (wb + 1)], DB("DK", wb))
                for i in range(4):
                    k = project(hs, 2560 + 128 * i, 128)
                    evac_store(k, DV_v[128 * i:128 * (i + 1), 512 * wb:512 * (wb + 1)], DB("DV", wb))

        cs_transposes(NBLK - 1)
        if debug:
            mk.dma(lambda e: e.dma_start(out=NC_dbg, in_=negc_all.rearrange("p t h -> p (t h)")), reads=bNC, eng="act")

        mk.barrier()
        ar.off = persist_mark
        KR = [ar.alloc(4096, BF16) for _ in range(4)]
        bKR = [B(f"KR{i}") for i in range(4)]
        VR = [ar.alloc(32 * 128, BF16).rearrange("p (t d) -> p t d", d=128) for _ in range(4)]
        bVR = [B(f"VR{i}") for i in range(4)]
        QTp2 = [ar.alloc(OWN, BF16) for _ in range(2)]
        bQTp2 = [B("QTp0"), B("QTp1")]
        CR = ar.alloc(OWN, F32)
        bCR = B("CR")
        T0b = ar.alloc(512, BF16); T1b = ar.alloc(512, BF16); T2b = ar.alloc(512, BF16)
        R1 = ar.alloc(512, F32); R2 = ar.alloc(512, F32); A0 = ar.alloc(512, F32); A1 = ar.alloc(512, F32)
        bSPL = B("split")
        NEGMB = ar.alloc(4 * 512, BF16).rearrange("p (j n) -> p j n", n=512)
        NEGMF = ar.alloc(512, F32)
        bNEGMF = B("NEGMF")
        bNEGM = B("NEGM")
        PT = [ar.alloc(1024, BF16) for _ in range(3)]
        bPT = [B(f"PT{i}") for i in range(3)]

        O_ps = [PSB[0], PSB[1], PSB[2], PSB[3]]
        bOp = [B(f"Op{i}") for i in range(4)]
        SP2 = [PSALL[:, 2048:3072], PSALL[:, 3072:4096]]
        bSP2 = [B("SP0"), B("SP1")]

        for j in range(4):
            mk.dma(lambda e, j=j: e.dma_start(out=NEGMF, in_=negm_d[j]), writes=[bNEGMF])
            mk.pool(lambda e, j=j: e.tensor_copy(out=NEGMB[:, j, :], in_=NEGMF), reads=[bNEGMF], writes=[bNEGM])
        ONES_PAIR = 1.0019378662109375
        for i in range(2):
            mk.dve(lambda e, i=i: e.memset(QTp2[i][64:128, :].bitcast(F32), 0.0), writes=[bQTp2[i]])
        for i in range(4):
            if i == 0:
                mk.dve(lambda e, i=i: e.memset(KR[i][64:128, :].bitcast(F32), 0.0), writes=[bKR[i]])
                mk.dve(lambda e, i=i: e.memset(KR[i][64:67, :].bitcast(F32), ONES_PAIR), writes=[bKR[i]])
            else:
                mk.pool(lambda e, i=i: e.memset(KR[i][64:128, :].bitcast(F32), 0.0), writes=[bKR[i]])
                mk.pool(lambda e, i=i: e.memset(KR[i][64:67, :].bitcast(F32), ONES_PAIR), writes=[bKR[i]])

        allKT = [DB("KT", b) for b in range(32)]
        allV = [DB("V", s_) for s_ in range(16)]
        allQT = [DB("QT", o) for o in range(4)]
        allC = [DB("C", o) for o in range(4)]
        WSb = [ar.alloc(1024, F32) for _ in range(3)]
        bWSb = [B(f"WSb{i}") for i in range(3)]
        WBb = [ar.alloc(1024, BF16) for _ in range(3)]
        bWBb = [B(f"WBb{i}") for i in range(3)]
        v3 = lambda a: a.rearrange("p (k c) -> p k c", c=128)
        precast = []
        for kc in range(8):
            precast.append((w_out_d[128 * kc:128 * (kc + 1), :], 1024, None, Wout_b[128 * kc:128 * (kc + 1), :], None))
        for (wd_, wb_, gi_) in ((w_xq_d, Wxq_b, 1), (w_xk_d, Wxk_b, 2), (w_xv_d, Wxv_b, 2)):
            for kc in range(8):
                precast.append((wd_[128 * kc:128 * (kc + 1), :], 256, None, wb_[128 * kc:128 * (kc + 1), :], (gi_, kc)))
        for c2 in range(2):
            precast.append((w_xo_d[128 * c2:128 * (c2 + 1), :], 1024, None, Wxo_b[128 * c2:128 * (c2 + 1), :], None))
        for j in range(NJ):
            precast.append((w_gate_d.rearrange("(k p) c -> p k c", p=128)[:, :, 128 * j:128 * (j + 1)], 1024, v3, Wg_b[j], (3, None)))
            precast.append((w_up_d.rearrange("(k p) c -> p k c", p=128)[:, :, 128 * j:128 * (j + 1)], 1024, v3, Wu_b[j], (3, None)))
            precast.append((w_down_d[128 * j:128 * (j + 1), :], 1024, None, Wd_b[j], None))
        pc_state = {"i": 0}

        def emit_precast(n):
            for _ in range(n):
                i = pc_state["i"]
                if i >= len(precast):
                    return
                pc_state["i"] += 1
                src_ap, ncols, view, dst_ap, gspec = precast[i]
                sl = i % 3
                stg = WSb[sl][:, 0:ncols]
                stg_v = view(stg) if view is not None else stg
                wb = WBb[sl][:, 0:ncols]
                mk.dma(lambda e, stg_v=stg_v, src_ap=src_ap: e.dma_start(out=stg_v, in_=src_ap), writes=[bWSb[sl]])
                if gspec is None:
                    mk.pool(lambda e, wb=wb, stg=stg: e.tensor_copy(out=wb, in_=stg), reads=[bWSb[sl]], writes=[bWBb[sl]])
                elif gspec[1] is not None:
                    gi_, kc_ = gspec
                    mk.pool(lambda e, wb=wb, stg=stg, gi_=gi_, kc_=kc_: e.tensor_scalar(
                        out=wb, in0=stg, scalar1=GPRE[:, gi_, kc_:kc_ + 1], scalar2=None, op0=ALU.mult),
                        reads=[bWSb[sl], bC], writes=[bWBb[sl]])
                else:
                    gi_ = gspec[0]
                    for kc_ in range(8):
                        mk.pool(lambda e, wb=wb, stg=stg, gi_=gi_, kc_=kc_: e.tensor_scalar(
                            out=wb[:, 128 * kc_:128 * (kc_ + 1)], in0=stg[:, 128 * kc_:128 * (kc_ + 1)],
                            scalar1=GPRE[:, gi_, kc_:kc_ + 1], scalar2=None, op0=ALU.mult),
                            reads=[bWSb[sl], bC], writes=[bWBb[sl]])
                mk.dma(lambda e, wb=wb, dst_ap=dst_ap: e.dma_start(out=dst_ap, in_=wb), reads=[bWBb[sl]],
                       writes=[DB("Wb", i)])

        P3 = slice(64, 67)

        def prep_head(h):
            QTp = QTp2[h % 2]
            bQ = bQTp2[h % 2]
            mk.dma(lambda e: e.dma_start(out=QTp[0:64, :], in_=QT_s[h]), reads=allQT, writes=[bQ])
            for r in range(3):
                mk.dma(lambda e, r=r: e.dma_start(out=CR[64 + r:65 + r, :], in_=C_s[h:h + 1, :]),
                       reads=allC, writes=[bCR])
            for q4 in range(4):
                cs_ = slice(512 * q4, 512 * (q4 + 1))
                mk.dve(lambda e, cs_=cs_: e.tensor_copy(out=T0b[P3, :], in_=CR[P3, cs_]), reads=[bCR], writes=[bSPL])
                mk.dve(lambda e, cs_=cs_: e.tensor_tensor(out=R1[P3, :], in0=CR[P3, cs_], in1=T0b[P3, :], op=ALU.subtract),
                       reads=[bCR, bSPL], writes=[bSPL])
                mk.dve(lambda e: e.tensor_copy(out=T1b[P3, :], in_=R1[P3, :]), reads=[bSPL], writes=[bSPL])
                mk.dve(lambda e: e.tensor_tensor(out=R2[P3, :], in0=R1[P3, :], in1=T1b[P3, :], op=ALU.subtract),
                       reads=[bSPL], writes=[bSPL])
                mk.dve(lambda e: e.tensor_copy(out=T2b[P3, :], in_=R2[P3, :]), reads=[bSPL], writes=[bSPL])
                mk.dve(lambda e: e.tensor_scalar(out=A0[P3, :], in0=T0b[P3, :], scalar1=MSEL[P3, 0:1], scalar2=None,
                                                 op0=ALU.mult), reads=[bSPL, bC], writes=[bSPL])
                mk.dve(lambda e: e.scalar_tensor_tensor(out=A1[P3, :], in0=T1b[P3, :], scalar=MSEL[P3, 1:2], in1=A0[P3, :],
                                                        op0=ALU.mult, op1=ALU.add), reads=[bSPL, bC], writes=[bSPL])
                mk.dve(lambda e, cs_=cs_: e.scalar_tensor_tensor(out=QTp[P3, cs_], in0=T2b[P3, :], scalar=MSEL[P3, 2:3],
                                                                 in1=A1[P3, :], op0=ALU.mult, op1=ALU.add),
                       reads=[bSPL, bC], writes=[bQ])

        def load_chunk(h, ci):
            mk.dma(lambda e: e.dma_start(out=KR[ci][0:64, :], in_=KT_s[h][:, 4096 * ci:4096 * (ci + 1)]),
                   reads=allKT, writes=[bKR[ci]])
            mk.dma(lambda e: e.dma_start(out=VR[ci].rearrange("p t d -> p (t d)"), in_=V_s[h][:, 4096 * ci:4096 * (ci + 1)]),
                   reads=allV, writes=[bVR[ci]])

        def last_kt(qb):
            return 115 + 4 * qb

        ust = {"sp": 0, "pt": 0}

        def unit_qk(h, kt, p):
            ci, kl = kt // 32, kt % 32
            QTp, bQ = QTp2[h % 2], bQTp2[h % 2]
            qbs = [qb for qb in (2 * p, 2 * p + 1) if kt <= last_kt(qb)]
            sl = ust["sp"] % 2
            ust["sp"] += 1
            for qb in qbs:
                diag = kt >= 112 + 4 * qb
                col = 512 * (qb % 2)
                mk.pe(lambda e, qb=qb, col=col, diag=diag: e.matmul(SP2[sl][:, col:col + 512], KR[ci][:, 128 * kl:128 * (kl + 1)],
                                                                   QTp[:, 512 * qb:512 * (qb + 1)], start=True, stop=not diag),
                      reads=[bKR[ci], bQ], writes=[bSP2[sl]])
                if diag:
                    j = kt - (112 + 4 * qb)
                    mk.pe(lambda e, col=col, j=j: e.matmul(SP2[sl][:, col:col + 512], IDB, NEGMB[:, j, :], start=False, stop=True),
                          reads=[bC2, bNEGM], writes=[bSP2[sl]])
            return (h, kt, p, qbs, sl)

        def unit_exp(u):
            h, kt, p, qbs, sl = u
            c0 = 512 * (qbs[0] % 2)
            c1 = 512 * (qbs[-1] % 2) + 512
            ps = ust["pt"] % 3
            ust["pt"] += 1
            mk.act(lambda e: e.activation(out=PT[ps][:, c0:c1], in_=SP2[sl][:, c0:c1], func=AF.Exp,
                                          bias=negc_all[:, kt, h:h + 1], scale=1.0),
                   reads=[bSP2[sl], bNC[kt // 4]], writes=[bPT[ps]])
            return ps

        def unit_pv(u, ps):
            h, kt, p, qbs, sl = u
            ci, kl = kt // 32, kt % 32
            for qb in qbs:
                col = 512 * (qb % 2)
                mk.pe(lambda e, qb=qb, col=col: e.matmul(O_ps[qb], VR[ci][:, kl, :], PT[ps][:, col:col + 512],
                                                         start=(kt == 0), stop=(kt == last_kt(qb))),
                      reads=[bVR[ci], bPT[ps]], writes=[bOp[qb]])

        RCPf = [ar.alloc(512, F32) for _ in range(2)]
        bRCPf = [B("RCPf0"), B("RCPf1")]

        def epilogue(h):
            fc, base = h // 2, 64 * (h % 2)
            for qb in range(4):
                r = qb % 2
                mk.dve(lambda e, qb=qb, r=r: e.reciprocal(out=RCPf[r][0:64, :], in_=O_ps[qb][64:128, :]),
                       reads=[bOp[qb]], writes=[bRCPf[r]])
                mk.dve(lambda e, qb=qb, r=r: e.tensor_tensor(out=OTall[base:base + 64, fc, 512 * qb:512 * (qb + 1)],
                                                             in0=O_ps[qb][0:64, :], in1=RCPf[r][0:64, :], op=ALU.mult),
                       reads=[bOp[qb], bRCPf[r]], writes=[bOT[fc]])

        prep_head(0)
        for ci in range(4):
            load_chunk(0, ci)
        per_head_pc = (len(precast) + 7) // 8
        for h in range(8):
            for ci in range(4):
                ulist = [(kt, p) for kt in range(32 * ci, 32 * ci + 32) for p in range(2) if kt <= last_kt(2 * p + 1)]
                units = {}
                n = len(ulist)
                for i in range(min(2, n)):
                    units[i] = unit_qk(h, ulist[i][0], ulist[i][1])
                for i in range(n):
                    ps = unit_exp(units[i])
                    if i + 2 < n:
                        units[i + 2] = unit_qk(h, ulist[i + 2][0], ulist[i + 2][1])
                    unit_pv(units[i], ps)
                if h + 1 < 8:
                    load_chunk(h + 1, ci)
                    if ci == 0:
                        prep_head(h + 1)
                if ci == 1:
                    emit_precast(per_head_pc)
            epilogue(h)

        mk.barrier()
        ar.off = persist_mark
        DQT2 = [ar.alloc(OWN, BF16) for _ in range(2)]; bDQT2 = [B("DQT0"), B("DQT1")]
        DKT2 = [ar.alloc(2 * OWN, BF16) for _ in range(2)]; bDKT2 = [B("DKT0"), B("DKT1")]
        DVT2 = [ar.alloc(2 * OWN, BF16) for _ in range(2)]; bDVT2 = [B("DVT0"), B("DVT1")]
        BIC2 = [ar.alloc(3 * 512, F32).rearrange("p (b n) -> p b n", n=512) for _ in range(2)]; bBIC2 = [B("BIC0"), B("BIC1")]
        BIP2 = [ar.alloc(3 * 512, F32).rearrange("p (b n) -> p b n", n=512) for _ in range(2)]; bBIP2 = [B("BIP0"), B("BIP1")]
        VD = ar.alloc(69 * 128, BF16).rearrange("p (t d) -> p t d", d=128)
        bVD = B("VD")
        SC2 = [ar.alloc(512, F32) for _ in range(2)]; bSC2 = [B("SC0"), B("SC1")]
        SPv2 = [ar.alloc(512, F32) for _ in range(2)]; bSPv2 = [B("SPv0"), B("SPv1")]
        PC2 = [ar.alloc(512, BF16) for _ in range(2)]; bPC2 = [B("PC0"), B("PC1")]
        PP2 = [ar.alloc(512, BF16) for _ in range(2)]; bPP2 = [B("PP0"), B("PP1")]
        ACCD = ar.alloc(OWN, F32); bACCD = B("ACCD")
        RCPd = ar.alloc(OWN, F32); bRCPd = B("RCPd")
        Sc_ps2 = [PSB[0], PSB[1]]; bScp2 = [B("Sc_ps0"), B("Sc_ps1")]
        Sp_ps2 = [PSB[2], PSB[3]]; bSpp2 = [B("Sp_ps0"), B("Sp_ps1")]
        OD_ps2 = [PSB[4], PSB[5]]; bODp2 = [B("OD_ps0"), B("OD_ps1")]
        TPv_ps = PSB[6].bitcast(BF16); bTPv = B("TPv")
        cst = {"n": 0}

        mk.dve(lambda e: e.memset(VD.rearrange("p t d -> p (t d)"), 1.0), writes=[bVD])
        allDQ = [DB("DQ", o) for o in range(4)]
        allDK = [DB("DK", o) for o in range(8)]
        allDV = [DB("DV", o) for o in range(8)]

        def kview(T, d):
            if d == 1:
                return T.rearrange("p (t i) -> p t i", i=128)
            if d == 4:
                return T.rearrange("p (m i r) -> p m r i", m=8, i=128, r=4)
            return T.rearrange("p (m i r) -> p m r i", m=2, i=128, r=16)

        def qview(T, d):
            if d == 1:
                return T.rearrange("p (t i) -> p t i", i=128)
            if d == 4:
                return T.rearrange("p (m i r) -> p m r i", m=4, i=128, r=4)
            return T.rearrange("p (i r) -> p r i", i=128, r=16)

        def c_loads(h):
            z = h % 2
            mk.dma(lambda e: e.dma_start(out=DQT2[z][0:64, :], in_=DQ_s[h]), reads=allDQ, writes=[bDQT2[z]])
            mk.dma(lambda e: e.dma_start(out=DKT2[z][0:64, :], in_=DK_s[h]), reads=allDK, writes=[bDKT2[z]])
            mk.dma(lambda e: e.dma_start(out=DVT2[z][0:64, :], in_=DV_s[h]), reads=allDV, writes=[bDVT2[z]])
            mk.dma(lambda e: e.dma_start(out=BIC2[z], in_=dbiasc_d[h].rearrange("b p n -> p b n")), writes=[bBIC2[z]])
            for rep in range(4):
                mk.dma(lambda e, rep=rep: e.dma_start(out=BIP2[z][:, :, 128 * rep:128 * (rep + 1)],
                                                      in_=dbiasp_d[h].rearrange("b p n -> p b n")), writes=[bBIP2[z]])

        def make_batch(h, bi, d, g, vidx):
            z = h % 2
            DQT, DKT, BIC, BIP = DQT2[z], DKT2[z], BIC2[z], BIP2[z]
            bDQT, bDKT, bBIC, bBIP = bDQT2[z], bDKT2[z], bBIC2[z], bBIP2[z]
            KV = kview(DKT[0:64, :], d)
            QV = qview(DQT[0:64, :], d)
            y = cst["n"] % 2
            cst["n"] += 1
            Sc_ps, bScp, Sp_ps, bSpp, OD_ps, bODp = Sc_ps2[y], bScp2[y], Sp_ps2[y], bSpp2[y], OD_ps2[y], bODp2[y]
            SC, bSC, SPv, bSPv, PC, bPC, PP, bPP = SC2[y], bSC2[y], SPv2[y], bSPv2[y], PC2[y], bPC2[y], PP2[y], bPP2[y]
            items = []
            for jj in range(4):
                if d == 1:
                    j = 4 * g + jj
                    items.append((QV[:, j, :], KV[:, 16 + j, :], KV[:, 15 + j, :], vidx[(1, 16 + j)], vidx[(1, 15 + j)], j == 0))
                elif d == 4:
                    m, r = g, jj
                    items.append((QV[:, m, r, :], KV[:, 4 + m, r, :], KV[:, 3 + m, r, :], vidx[(4, 4 + m, r)],
                                  vidx[(4, 3 + m, r)], m == 0))
                else:
                    r = 4 * g + jj
                    items.append((QV[:, r, :], KV[:, 1, r, :], KV[:, 0, r, :], vidx[(16, 1, r)], vidx[(16, 0, r)], True))

            def s1():
                for jj, it in enumerate(items):
                    mk.pe(lambda e, jj=jj, it=it: e.matmul(Sc_ps[:, 128 * jj:128 * (jj + 1)], it[1], it[0], start=True, stop=True),
                          reads=[bDKT, bDQT], writes=[bScp])
                for jj, it in enumerate(items):
                    mk.pe(lambda e, jj=jj, it=it: e.matmul(Sp_ps[:, 128 * jj:128 * (jj + 1)], it[2], it[0], start=True, stop=True),
                          reads=[bDKT, bDQT], writes=[bSpp])

            def s23():
                mk.dve(lambda e: e.tensor_tensor(out=SC, in0=Sc_ps, in1=BIC[:, bi, :], op=ALU.add),
                       reads=[bScp, bBIC], writes=[bSC])
                halo = [it[5] for it in items]
                if all(halo):
                    mk.dve(lambda e: e.scalar_tensor_tensor(out=SPv, in0=Sp_ps, scalar=HMASK[:, 0:1], in1=BIP[:, bi, :],
                                                            op0=ALU.add, op1=ALU.add), reads=[bSpp, bBIP, bC], writes=[bSPv])
                elif not any(halo):
                    mk.dve(lambda e: e.tensor_tensor(out=SPv, in0=Sp_ps, in1=BIP[:, bi, :], op=ALU.add),
                           reads=[bSpp, bBIP], writes=[bSPv])
                else:
                    assert halo == [True, False, False, False]
                    mk.dve(lambda e: e.scalar_tensor_tensor(out=SPv[:, 0:128], in0=Sp_ps[:, 0:128], scalar=HMASK[:, 0:1],
                                                            in1=BIP[:, bi, 0:128], op0=ALU.add, op1=ALU.add),
                           reads=[bSpp, bBIP, bC], writes=[bSPv])
                    mk.dve(lambda e: e.tensor_tensor(out=SPv[:, 128:512], in0=Sp_ps[:, 128:512], in1=BIP[:, bi, 128:512], op=ALU.add),
                           reads=[bSpp, bBIP], writes=[bSPv])
                mk.act(lambda e: e.activation(out=PC, in_=SC, func=AF.Exp), reads=[bSC], writes=[bPC])
                mk.act(lambda e: e.activation(out=PP, in_=SPv, func=AF.Exp), reads=[bSPv], writes=[bPP])

            def s45():
                for jj, it in enumerate(items):
                    sl_ = slice(128 * jj, 128 * (jj + 1))
                    mk.pe(lambda e, sl_=sl_, it=it: e.matmul(OD_ps[:, sl_], VD[:, it[3], :], PC[:, sl_], start=True, stop=False),
                          reads=[bVD, bPC], writes=[bODp])
                    mk.pe(lambda e, sl_=sl_, it=it: e.matmul(OD_ps[:, sl_], VD[:, it[4], :], PP[:, sl_], start=False, stop=True),
                          reads=[bVD, bPP], writes=[bODp])
                if d == 1:
                    oap = ACCD[:, 512 * g:512 * (g + 1)]
                    iap = OD_ps[:, :]
                elif d == 4:
                    oap = ACCD[:, 512 * g:512 * (g + 1)].rearrange("p (i r) -> p r i", r=4)
                    iap = OD_ps[:, :].rearrange("p (r i) -> p r i", r=4)
                else:
                    oap = ACCD[:, :].rearrange("p (i r) -> p r i", r=16)[:, 4 * g:4 * g + 4, :]
                    iap = OD_ps[:, :].rearrange("p (r i) -> p r i", r=4)
                if d == 1:
                    mk.dve(lambda e: e.tensor_copy(out=oap, in_=iap), reads=[bODp], writes=[bACCD])
                else:
                    mk.dve(lambda e: e.tensor_tensor(out=oap, in0=oap, in1=iap, op=ALU.add), reads=[bODp, bACCD], writes=[bACCD])
            return (s1, s23, s45)

        c_loads(0)
        for h in range(8):
            z = h % 2
            DVT, bDVT = DVT2[z], bDVT2[z]
            if h + 1 < 8:
                c_loads(h + 1)
            vidx = {}
            tiles = []
            for i in range(15, 32):
                tiles.append((1, (i,), kview(DVT[0:64, :], 1)[:, i, :]))
            for m in range(3, 8):
                for r in range(4):
                    tiles.append((4, (m, r), kview(DVT[0:64, :], 4)[:, m, r, :]))
            for m in range(2):
                for r in range(16):
                    tiles.append((16, (m, r), kview(DVT[0:64, :], 16)[:, m, r, :]))
            for n0 in range(0, len(tiles), 8):
                grp = tiles[n0:n0 + 8]
                for gi, (d, key, ap_) in enumerate(grp):
                    vidx[(d,) + key] = n0 + gi
                    mk.pe(lambda e, gi=gi, ap_=ap_: e.transpose(TPv_ps[:, 64 * gi:64 * (gi + 1)], ap_, IDB[0:64, 0:64]),
                          reads=[bDVT, bC2], writes=[bTPv])
                ng = len(grp)
                mk.dve(lambda e, n0=n0, ng=ng: e.tensor_copy(out=VD[:, n0:n0 + ng, 0:64],
                                                             in_=TPv_ps[:, 0:64 * ng].rearrange("p (t d) -> p t d", d=64)),
                       reads=[bTPv], writes=[bVD])
            batches = [make_batch(h, bi, d, g, vidx) for bi, d in enumerate((1, 4, 16)) for g in range(4)]
            nb = len(batches)
            for i in range(min(2, nb)):
                batches[i][0]()
                batches[i][1]()
            for i in range(nb):
                batches[i][2]()
                if i + 2 < nb:
                    batches[i + 2][0]()
                    batches[i + 2][1]()
            fc, base = 4 + h // 2, 64 * (h % 2)
            mk.act(lambda e: e.activation(out=RCPd[0:64, :], in_=ACCD[64:128, :], func=AF.Ln), reads=[bACCD], writes=[bRCPd])
            mk.act(lambda e: e.activation(out=RCPd[0:64, :], in_=RCPd[0:64, :], func=AF.Exp, scale=-1.0), reads=[bRCPd], writes=[bRCPd])
            mk.dve(lambda e, fc=fc, base=base: e.tensor_tensor(out=OTall[base:base + 64, fc, :], in0=ACCD[0:64, :],
                                                               in1=RCPd[0:64, :], op=ALU.mult),
                   reads=[bACCD, bRCPd], writes=[bOT[fc]])

        mk.barrier()
        ar.off = markD
        GPOST = ar.alloc(3 * 1024, F32).rearrange("p (g n) -> p g n", n=1024); bGP = B("GPOST")
        WOUT = ar.alloc(8 * 1024, BF16).rearrange("p (k n) -> p k n", n=1024); bWOUT = B("WOUT")
        WXQ = ar.alloc(8 * 256, BF16).rearrange("p (k n) -> p k n", n=256); bWXQ = B("WXQ")
        WXO = ar.alloc(2 * 1024, BF16).rearrange("p (k n) -> p k n", n=1024); bWXO = B("WXO")
        KMT = ar.alloc(2 * 256, BF16).rearrange("p (c n) -> p c n", n=256); bKMT = B("KMT")
        VMp = ar.alloc(2 * 4 * 128, BF16).rearrange("p (t h d) -> p t h d", t=2, h=4); bVMp = B("VMp")
        NWC = 6
        WC = [ar.alloc(1024, BF16) for _ in range(NWC)]
        bWC = [B(f"WC{i}") for i in range(NWC)]
        X2 = [ar.alloc(4 * 1024, F32).rearrange("p (t n) -> p t n", n=1024) for _ in range(2)]
        bX2 = [[B(f"X{z}_{t}") for t in range(4)] for z in range(2)]
        Hb2 = [ar.alloc(1024, BF16) for _ in range(2)]; bHb2 = [B("Hb0"), B("Hb1")]
        HTa = ar.alloc(8 * 512, BF16).rearrange("p (k n) -> p k n", n=512); bHTa = B("HTa")
        HT3 = [ar.alloc(8 * 512, BF16).rearrange("p (k n) -> p k n", n=512) for _ in range(2)]; bHT3 = [B("HT3_0"), B("HT3_1")]
        ATraw = ar.alloc(NJ * 512, BF16)
        ATt = ATraw.rearrange("p (j n) -> p j n", n=512); bAT = B("AT")
        WXK = ATraw[:, 0:2048].rearrange("p (k n) -> p k n", n=256); bWXK = B("WXK")
        WXV = ATraw[:, 2048:4096].rearrange("p (k n) -> p k n", n=256); bWXV = B("WXV")
        QM = ar.alloc(2 * 512, BF16).rearrange("p (c n) -> p c n", n=512); bQM = B("QM")
        OMT = ar.alloc(2 * 512, BF16).rearrange("p (c n) -> p c n", n=512); bOMT = B("OMT")
        Tn2 = [ar.alloc(1024, F32) for _ in range(2)]; bTn2 = [B("Tn0"), B("Tn1")]
        SQJ = ar.alloc(1024, BF16); bSQJ = B("SQJ")
        SSd2 = [ar.alloc(4, F32) for _ in range(2)]; bSSd2 = [B("SSd0"), B("SSd1")]
        PM2 = [ar.alloc(512, BF16) for _ in range(2)]; bPM2 = [B("PM0"), B("PM1")]
        SG2 = [ar.alloc(512, F32) for _ in range(2)]; bSG2 = [B("SG0"), B("SG1")]
        RCPm = ar.alloc(512, F32); bRCPm = B("RCPm")

        bB = [B(f"bank{i}") for i in range(8)]
        YB = 4
        TPd_ps = PSB[6].bitcast(BF16); bTPd = bB[6]
        Om_ps, bOm = PSB[6], bB[6]
        G_ps, bG = PSB[7], bB[7]
        Sm_ps, bSm = PSB[7], bB[7]
        G_ps2 = [PSB[0], PSB[2]]; bG2 = [bB[0], bB[2]]
        U_ps2 = [PSB[1], PSB[3]]; bU2 = [bB[1], bB[3]]

        wst = {"i": 0, "c": 0, "n": 0}
        allWb = [DB("Wb", i) for i in range(8 + 24 + 2 + 3 * NJ)]

        def load_wb(src_ap, dst_ap, dst_buf):
            mk.dma(lambda e: e.dma_start(out=dst_ap, in_=src_ap), reads=allWb, writes=[dst_buf])

        mk.dma(lambda e: e.dma_start(out=GPOST, in_=gpost_d.rearrange("g p n -> p g n")), writes=[bGP])
        load_wb(Wout_b.rearrange("(k p) n -> p k n", p=128), WOUT, bWOUT)
        load_wb(Wxq_b.rearrange("(k p) n -> p k n", p=128), WXQ, bWXQ)
        load_wb(Wxk_b.rearrange("(k p) n -> p k n", p=128), WXK, bWXK)
        load_wb(Wxv_b.rearrange("(k p) n -> p k n", p=128), WXV, bWXV)
        load_wb(Wxo_b.rearrange("(k p) n -> p k n", p=128), WXO, bWXO)

        def rstd_stages(st, parts, src_bufs, z):
            SSd, bSSd = SSd2[z], bSSd2[z]

            def s_a():
                mk.dve(lambda e: e.memset(SSd[:, 0:2], 0.0), reads=[], writes=[bSSd])
                off = 0
                for col, iap in enumerate(parts):
                    n = iap.shape[-1]
                    mk.act(lambda e, iap=iap, col=col, n=n, off=off: e.activation(out=SQJ[:, off:off + n], in_=iap, func=AF.Square,
                                                                                 accum_out=SSd[:, col:col + 1]),
                           reads=src_bufs + [bSSd], writes=[bSQJ, bSSd])
                    off += n

            def s_b():
                if len(parts) == 2:
                    mk.dve(lambda e: e.tensor_tensor(out=SSd[:, 0:1], in0=SSd[:, 0:1], in1=SSd[:, 1:2], op=ALU.add),
                           reads=[bSSd], writes=[bSSd])
                mk.act(lambda e: e.activation(out=SSd[:, 2:3], in_=SSd[:, 0:1], func=AF.Ln, bias=EPS, scale=1.0 / 1024),
                       reads=[bSSd], writes=[bSSd])
                mk.act(lambda e: e.activation(out=SSd[:, 3:4], in_=SSd[:, 2:3], func=AF.Exp, scale=-0.5),
                       reads=[bSSd], writes=[bSSd])
            st.append(s_a)
            st.append(s_b)

        def post_norm_stages(st, X, t, gi, xbuf, ba):
            z = wst["n"] % 2
            wst["n"] += 1
            ybufs = [bB[ba], bB[ba + 1]]
            rstd_stages(st, [PSB[ba], PSB[ba + 1]], ybufs, z)
            Tn, bTn = Tn2[z], bTn2[z]

            def s_c():
                for hh in range(2):
                    mk.dve(lambda e, hh=hh: e.scalar_tensor_tensor(out=Tn[:, 512 * hh:512 * (hh + 1)], in0=PSB[ba + hh],
                                                                   scalar=SSd2[z][:, 3:4], in1=GPOST[:, gi, 512 * hh:512 * (hh + 1)],
                                                                   op0=ALU.mult, op1=ALU.mult),
                           reads=[bB[ba + hh], bSSd2[z], bGP], writes=[bTn])

            def s_d():
                mk.dve(lambda e: e.tensor_tensor(out=X[:, t, :], in0=X[:, t, :], in1=Tn, op=ALU.add),
                       reads=[bTn, xbuf], writes=[xbuf])
            st.append(s_c)
            st.append(s_d)

        def pre_norm_stages(st, src_ap, src_bufs, dstT, dst_buf, col0):
            z = wst["n"] % 2
            wst["n"] += 1
            rstd_stages(st, [src_ap], src_bufs, z)
            Hb, bHb = Hb2[z], bHb2[z]

            def s_c():
                mk.dve(lambda e: e.tensor_scalar(out=Hb, in0=src_ap, scalar1=SSd2[z][:, 3:4], scalar2=None, op0=ALU.mult),
                       reads=src_bufs + [bSSd2[z]], writes=[bHb])

            def s_d():
                for kc in range(8):
                    mk.pe(lambda e, kc=kc: e.transpose(TPd_ps[:, 128 * kc:128 * (kc + 1)], Hb[:, 128 * kc:128 * (kc + 1)], IDB),
                          reads=[bHb, bC2], writes=[bTPd])

            def s_e():
                mk.act(lambda e: e.activation(out=dstT[:, :, col0:col0 + 128], in_=TPd_ps.rearrange("p (k n) -> p k n", n=128),
                                              func=AF.Copy), reads=[bTPd], writes=[dst_buf])
            st.append(s_c)
            st.append(s_d)
            st.append(s_e)

        def run_all(st):
            for f in st:
                f()

        X0 = X2[0]
        HMT = HTa
        st0 = []
        for mt in range(2):
            mk.dma(lambda e, mt=mt: e.dma_start(out=X0[:, mt, :], in_=memx[128 * mt:128 * (mt + 1), :]), writes=[bX2[0][mt]])
            pre_norm_stages(st0, X0[:, mt, :], [bX2[0][mt]], HMT, bHTa, 128 * mt)
        run_all(st0)
        for c2 in range(2):
            for kc in range(8):
                mk.pe(lambda e, kc=kc, c2=c2: e.matmul(G_ps[:, 0:256], WXK[:, kc, 128 * c2:128 * (c2 + 1)], HMT[:, kc, 0:256],
                                                        start=(kc == 0), stop=(kc == 7)),
                      reads=[bWXK, bHTa], writes=[bG])
            mk.act(lambda e, c2=c2: e.activation(out=KMT[:, c2, :], in_=G_ps[:, 0:256], func=AF.Copy),
                   reads=[bG], writes=[bKMT])
        mk.dve(lambda e: e.memset(VMp.rearrange("p t h d -> p (t h d)"), 1.0), writes=[bVMp])
        for mt in range(2):
            for kc in range(8):
                mk.pe(lambda e, kc=kc, mt=mt: e.matmul(Om_ps[:, 0:256], HMT[:, kc, 128 * mt:128 * (mt + 1)], WXV[:, kc, :],
                                                        start=(kc == 0), stop=(kc == 7)),
                      reads=[bWXV, bHTa], writes=[bOm])
            mk.act(lambda e, mt=mt: e.activation(out=VMp[:, mt, :, 0:64],
                                                 in_=Om_ps[:, 0:256].rearrange("p (h d) -> p h d", d=64), func=AF.Copy),
                   reads=[bOm], writes=[bVMp])
        mk.dve(lambda e: e.memset(ATraw[:, 0:8], 0.0), reads=[bKMT, bVMp], writes=[bAT, bWXK, bWXV])

        def front_stages(tb, xs):
            st = []
            X, bX = X2[xs], bX2[xs]

            def s_load():
                for t in range(4):
                    gt = 4 * tb + t
                    mk.dma(lambda e, t=t, gt=gt: e.dma_start(out=X[:, t, :], in_=x_own[128 * gt:128 * (gt + 1), :]),
                           writes=[bX[t]])
            st.append(s_load)
            for t in range(4):
                gt = 4 * tb + t
                for hh in range(2):
                    def s_wout(gt=gt, hh=hh):
                        for kc in range(8):
                            mk.pe(lambda e, kc=kc: e.matmul(PSB[YB + hh], OTall[:, kc, 128 * gt:128 * (gt + 1)],
                                                            WOUT[:, kc, 512 * hh:512 * (hh + 1)], start=(kc == 0), stop=(kc == 7)),
                                  reads=[bOT[kc], bWOUT], writes=[bB[YB + hh]])
                    st.append(s_wout)
                post_norm_stages(st, X, t, 0, bX[t], YB)
            for t in range(4):
                pre_norm_stages(st, X[:, t, :], [bX[t]], HTa, bHTa, 128 * t)
            for c2 in range(2):
                def s_q(c2=c2):
                    for kc in range(8):
                        mk.pe(lambda e, kc=kc: e.matmul(G_ps, WXQ[:, kc, 128 * c2:128 * (c2 + 1)], HTa[:, kc, :],
                                                        start=(kc == 0), stop=(kc == 7)),
                              reads=[bWXQ, bHTa], writes=[bG])

                def s_q2(c2=c2):
                    mk.act(lambda e: e.activation(out=QM[:, c2, :], in_=G_ps, func=AF.Copy, scale=0.125),
                           reads=[bG], writes=[bQM])
                st.append(s_q)
                st.append(s_q2)
            for hm in range(4):
                c2, base = hm // 2, 64 * (hm % 2)
                for mt in range(2):
                    def s_qk(c2=c2, base=base, mt=mt):
                        mk.pe(lambda e: e.matmul(Sm_ps, KMT[base:base + 64, c2, 128 * mt:128 * (mt + 1)], QM[base:base + 64, c2, :],
                                                 start=True, stop=True), reads=[bKMT, bQM], writes=[bSm])

                    def s_ex(mt=mt):
                        mk.act(lambda e: e.activation(out=PM2[mt], in_=Sm_ps, func=AF.Exp), reads=[bSm], writes=[bPM2[mt]])

                    def s_pv(mt=mt, hm=hm):
                        mk.pe(lambda e: e.matmul(Om_ps, VMp[:, mt, hm, :], PM2[mt], start=(mt == 0), stop=(mt == 1)),
                              reads=[bVMp, bPM2[mt]], writes=[bOm])
                    st.append(s_qk)
                    st.append(s_ex)
                    st.append(s_pv)

                def s_nrm(c2=c2, base=base):
                    mk.dve(lambda e: e.reciprocal(out=RCPm[0:64, :], in_=Om_ps[64:128, :]), reads=[bOm], writes=[bRCPm])
                    mk.dve(lambda e: e.tensor_tensor(out=OMT[base:base + 64, c2, :], in0=Om_ps[0:64, :],
                                                     in1=RCPm[0:64, :], op=ALU.mult),
                           reads=[bOm, bRCPm], writes=[bOMT])
                st.append(s_nrm)
            for t in range(4):
                for hh in range(2):
                    def s_wxo(t=t, hh=hh):
                        for c2 in range(2):
                            mk.pe(lambda e, c2=c2: e.matmul(PSB[YB + hh], OMT[:, c2, 128 * t:128 * (t + 1)],
                                                            WXO[:, c2, 512 * hh:512 * (hh + 1)], start=(c2 == 0), stop=(c2 == 1)),
                                  reads=[bOMT, bWXO], writes=[bB[YB + hh]])
                    st.append(s_wxo)
                post_norm_stages(st, X, t, 1, bX[t], YB)
            for t in range(4):
                pre_norm_stages(st, X[:, t, :], [bX[t]], HT3[xs], bHT3[xs], 128 * t)
            return st

        out_ops = []

        def ffn(tb, xs, side):
            X, bX = X2[xs], bX2[xs]
            HTd, bHTd = HT3[xs], bHT3[xs]

            def drain(n):
                for _ in range(n):
                    if side:
                        side.pop(0)()
            for j in range(NJ):
                y = j % 2
                wg = wst["c"] % NWC
                wst["c"] += 1
                load_wb(Wg_b[j], WC[wg], bWC[wg])
                wu = wst["c"] % NWC
                wst["c"] += 1
                load_wb(Wu_b[j], WC[wu], bWC[wu])
                for kc in range(8):
                    mk.pe(lambda e, kc=kc, wg=wg, y=y: e.matmul(G_ps2[y], WC[wg][:, 128 * kc:128 * (kc + 1)], HTd[:, kc, :],
                                                                start=(kc == 0), stop=(kc == 7)),
                          reads=[bWC[wg], bHTd], writes=[bG2[y]])
                drain(3)
                for kc in range(8):
                    mk.pe(lambda e, kc=kc, wu=wu, y=y: e.matmul(U_ps2[y], WC[wu][:, 128 * kc:128 * (kc + 1)], HTd[:, kc, :],
                                                                start=(kc == 0), stop=(kc == 7)),
                          reads=[bWC[wu], bHTd], writes=[bU2[y]])
                mk.act(lambda e, y=y: e.activation(out=SG2[y], in_=G_ps2[y], func=AF.Exp, scale=-1.0), reads=[bG2[y]], writes=[bSG2[y]])
                mk.act(lambda e, y=y: e.activation(out=SG2[y], in_=SG2[y], func=AF.Ln, bias=1.0, scale=1.0), reads=[bSG2[y]], writes=[bSG2[y]])
                mk.act(lambda e, y=y: e.activation(out=SG2[y], in_=SG2[y], func=AF.Exp, scale=-1.0), reads=[bSG2[y]], writes=[bSG2[y]])
                mk.dve(lambda e, y=y: e.tensor_tensor(out=SG2[y], in0=SG2[y], in1=G_ps2[y], op=ALU.mult),
                       reads=[bSG2[y], bG2[y]], writes=[bSG2[y]])
                mk.dve(lambda e, j=j, y=y: e.tensor_tensor(out=ATt[:, j, :], in0=SG2[y], in1=U_ps2[y], op=ALU.mult),
                       reads=[bSG2[y], bU2[y]], writes=[bAT])
                drain(3)
            for half in range(2):
                for j in range(NJ):
                    wd = wst["c"] % NWC
                    wst["c"] += 1
                    load_wb(Wd_b[j], WC[wd], bWC[wd])
                    for tt in range(2):
                        t = 2 * half + tt
                        for hh in range(2):
                            mk.pe(lambda e, t=t, tt=tt, j=j, hh=hh, wd=wd: e.matmul(PSB[2 * tt + hh], ATt[:, j, 128 * t:128 * (t + 1)],
                                                                                  WC[wd][:, 512 * hh:512 * (hh + 1)],
                                                                                  start=(j == 0), stop=(j == NJ - 1)),
                                  reads=[bAT, bWC[wd]], writes=[bB[2 * tt + hh]])
                    drain(1)
                for tt in range(2):
                    t = 2 * half + tt
                    stf = []
                    post_norm_stages(stf, X, t, 2, bX[t], 2 * tt)
                    run_all(stf)
                    gt = 4 * tb + t
                    out_ops.append(mk.dma(lambda e, t=t, gt=gt: e.dma_start(out=out_d[128 * gt:128 * (gt + 1), :], in_=X[:, t, :]),
                                          reads=[bX[t]], eng="sp"))
            while side:
                side.pop(0)()

        run_all(front_stages(0, 0))
        for tb in range(4):
            side = front_stages(tb + 1, (tb + 1) % 2) if tb + 1 < 4 else []
            ffn(tb, tb % 2, side)

        fin = list(out_ops)
        if debug:
            fin = [o for o in mk.ops if o.is_dma]
        seen = set()
        fin2 = []
        for o in fin:
            if id(o.key) not in seen:
                seen.add(id(o.key))
                fin2.append(o)
        mk.emit(final_ops=fin2)
    return nc


def _t5_bucket(dist):
    n_buckets, max_distance = 32, 2048
    max_exact = n_buckets // 2
    d = np.maximum(dist, 1).astype(np.float32)
    large = max_exact + (np.log(d / max_exact) / np.log(max_distance / max_exact)
                         * (n_buckets - max_exact)).astype(np.int32)
    large = np.minimum(large, n_buckets - 1)
    return np.where(dist < max_exact, dist, large).astype(np.int32)


def _prep_inputs(x, mem, g_mix_pre, w_in, b_f, rel_bias, w_out, g_mix_post, g_xattn_pre, g_mem,
                 w_xq, w_xk, w_xv, w_xo, g_xattn_post, g_ffn_pre, w_gate, w_up, w_down, g_ffn_post):
    f = lambda a: np.ascontiguousarray(np.asarray(a, dtype=np.float32))
    x = f(x)[0]
    mem = f(mem)[0]
    w_in = f(w_in)[0]
    xTn = np.ascontiguousarray(x.T)
    fq, fk, fv = w_in[:, 0:512], w_in[:, 512:1024], w_in[:, 1024:1536]
    gate = w_in[:, 1536:1544]
    dq, dk, dv = w_in[:, 1544:2056], w_in[:, 2056:2568], w_in[:, 2568:3080]
    w_a = np.ascontiguousarray(np.concatenate([fk, fv, fq, dq, dk, dv, gate], axis=1))

    def pk(g):
        return f(g)[0].reshape(8, 128).T
    gpre = np.ascontiguousarray(np.stack([pk(g_mix_pre), pk(g_xattn_pre), pk(g_mem), pk(g_ffn_pre)], 0))
    gpost = np.ascontiguousarray(np.stack([np.broadcast_to(f(g)[0][None, :], (128, 1024))
                                           for g in (g_mix_post, g_xattn_post, g_ffn_post)], 0))
    rb = f(rel_bias)
    p = np.arange(128)[:, None]
    q = np.arange(128)[None, :]
    dbiasc = np.empty((8, 3, 128, 512), np.float32)
    dbiasp = np.empty((8, 3, 128, 128), np.float32)
    for bi, d in enumerate((1, 4, 16)):
        dc = q - p
        dp = q + 128 - p
        bc = _t5_bucket(np.clip(dc, 0, 128) * d)
        bp = _t5_bucket(np.clip(dp, 0, 128) * d)
        for h in range(8):
            tc_ = np.where(dc >= 0, rb[bc, h], np.float32(NEG)).astype(np.float32)
            tp_ = np.where(dp <= 128, rb[bp, h], np.float32(NEG)).astype(np.float32)
            dbiasc[h, bi] = np.tile(tc_, (1, 4))
            dbiasp[h, bi] = tp_
    qq = np.arange(512)[None, :]
    negm = np.stack([np.where(qq >= 128 * j + p, 0.0, NEG).astype(np.float32) for j in range(4)], 0)
    ident = np.eye(128, dtype=np.float32)
    msel = np.zeros((128, 3), np.float32)
    for r in range(3):
        msel[64 + r, r] = 1.0
    common = {
        "memx": mem, "w_a": w_a, "gpre": gpre, "bf": f(b_f)[0].reshape(8, 1), "dbiasc": dbiasc, "dbiasp": dbiasp,
        "negm": negm, "ident": ident, "msel": msel, "w_out": f(w_out)[0], "w_xq": f(w_xq)[0], "w_xk": f(w_xk)[0],
        "w_xv": f(w_xv)[0], "w_xo": f(w_xo)[0], "w_gate": f(w_gate)[0], "w_up": f(w_up)[0], "w_down": f(w_down)[0],
        "gpost": gpost,
    }
    in_maps = []
    for c in range(NCORE):
        roll = (c + 1) * OWN
        p0 = S - roll
        idx = (np.arange(S) + roll) % S
        m = dict(common)
        m["xT"] = np.ascontiguousarray(xTn[:, idx])
        m["x_own"] = np.ascontiguousarray(x[c * OWN:(c + 1) * OWN])
        pos = np.arange(S).reshape(128, 128).T
        m["keymask"] = np.where(pos >= p0, 0.0, NEG).astype(np.float32)
        m["halomask"] = np.full((128, 1), NEG if c == 0 else 0.0, np.float32)
        in_maps.append(m)
    return in_maps


_CACHE = {}


def kernel(**inputs):
    in_maps = _prep_inputs(**inputs)
    if "nc" not in _CACHE:
        _CACHE["nc"] = build_program(debug=False)
    nc = _CACHE["nc"]
    res = run_bass_kernel_spmd(nc, in_maps, core_ids=list(range(NCORE)))
    outs = [np.asarray(r["out"], dtype=np.float32) for r in res.results]
    return np.concatenate(outs, axis=0)[None, :, :]
```

```python
import numpy as np
from contextlib import ExitStack
import concourse.bass as bass
import concourse.mybir as mybir
from concourse.bass_utils import run_bass_kernel_spmd

F32 = mybir.dt.float32
BF16 = mybir.dt.bfloat16
AF = mybir.ActivationFunctionType
ALU = mybir.AluOpType

COMPUTE = ("pe", "act", "dve", "pool")
EPOCH = 16000
NEG = -30000.0
S = 16384
OWN = 2048
NCORE = 8
EPS = 1e-6
DFF = 2816
NJ = DFF // 128


class Buf:
    __slots__ = ("name", "last_writer", "readers", "dma_count", "sem", "inc_amt")

    def __init__(self, name):
        self.name = name
        self.last_writer = None
        self.readers = []
        self.dma_count = 0
        self.sem = None
        self.inc_amt = 16


class Op:
    __slots__ = ("idx", "eng", "fn", "deps", "is_dma", "key", "needs_signal", "sigval", "bar")

    def __init__(self, idx, eng, fn, is_dma, key):
        self.idx = idx
        self.eng = eng
        self.fn = fn
        self.deps = []
        self.is_dma = is_dma
        self.key = key
        self.needs_signal = False
        self.sigval = None
        self.bar = None


class MK:
    def __init__(self, nc):
        self.nc = nc
        self.ops = []
        self.streams = {"pe": [], "act": [], "dve": [], "pool": [], "sp": []}
        self.dma_keys = []
        self.pending_bar = {}

    def buf(self, name):
        return Buf(name)

    def barrier(self):
        last = [self.streams[e][-1] for e in COMPUTE if self.streams[e]]
        last = [o for o in last if not o.is_dma]
        lastc = []
        for e in COMPUTE:
            for o in reversed(self.streams[e]):
                if not o.is_dma:
                    lastc.append(o)
                    break
        for o in lastc:
            o.needs_signal = True
        dstate = [(k, k.dma_count) for k in self.dma_keys if k.dma_count > 0]
        for e in self.streams:
            self.pending_bar[e] = (lastc, dstate)

    def op(self, eng, fn, reads=(), writes=(), dma=False):
        key = None
        if dma:
            for h in writes:
                if not h.name.startswith("dram:"):
                    key = h
                    break
            if key is None:
                for h in reads:
                    if not h.name.startswith("dram:"):
                        key = h
                        break
            if key is None:
                key = reads[0] if reads else writes[0]
            if key not in self.dma_keys:
                self.dma_keys.append(key)
        o = Op(len(self.ops), eng, fn, dma, key)
        if eng in self.pending_bar:
            o.bar = self.pending_bar.pop(eng)
        deps = {}

        def add_dep(d):
            if d is None or d is o:
                return
            if (not d.is_dma) and d.eng == "pe" and eng == "pe" and not dma:
                return
            if d.is_dma:
                deps[d.idx] = (d, d.key.dma_count)
            else:
                deps[d.idx] = (d, None)
            d.needs_signal = True

        for h in reads:
            add_dep(h.last_writer)
        for h in writes:
            add_dep(h.last_writer)
            for r in h.readers:
                add_dep(r)
        if dma:
            key.dma_count += 1
        for h in writes:
            h.last_writer = o
            h.readers = []
        for h in reads:
            if h.last_writer is not o:
                h.readers.append(o)
        o.deps = list(deps.values())
        self.ops.append(o)
        self.streams[eng].append(o)
        return o

    def pe(self, fn, reads=(), writes=()):
        return self.op("pe", fn, reads, writes)

    def act(self, fn, reads=(), writes=()):
        return self.op("act", fn, reads, writes)

    def dve(self, fn, reads=(), writes=()):
        return self.op("dve", fn, reads, writes)

    def pool(self, fn, reads=(), writes=()):
        return self.op("pool", fn, reads, writes)

    def dma(self, fn, reads=(), writes=(), eng="sp"):
        return self.op(eng, fn, reads, writes, dma=True)

    def emit(self, final_ops=()):
        nc = self.nc
        nep = {}
        for e in COMPUTE:
            c = 0
            for o in self.streams[e]:
                if o.is_dma:
                    continue
                if o.needs_signal:
                    c += 1
                    o.sigval = ((c - 1) // EPOCH, (c - 1) % EPOCH + 1)
            nep[e] = max(c - 1, 0) // EPOCH + 1
        with ExitStack() as st:
            esem = {e: [st.enter_context(nc.semaphore(f"s_{e}{j}")) for j in range(nep[e])]
                    for e in COMPUTE}
            for i, k in enumerate(self.dma_keys):
                k.sem = st.enter_context(nc.semaphore(f"d_{i}"))
            block = st.enter_context(nc.Block())
            engobj = {"pe": "tensor", "act": "scalar", "dve": "vector", "pool": "gpsimd",
                      "sp": "sync"}
            fin = [(o.key, o.key.inc_amt * o.key.dma_count) for o in final_ops]

            def make_body(ename):
                stream = self.streams[ename]

                def body(eng):
                    known = {}

                    def wait(sem, val):
                        kid = id(sem)
                        if known.get(kid, 0) >= val:
                            return
                        known[kid] = val
                        eng.wait_ge(sem, val)

                    for o in stream:
                        if o.bar is not None:
                            lastc, dstate = o.bar
                            for d in lastc:
                                wait(esem[d.eng][d.sigval[0]], d.sigval[1])
                            for (k, cnt) in dstate:
                                wait(k.sem, k.inc_amt * cnt)
                        for (d, dv) in o.deps:
                            if d.is_dma:
                                wait(d.key.sem, d.key.inc_amt * dv)
                            else:
                                wait(esem[d.eng][d.sigval[0]], d.sigval[1])
                        ins = o.fn(eng)
                        if o.is_dma:
                            ins.then_inc(o.key.sem, o.key.inc_amt)
                        elif o.needs_signal:
                            ins.then_inc(esem[o.eng][o.sigval[0]], 1)
                    if ename == "sp":
                        for (k, v) in fin:
                            eng.wait_ge(k.sem, v)
                return body

            for ename in ("sp", "pe", "act", "dve", "pool"):
                if not self.streams[ename] and not (ename == "sp" and fin):
                    continue
                getattr(block, engobj[ename])(make_body(ename))


class Arena:
    def __init__(self, t, nfloats):
        self.t = t
        self.n = nfloats
        self.off = 0

    def alloc(self, nelem, dt=F32):
        nb = nelem * (4 if dt == F32 else 2)
        nf = ((nb + 3) // 4 + 15) // 16 * 16
        ap = self.t[:, self.off:self.off + nf]
        self.off += nf
        assert self.off <= self.n, ("SBUF arena overflow", self.off, self.n)
        if dt != F32:
            ap = ap.bitcast(dt)
        return ap[:, 0:nelem]


def build_program(debug=False):
    nc = bass.Bass("TRN2", target_bir_lowering=False)
    mk = MK(nc)

    def din(name, shape, dt=F32):
        return nc.dram_tensor(name, shape, dt, kind="ExternalInput").ap()

    xT = din("xT", [1024, S])
    x_own = din("x_own", [OWN, 1024])
    memx = din("memx", [256, 1024])
    keymask_d = din("keymask", [128, 128])
    halomask_d = din("halomask", [128, 1])
    w_a = din("w_a", [1024, 3080])
    gpre_d = din("gpre", [4, 128, 8])
    bf_d = din("bf", [8, 1])
    dbiasc_d = din("dbiasc", [8, 3, 128, 512])
    dbiasp_d = din("dbiasp", [8, 3, 128, 128])
    negm_d = din("negm", [4, 128, 512])
    ident_d = din("ident", [128, 128])
    msel_d = din("msel", [128, 3])
    w_out_d = din("w_out", [1024, 1024])
    w_xq_d = din("w_xq", [1024, 256])
    w_xk_d = din("w_xk", [1024, 256])
    w_xv_d = din("w_xv", [1024, 256])
    w_xo_d = din("w_xo", [256, 1024])
    w_gate_d = din("w_gate", [1024, DFF])
    w_up_d = din("w_up", [1024, DFF])
    w_down_d = din("w_down", [DFF, 1024])
    gpost_d = din("gpost", [3, 128, 1024])
    out_d = nc.dram_tensor("out", [OWN, 1024], F32, kind="ExternalOutput").ap()

    skind = "ExternalOutput" if debug else "Internal"

    def dscr(name, shape, dt):
        if debug:
            return nc.dram_tensor(name, shape, dt, kind="ExternalOutput").ap()
        return nc.dram_tensor(name, shape, dt).ap()

    KT_s = dscr("KT_s", [8, 64, S], BF16)
    V_s = dscr("V_s", [8, 128, 128 * 128], BF16)
    QT_s = dscr("QT_s", [8, 64, OWN], BF16)
    C_s = dscr("C_s", [8, OWN], F32)
    DQ_s = dscr("DQ_s", [8, 64, OWN], BF16)
    DK_s = dscr("DK_s", [8, 64, 2 * OWN], BF16)
    DV_s = dscr("DV_s", [8, 64, 2 * OWN], BF16)
    Wout_b = dscr("Wout_b", [1024, 1024], BF16)
    Wxq_b = dscr("Wxq_b", [1024, 256], BF16)
    Wxk_b = dscr("Wxk_b", [1024, 256], BF16)
    Wxv_b = dscr("Wxv_b", [1024, 256], BF16)
    Wxo_b = dscr("Wxo_b", [256, 1024], BF16)
    Wg_b = dscr("Wg_b", [NJ, 128, 1024], BF16)
    Wu_b = dscr("Wu_b", [NJ, 128, 1024], BF16)
    Wd_b = dscr("Wd_b", [NJ, 128, 1024], BF16)
    if debug:
        OT_dbg = nc.dram_tensor("OT_dbg", [128, 16 * 1024], BF16, kind="ExternalOutput").ap()
        NC_dbg = nc.dram_tensor("NC_dbg", [128, 1024], F32, kind="ExternalOutput").ap()

    dbufs = {}

    def DB(name, idx=0):
        k = (name, idx)
        if k not in dbufs:
            dbufs[k] = mk.buf(f"dram:{name}{idx}")
        return dbufs[k]

    with ExitStack() as st:
        ARENA_F = 49 * 1024
        arena_t = st.enter_context(nc.sbuf_tensor("arena", [128, ARENA_F], F32))
        ar = Arena(arena_t, ARENA_F)
        PSALL = st.enter_context(nc.psum_tensor("psall", [128, 4096], F32))
        PSB = [PSALL[:, 512 * i:512 * (i + 1)] for i in range(8)]

        def B(name):
            return mk.buf(name)

        OTall = ar.alloc(8 * OWN, BF16).rearrange("p (k n) -> p k n", n=OWN)
        bOT = [B(f"OTall{k}") for k in range(8)]
        IDF = ar.alloc(128, F32)
        IDB = ar.alloc(128, BF16)
        ONESB = ar.alloc(128, BF16)
        MSEL = ar.alloc(3, F32)
        KMASK = ar.alloc(128, F32)
        HMASK = ar.alloc(1, F32)
        GPRE = ar.alloc(32, F32).rearrange("p (g k) -> p g k", k=8)
        NBF = ar.alloc(1, F32)
        ONES8 = ar.alloc(512, F32)
        bC = B("consts")

        mk.dma(lambda e: e.dma_start(out=IDF, in_=ident_d), writes=[bC])
        mk.dma(lambda e: e.dma_start(out=MSEL, in_=msel_d), writes=[bC])
        mk.dma(lambda e: e.dma_start(out=KMASK, in_=keymask_d), writes=[bC])
        mk.dma(lambda e: e.dma_start(out=HMASK, in_=halomask_d), writes=[bC])
        mk.dma(lambda e: e.dma_start(out=GPRE, in_=gpre_d.rearrange("g p k -> p g k")), writes=[bC])
        mk.dma(lambda e: e.dma_start(out=NBF[0:8, :], in_=bf_d), writes=[bC])
        bC2 = B("consts2")
        mk.dve(lambda e: e.tensor_copy(out=IDB, in_=IDF), reads=[bC], writes=[bC2])
        mk.dve(lambda e: e.memset(ONESB, 1.0), writes=[bC2])
        mk.dve(lambda e: e.memset(ONES8, 1.0), writes=[bC2])
        mk.dve(lambda e: e.tensor_scalar(out=NBF[0:8, :], in0=NBF[0:8, :], scalar1=-1.0, scalar2=None,
                                         op0=ALU.mult), reads=[bC], writes=[bC, bC2])
        markD = ar.off
        negc_all = ar.alloc(128 * 8, F32).rearrange("p (t h) -> p t h", h=8)
        bNC = [B(f"negc{b}") for b in range(32)]
        persist_mark = ar.off

        WA = ar.alloc(8 * 3080, BF16).rearrange("p (k c) -> p k c", c=3080)
        bWAc = [B(f"WA{i}") for i in range(7)]
        XTraw = [ar.alloc(8 * 512, F32) for _ in range(2)]
        XT = [a.rearrange("p (k n) -> p k n", n=512) for a in XTraw]
        bXT = [B("XT0"), B("XT1")]
        w_a_v = w_a.rearrange("(k p) c -> p k c", p=128)

        XSQ = ar.alloc(8 * 512, BF16).rearrange("p (k n) -> p k n", n=512)
        bXSQ = B("XSQ")
        HT = [ar.alloc(8 * 512, BF16).rearrange("p (k n) -> p k n", n=512) for _ in range(2)]
        bHT = [B("HT0"), B("HT1")]
        LN = ar.alloc(512, F32)
        bLN = B("LN")
        RSTD = ar.alloc(512, F32)
        bRSTD = B("RSTD")
        EST = [ar.alloc(512, BF16) for _ in range(3)]
        bEST = [B(f"EST{i}") for i in range(3)]
        VT = [ar.alloc(512, BF16) for _ in range(2)]
        bVT = [B("VT0"), B("VT1")]
        VST = ar.alloc(8 * 8 * 128, BF16).rearrange("p (h t d) -> p h t d", h=8, t=8)
        bVST = B("VST")
        GE = ar.alloc(512, F32)
        bGE = B("GE")
        GSP = ar.alloc(512, F32)
        bGSP = B("GSP")
        CS = [ar.alloc(512, F32) for _ in range(2)]
        bCS = [B("CS0"), B("CS1")]
        NCS = ar.alloc(512, F32)
        bNCS = B("NCS")


        SS_ps, bSS = PSB[0], B("SS_ps")
        PR_ps = [PSB[1], PSB[2], PSB[3]]
        bPR = [B("PR0"), B("PR1"), B("PR2")]
        TPb_ps = PSB[4].bitcast(BF16)
        bTPb = B("TPb")
        TPc_ps, bTPc = PSB[5], B("TPc")

        xT_v = xT.rearrange("(k p) n -> p k n", p=128)
        KT_v = KT_s.rearrange("h d n -> (h d) n")
        QT_v = QT_s.rearrange("h d n -> (h d) n")
        DQ_v = DQ_s.rearrange("h d n -> (h d) n")
        DK_v = DK_s.rearrange("h d n -> (h d) n")
        DV_v = DV_s.rearrange("h d n -> (h d) n")
        V_v = V_s.rearrange("h p n -> p h n")
        state = {"pr": 0, "est": 0}

        def project(hs, col0, M, rd_extra=()):
            k = state["pr"] % 3
            state["pr"] += 1
            for kc in range(8):
                mk.pe(lambda e, kc=kc, k=k: e.matmul(PR_ps[k][0:M, :], WA[:, kc, col0:col0 + M], HT[hs][:, kc, :],
                                                       start=(kc == 0), stop=(kc == 7)),
                      reads=bWAc[col0 // 440:(col0 + M - 1) // 440 + 1] + [bHT[hs]], writes=[bPR[k]])
            return k

        def evac_store(k, dst_ap, dst_buf, scale=1.0):
            s = state["est"] % 3
            state["est"] += 1
            mk.act(lambda e: e.activation(out=EST[s], in_=PR_ps[k], func=AF.Copy, scale=scale),
                   reads=[bPR[k]], writes=[bEST[s]])
            mk.dma(lambda e: e.dma_start(out=dst_ap, in_=EST[s]), reads=[bEST[s]], writes=[dst_buf], eng="act")

        NBLK = S // 512

        def xload(b):
            xs = b % 2
            mk.dma(lambda e: e.dma_start(out=XT[xs], in_=xT_v[:, :, 512 * b:512 * (b + 1)]), writes=[bXT[xs]])

        def stats(b):
            xs = b % 2
            hs = b % 2
            mk.pool(lambda e: e.tensor_tensor(out=XSQ, in0=XT[xs], in1=XT[xs], op=ALU.mult),
                    reads=[bXT[xs]], writes=[bXSQ])
            for kc in range(8):
                mk.pe(lambda e, kc=kc: e.matmul(SS_ps, ONESB, XSQ[:, kc, :], start=(kc == 0), stop=(kc == 7)),
                      reads=[bXSQ, bC2], writes=[bSS])
            mk.act(lambda e: e.activation(out=LN, in_=SS_ps, func=AF.Ln, bias=EPS, scale=1.0 / 1024),
                   reads=[bSS], writes=[bLN])
            mk.act(lambda e: e.activation(out=RSTD, in_=LN, func=AF.Exp, scale=-0.5), reads=[bLN], writes=[bRSTD])
            for kc in range(8):
                fn = (lambda e, kc=kc: e.tensor_tensor(out=HT[hs][:, kc, :], in0=XT[xs][:, kc, :], in1=RSTD, op=ALU.mult))
                if kc % 4 == 3:
                    mk.pool(fn, reads=[bXT[xs], bRSTD], writes=[bHT[hs]])
                else:
                    mk.dve(fn, reads=[bXT[xs], bRSTD], writes=[bHT[hs]])

        def cs_transposes(bb):
            cs = bb % 2
            for j in range(4):
                mk.pe(lambda e, j=j: e.transpose(TPc_ps[:, 8 * j:8 * j + 8], CS[cs][0:8, 128 * j:128 * (j + 1)],
                                                 IDF[0:8, 0:8]), reads=[bCS[cs], bC], writes=[bTPc])
            for j in range(4):
                kt = 4 * bb + j
                mk.dve(lambda e, j=j, kt=kt: e.tensor_scalar(out=negc_all[:, kt, :], in0=TPc_ps[:, 8 * j:8 * j + 8],
                                                             scalar1=KMASK[:, kt:kt + 1], scalar2=None, op0=ALU.add),
                       reads=[bTPc, bC], writes=[bNC[bb]])

        mk.pool(lambda e: e.memset(VST.rearrange("p h t d -> p (h t d)"), 1.0), writes=[bVST])
        xload(0)
        xload(1)
        stats(0)
        OTf = OTall.rearrange("p k n -> p (k n)").bitcast(F32)
        WST = [OTf[:, 4096 * i:4096 * i + 8 * 440].rearrange("p (k c) -> p k c", c=440) for i in range(2)]
        bWST = [B("WST0"), B("WST1")]
        for i in range(7):
            s = i % 2
            mk.dma(lambda e, i=i, s=s: e.dma_start(out=WST[s], in_=w_a_v[:, :, 440 * i:440 * (i + 1)]),
                   writes=[bWST[s]])
            for kc in range(8):
                if kc % 2 == 0:
                    mk.dve(lambda e, i=i, s=s, kc=kc: e.tensor_scalar(out=WA[:, kc, 440 * i:440 * (i + 1)], in0=WST[s][:, kc, :],
                                                                      scalar1=GPRE[:, 0, kc:kc + 1], scalar2=None, op0=ALU.mult),
                           reads=[bWST[s], bC], writes=[bWAc[i]])
                else:
                    mk.act(lambda e, i=i, s=s, kc=kc: e.activation(out=WA[:, kc, 440 * i:440 * (i + 1)], in_=WST[s][:, kc, :],
                                                                   func=AF.Copy, scale=GPRE[:, 0, kc:kc + 1]),
                           reads=[bWST[s], bC], writes=[bWAc[i]])
        for b in range(NBLK):
            xs = b % 2
            hs = b % 2
            if b + 1 < NBLK:
                stats(b + 1)
            if b + 2 < NBLK:
                xload(b + 2)
            if b >= 1:
                cs_transposes(b - 1)
            for i in range(4):
                k = project(hs, 128 * i, 128)
                evac_store(k, KT_v[128 * i:128 * (i + 1), 512 * b:512 * (b + 1)], DB("KT", b))
            def v_transposes(i, vs):
                for j in range(4):
                    mk.pe(lambda e, j=j, vs=vs: e.transpose(TPb_ps[:, 128 * j:128 * (j + 1)],
                                                            VT[vs][:, 128 * j:128 * (j + 1)], IDB),
                          reads=[bVT[vs], bC2], writes=[bTPb])
                t0 = 4 * (b % 2)
                mk.dve(lambda e, i=i, t0=t0: e.tensor_copy(
                    out=VST[:, 2 * i:2 * i + 2, t0:t0 + 4, 0:64],
                    in_=TPb_ps[:, 0:512].rearrange("p (t h d) -> p h t d", t=4, h=2)),
                    reads=[bTPb], writes=[bVST])
            for i in range(4):
                k = project(hs, 512 + 128 * i, 128)
                vs = i % 2
                mk.act(lambda e, k=k, vs=vs: e.activation(out=VT[vs], in_=PR_ps[k], func=AF.Copy),
                       reads=[bPR[k]], writes=[bVT[vs]])
                if i >= 1:
                    v_transposes(i - 1, (i - 1) % 2)
            k = project(hs, 3072, 8)
            v_transposes(3, 1)
            if b % 2 == 1:
                sb_ = b // 2
                mk.dma(lambda e, sb_=sb_: e.dma_start(out=V_v[:, :, 1024 * sb_:1024 * (sb_ + 1)],
                                                      in_=VST.rearrange("p h t d -> p h (t d)")),
                       reads=[bVST], writes=[DB("V", sb_)], eng="act")
            mk.act(lambda e, k=k: e.activation(out=GE[0:8, :], in_=PR_ps[k][0:8, :], func=AF.Exp,
                                               bias=NBF[0:8, :], scale=-1.0), reads=[bPR[k], bC2], writes=[bGE])
            mk.act(lambda e: e.activation(out=GSP[0:8, :], in_=GE[0:8, :], func=AF.Ln, bias=1.0, scale=1.0),
                   reads=[bGE], writes=[bGSP])
            cs = b % 2
            if b == 0:
                mk.dve(lambda e: e.tensor_tensor_scan(out=CS[0][0:8, :], data0=ONES8[0:8, :], data1=GSP[0:8, :],
                                                      initial=0.0, op0=ALU.mult, op1=ALU.add),
                       reads=[bGSP, bC2], writes=[bCS[0]])
            else:
                mk.dve(lambda e, cs=cs: e.tensor_tensor_scan(out=CS[cs][0:8, :], data0=ONES8[0:8, :],
                                                             data1=GSP[0:8, :], initial=CS[1 - cs][0:8, 511:512],
                                                             op0=ALU.mult, op1=ALU.add),
                       reads=[bGSP, bC2, bCS[1 - cs]], writes=[bCS[cs]])
            if b >= 28:
                ob = b - 28
                mk.dve(lambda e, cs=cs: e.tensor_scalar(out=NCS[0:8, :], in0=CS[cs][0:8, :], scalar1=-1.0, scalar2=None,
                                                        op0=ALU.mult), reads=[bCS[cs]], writes=[bNCS])
                mk.dma(lambda e, ob=ob: e.dma_start(out=C_s[:, 512 * ob:512 * (ob + 1)], in_=NCS[0:8, :]),
                       reads=[bNCS], writes=[DB("C", ob)], eng="act")
                for i in range(4):
                    k = project(hs, 1024 + 128 * i, 128)
                    evac_store(k, QT_v[128 * i:128 * (i + 1), 512 * ob:512 * (ob + 1)], DB("QT", ob), scale=0.125)
                for i in range(4):
                    k = project(hs, 1536 + 128 * i, 128)
                    evac_store(k, DQ_v[128 * i:128 * (i + 1), 512 * ob:512 * (ob + 1)], DB("DQ", ob), scale=0.125)
            if b >= 24:
                wb = b - 24
                for i in range(4):
                    k = project(hs, 2048 + 128 * i, 128)
                    evac_store(k, DK_v[128 * i:128 * (i + 1), 512 * wb:512 * (wb + 1)], DB("DK", wb))
                for i in range(4):
                    k = project(hs, 2560 + 128 * i, 128)
                    evac_store(k, DV_v[128 * i:128 * (i + 1), 512 * wb:512 * (wb + 1)], DB("DV", wb))

        cs_transposes(NBLK - 1)
        if debug:
            mk.dma(lambda e: e.dma_start(out=NC_dbg, in_=negc_all.rearrange("p t h -> p (t h)")), reads=bNC, eng="act")

        mk.barrier()
        ar.off = persist_mark
        KR = [ar.alloc(4096, BF16) for _ in range(4)]
        bKR = [B(f"KR{i}") for i in range(4)]
        VR = [ar.alloc(32 * 128, BF16).rearrange("p (t d) -> p t d", d=128) for _ in range(4)]
        bVR = [B(f"VR{i}") for i in range(4)]
        QTp2 = [ar.alloc(OWN, BF16) for _ in range(2)]
        bQTp2 = [B("QTp0"), B("QTp1")]
        CR = ar.alloc(OWN, F32)
        bCR = B("CR")
        T0b = ar.alloc(512, BF16); T1b = ar.alloc(512, BF16); T2b = ar.alloc(512, BF16)
        R1 = ar.alloc(512, F32); R2 = ar.alloc(512, F32); A0 = ar.alloc(512, F32); A1 = ar.alloc(512, F32)
        bSPL = B("split")
        NEGMB = ar.alloc(4 * 512, BF16).rearrange("p (j n) -> p j n", n=512)
        NEGMF = ar.alloc(512, F32)
        bNEGMF = B("NEGMF")
        bNEGM = B("NEGM")
        PT = [ar.alloc(1024, BF16) for _ in range(3)]
        bPT = [B(f"PT{i}") for i in range(3)]

        O_ps = [PSB[0], PSB[1], PSB[2], PSB[3]]
        bOp = [B(f"Op{i}") for i in range(4)]
        SP2 = [PSALL[:, 2048:3072], PSALL[:, 3072:4096]]
        bSP2 = [B("SP0"), B("SP1")]

        for j in range(4):
            mk.dma(lambda e, j=j: e.dma_start(out=NEGMF, in_=negm_d[j]), writes=[bNEGMF])
            mk.pool(lambda e, j=j: e.tensor_copy(out=NEGMB[:, j, :], in_=NEGMF), reads=[bNEGMF], writes=[bNEGM])
        ONES_PAIR = 1.0019378662109375
        for i in range(2):
            mk.dve(lambda e, i=i: e.memset(QTp2[i][64:128, :].bitcast(F32), 0.0), writes=[bQTp2[i]])
        for i in range(4):
            if i == 0:
                mk.dve(lambda e, i=i: e.memset(KR[i][64:128, :].bitcast(F32), 0.0), writes=[bKR[i]])
                mk.dve(lambda e, i=i: e.memset(KR[i][64:67, :].bitcast(F32), ONES_PAIR), writes=[bKR[i]])
            else:
                mk.pool(lambda e, i=i: e.memset(KR[i][64:128, :].bitcast(F32), 0.0), writes=[bKR[i]])
                mk.pool(lambda e, i=i: e.memset(KR[i][64:67, :].bitcast(F32), ONES_PAIR), writes=[bKR[i]])

        allKT = [DB("KT", b) for b in range(32)]
        allV = [DB("V", s_) for s_ in range(16)]
        allQT = [DB("QT", o) for o in range(4)]
        allC = [DB("C", o) for o in range(4)]
        WSb = [ar.alloc(1024, F32) for _ in range(3)]
        bWSb = [B(f"WSb{i}") for i in range(3)]
        WBb = [ar.alloc(1024, BF16) for _ in range(3)]
        bWBb = [B(f"WBb{i}") for i in range(3)]
        v3 = lambda a: a.rearrange("p (k c) -> p k c", c=128)
        precast = []
        for kc in range(8):
            precast.append((w_out_d[128 * kc:128 * (kc + 1), :], 1024, None, Wout_b[128 * kc:128 * (kc + 1), :], None))
        for (wd_, wb_, gi_) in ((w_xq_d, Wxq_b, 1), (w_xk_d, Wxk_b, 2), (w_xv_d, Wxv_b, 2)):
            for kc in range(8):
                precast.append((wd_[128 * kc:128 * (kc + 1), :], 256, None, wb_[128 * kc:128 * (kc + 1), :], (gi_, kc)))
        for c2 in range(2):
            precast.append((w_xo_d[128 * c2:128 * (c2 + 1), :], 1024, None, Wxo_b[128 * c2:128 * (c2 + 1), :], None))
        for j in range(NJ):
            precast.append((w_gate_d.rearrange("(k p) c -> p k c", p=128)[:, :, 128 * j:128 * (j + 1)], 1024, v3, Wg_b[j], (3, None)))
            precast.append((w_up_d.rearrange("(k p) c -> p k c", p=128)[:, :, 128 * j:128 * (j + 1)], 1024, v3, Wu_b[j], (3, None)))
            precast.append((w_down_d[128 * j:128 * (j + 1), :], 1024, None, Wd_b[j], None))
        pc_state = {"i": 0}

        def emit_precast(n):
            for _ in range(n):
                i = pc_state["i"]
                if i >= len(precast):
                    return
                pc_state["i"] += 1
                src_ap, ncols, view, dst_ap, gspec = precast[i]
                sl = i % 3
                stg = WSb[sl][:, 0:ncols]
                stg_v = view(stg) if view is not None else stg
                wb = WBb[sl][:, 0:ncols]
                mk.dma(lambda e, stg_v=stg_v, src_ap=src_ap: e.dma_start(out=stg_v, in_=src_ap), writes=[bWSb[sl]])
                if gspec is None:
                    mk.pool(lambda e, wb=wb, stg=stg: e.tensor_copy(out=wb, in_=stg), reads=[bWSb[sl]], writes=[bWBb[sl]])
                elif gspec[1] is not None:
                    gi_, kc_ = gspec
                    mk.pool(lambda e, wb=wb, stg=stg, gi_=gi_, kc_=kc_: e.tensor_scalar(
                        out=wb, in0=stg, scalar1=GPRE[:, gi_, kc_:kc_ + 1], scalar2=None, op0=ALU.mult),
                        reads=[bWSb[sl], bC], writes=[bWBb[sl]])
                else:
                    gi_ = gspec[0]
                    for kc_ in range(8):
                        mk.pool(lambda e, wb=wb, stg=stg, gi_=gi_, kc_=kc_: e.tensor_scalar(
                            out=wb[:, 128 * kc_:128 * (kc_ + 1)], in0=stg[:, 128 * kc_:128 * (kc_ + 1)],
                            scalar1=GPRE[:, gi_, kc_:kc_ + 1], scalar2=None, op0=ALU.mult),
                            reads=[bWSb[sl], bC], writes=[bWBb[sl]])
                mk.dma(lambda e, wb=wb, dst_ap=dst_ap: e.dma_start(out=dst_ap, in_=wb), reads=[bWBb[sl]],
                       writes=[DB("Wb", i)])

        P3 = slice(64, 67)

        def prep_head(h):
            QTp = QTp2[h % 2]
            bQ = bQTp2[h % 2]
            mk.dma(lambda e: e.dma_start(out=QTp[0:64, :], in_=QT_s[h]), reads=allQT, writes=[bQ])
            for r in range(3):
                mk.dma(lambda e, r=r: e.dma_start(out=CR[64 + r:65 + r, :], in_=C_s[h:h + 1, :]),
                       reads=allC, writes=[bCR])
            for q4 in range(4):
                cs_ = slice(512 * q4, 512 * (q4 + 1))
                mk.dve(lambda e, cs_=cs_: e.tensor_copy(out=T0b[P3, :], in_=CR[P3, cs_]), reads=[bCR], writes=[bSPL])
                mk.dve(lambda e, cs_=cs_: e.tensor_tensor(out=R1[P3, :], in0=CR[P3, cs_], in1=T0b[P3, :], op=ALU.subtract),
                       reads=[bCR, bSPL], writes=[bSPL])
                mk.dve(lambda e: e.tensor_copy(out=T1b[P3, :], in_=R1[P3, :]), reads=[bSPL], writes=[bSPL])
                mk.dve(lambda e: e.tensor_tensor(out=R2[P3, :], in0=R1[P3, :], in1=T1b[P3, :], op=ALU.subtract),
                       reads=[bSPL], writes=[bSPL])
                mk.dve(lambda e: e.tensor_copy(out=T2b[P3, :], in_=R2[P3, :]), reads=[bSPL], writes=[bSPL])
                mk.dve(lambda e: e.tensor_scalar(out=A0[P3, :], in0=T0b[P3, :], scalar1=MSEL[P3, 0:1], scalar2=None,
                                                 op0=ALU.mult), reads=[bSPL, bC], writes=[bSPL])
                mk.dve(lambda e: e.scalar_tensor_tensor(out=A1[P3, :], in0=T1b[P3, :], scalar=MSEL[P3, 1:2], in1=A0[P3, :],
                                                        op0=ALU.mult, op1=ALU.add), reads=[bSPL, bC], writes=[bSPL])
                mk.dve(lambda e, cs_=cs_: e.scalar_tensor_tensor(out=QTp[P3, cs_], in0=T2b[P3, :], scalar=MSEL[P3, 2:3],
                                                                 in1=A1[P3, :], op0=ALU.mult, op1=ALU.add),
                       reads=[bSPL, bC], writes=[bQ])

        def load_chunk(h, ci):
            mk.dma(lambda e: e.dma_start(out=KR[ci][0:64, :], in_=KT_s[h][:, 4096 * ci:4096 * (ci + 1)]),
                   reads=allKT, writes=[bKR[ci]])
            mk.dma(lambda e: e.dma_start(out=VR[ci].rearrange("p t d -> p (t d)"), in_=V_s[h][:, 4096 * ci:4096 * (ci + 1)]),
                   reads=allV, writes=[bVR[ci]])

        def last_kt(qb):
            return 115 + 4 * qb

        ust = {"sp": 0, "pt": 0}

        def unit_qk(h, kt, p):
            ci, kl = kt // 32, kt % 32
            QTp, bQ = QTp2[h % 2], bQTp2[h % 2]
            qbs = [qb for qb in (2 * p, 2 * p + 1) if kt <= last_kt(qb)]
            sl = ust["sp"] % 2
            ust["sp"] += 1
            for qb in qbs:
                diag = kt >= 112 + 4 * qb
                col = 512 * (qb % 2)
                mk.pe(lambda e, qb=qb, col=col, diag=diag: e.matmul(SP2[sl][:, col:col + 512], KR[ci][:, 128 * kl:128 * (kl + 1)],
                                                                   QTp[:, 512 * qb:512 * (qb + 1)], start=True, stop=not diag),
                      reads=[bKR[ci], bQ], writes=[bSP2[sl]])
                if diag:
                    j = kt - (112 + 4 * qb)
                    mk.pe(lambda e, col=col, j=j: e.matmul(SP2[sl][:, col:col + 512], IDB, NEGMB[:, j, :], start=False, stop=True),
                          reads=[bC2, bNEGM], writes=[bSP2[sl]])
            return (h, kt, p, qbs, sl)

        def unit_exp(u):
            h, kt, p, qbs, sl = u
            c0 = 512 * (qbs[0] % 2)
            c1 = 512 * (qbs[-1] % 2) + 512
            ps = ust["pt"] % 3
            ust["pt"] += 1
            mk.act(lambda e: e.activation(out=PT[ps][:, c0:c1], in_=SP2[sl][:, c0:c1], func=AF.Exp,
                                          bias=negc_all[:, kt, h:h + 1], scale=1.0),
                   reads=[bSP2[sl], bNC[kt // 4]], writes=[bPT[ps]])
            return ps

        def unit_pv(u, ps):
            h, kt, p, qbs, sl = u
            ci, kl = kt // 32, kt % 32
            for qb in qbs:
                col = 512 * (qb % 2)
                mk.pe(lambda e, qb=qb, col=col: e.matmul(O_ps[qb], VR[ci][:, kl, :], PT[ps][:, col:col + 512],
                                                         start=(kt == 0), stop=(kt == last_kt(qb))),
                      reads=[bVR[ci], bPT[ps]], writes=[bOp[qb]])

        RCPf = [ar.alloc(512, F32) for _ in range(2)]
        bRCPf = [B("RCPf0"), B("RCPf1")]

        def epilogue(h):
            fc, base = h // 2, 64 * (h % 2)
            for qb in range(4):
                r = qb % 2
                mk.dve(lambda e, qb=qb, r=r: e.reciprocal(out=RCPf[r][0:64, :], in_=O_ps[qb][64:128, :]),
                       reads=[bOp[qb]], writes=[bRCPf[r]])
                mk.dve(lambda e, qb=qb, r=r: e.tensor_tensor(out=OTall[base:base + 64, fc, 512 * qb:512 * (qb + 1)],
                                                             in0=O_ps[qb][0:64, :], in1=RCPf[r][0:64, :], op=ALU.mult),
                       reads=[bOp[qb], bRCPf[r]], writes=[bOT[fc]])

        prep_head(0)
        for ci in range(4):
            load_chunk(0, ci)
        per_head_pc = (len(precast) + 7) // 8
        for h in range(8):
            for ci in range(4):
                ulist = [(kt, p) for kt in range(32 * ci, 32 * ci + 32) for p in range(2) if kt <= last_kt(2 * p + 1)]
                units = {}
                n = len(ulist)
                for i in range(min(2, n)):
                    units[i] = unit_qk(h, ulist[i][0], ulist[i][1])
                for i in range(n):
                    ps = unit_exp(units[i])
                    if i + 2 < n:
                        units[i + 2] = unit_qk(h, ulist[i + 2][0], ulist[i + 2][1])
                    unit_pv(units[i], ps)
                if h + 1 < 8:
                    load_chunk(h + 1, ci)
                    if ci == 0:
                        prep_head(h + 1)
                if ci == 1:
                    emit_precast(per_head_pc)
            epilogue(h)

        mk.barrier()
        ar.off = persist_mark
        DQT2 = [ar.alloc(OWN, BF16) for _ in range(2)]; bDQT2 = [B("DQT0"), B("DQT1")]
        DKT2 = [ar.alloc(2 * OWN, BF16) for _ in range(2)]; bDKT2 = [B("DKT0"), B("DKT1")]
        DVT2 = [ar.alloc(2 * OWN, BF16) for _ in range(2)]; bDVT2 = [B("DVT0"), B("DVT1")]
        BIC2 = [ar.alloc(3 * 512, F32).rearrange("p (b n) -> p b n", n=512) for _ in range(2)]; bBIC2 = [B("BIC0"), B("BIC1")]
        BIP2 = [ar.alloc(3 * 512, F32).rearrange("p (b n) -> p b n", n=512) for _ in range(2)]; bBIP2 = [B("BIP0"), B("BIP1")]
        VD = ar.alloc(69 * 128, BF16).rearrange("p (t d) -> p t d", d=128)
        bVD = B("VD")
        SC2 = [ar.alloc(512, F32) for _ in range(2)]; bSC2 = [B("SC0"), B("SC1")]
        SPv2 = [ar.alloc(512, F32) for _ in range(2)]; bSPv2 = [B("SPv0"), B("SPv1")]
        PC2 = [ar.alloc(512, BF16) for _ in range(2)]; bPC2 = [B("PC0"), B("PC1")]
        PP2 = [ar.alloc(512, BF16) for _ in range(2)]; bPP2 = [B("PP0"), B("PP1")]
        ACCD = ar.alloc(OWN, F32); bACCD = B("ACCD")
        RCPd = ar.alloc(OWN, F32); bRCPd = B("RCPd")
        Sc_ps2 = [PSB[0], PSB[1]]; bScp2 = [B("Sc_ps0"), B("Sc_ps1")]
        Sp_ps2 = [PSB[2], PSB[3]]; bSpp2 = [B("Sp_ps0"), B("Sp_ps1")]
        OD_ps2 = [PSB[4], PSB[5]]; bODp2 = [B("OD_ps0"), B("OD_ps1")]
        TPv_ps = PSB[6].bitcast(BF16); bTPv = B("TPv")
        cst = {"n": 0}

        mk.dve(lambda e: e.memset(VD.rearrange("p t d -> p (t d)"), 1.0), writes=[bVD])
        allDQ = [DB("DQ", o) for o in range(4)]
        allDK = [DB("DK", o) for o in range(8)]
        allDV = [DB("DV", o) for o in range(8)]

        def kview(T, d):
            if d == 1:
                return T.rearrange("p (t i) -> p t i", i=128)
            if d == 4:
                return T.rearrange("p (m i r) -> p m r i", m=8, i=128, r=4)
            return T.rearrange("p (m i r) -> p m r i", m=2, i=128, r=16)

        def qview(T, d):
            if d == 1:
                return T.rearrange("p (t i) -> p t i", i=128)
            if d == 4:
                return T.rearrange("p (m i r) -> p m r i", m=4, i=128, r=4)
            return T.rearrange("p (i r) -> p r i", i=128, r=16)

        def c_loads(h):
            z = h % 2
            mk.dma(lambda e: e.dma_start(out=DQT2[z][0:64, :], in_=DQ_s[h]), reads=allDQ, writes=[bDQT2[z]])
            mk.dma(lambda e: e.dma_start(out=DKT2[z][0:64, :], in_=DK_s[h]), reads=allDK, writes=[bDKT2[z]])
            mk.dma(lambda e: e.dma_start(out=DVT2[z][0:64, :], in_=DV_s[h]), reads=allDV, writes=[bDVT2[z]])
            mk.dma(lambda e: e.dma_start(out=BIC2[z], in_=dbiasc_d[h].rearrange("b p n -> p b n")), writes=[bBIC2[z]])
            for rep in range(4):
                mk.dma(lambda e, rep=rep: e.dma_start(out=BIP2[z][:, :, 128 * rep:128 * (rep + 1)],
                                                      in_=dbiasp_d[h].rearrange("b p n -> p b n")), writes=[bBIP2[z]])

        def make_batch(h, bi, d, g, vidx):
            z = h % 2
            DQT, DKT, BIC, BIP = DQT2[z], DKT2[z], BIC2[z], BIP2[z]
            bDQT, bDKT, bBIC, bBIP = bDQT2[z], bDKT2[z], bBIC2[z], bBIP2[z]
            KV = kview(DKT[0:64, :], d)
            QV = qview(DQT[0:64, :], d)
            y = cst["n"] % 2
            cst["n"] += 1
            Sc_ps, bScp, Sp_ps, bSpp, OD_ps, bODp = Sc_ps2[y], bScp2[y], Sp_ps2[y], bSpp2[y], OD_ps2[y], bODp2[y]
            SC, bSC, SPv, bSPv, PC, bPC, PP, bPP = SC2[y], bSC2[y], SPv2[y], bSPv2[y], PC2[y], bPC2[y], PP2[y], bPP2[y]
            items = []
            for jj in range(4):
                if d == 1:
                    j = 4 * g + jj
                    items.append((QV[:, j, :], KV[:, 16 + j, :], KV[:, 15 + j, :], vidx[(1, 16 + j)], vidx[(1, 15 + j)], j == 0))
                elif d == 4:
                    m, r = g, jj
                    items.append((QV[:, m, r, :], KV[:, 4 + m, r, :], KV[:, 3 + m, r, :], vidx[(4, 4 + m, r)],
                                  vidx[(4, 3 + m, r)], m == 0))
                else:
                    r = 4 * g + jj
                    items.append((QV[:, r, :], KV[:, 1, r, :], KV[:, 0, r, :], vidx[(16, 1, r)], vidx[(16, 0, r)], True))

            def s1():
                for jj, it in enumerate(items):
                    mk.pe(lambda e, jj=jj, it=it: e.matmul(Sc_ps[:, 128 * jj:128 * (jj + 1)], it[1], it[0], start=True, stop=True),
                          reads=[bDKT, bDQT], writes=[bScp])
                for jj, it in enumerate(items):
                    mk.pe(lambda e, jj=jj, it=it: e.matmul(Sp_ps[:, 128 * jj:128 * (jj + 1)], it[2], it[0], start=True, stop=True),
                          reads=[bDKT, bDQT], writes=[bSpp])

            def s23():
                mk.dve(lambda e: e.tensor_tensor(out=SC, in0=Sc_ps, in1=BIC[:, bi, :], op=ALU.add),
                       reads=[bScp, bBIC], writes=[bSC])
                halo = [it[5] for it in items]
                if all(halo):
                    mk.dve(lambda e: e.scalar_tensor_tensor(out=SPv, in0=Sp_ps, scalar=HMASK[:, 0:1], in1=BIP[:, bi, :],
                                                            op0=ALU.add, op1=ALU.add), reads=[bSpp, bBIP, bC], writes=[bSPv])
                elif not any(halo):
                    mk.dve(lambda e: e.tensor_tensor(out=SPv, in0=Sp_ps, in1=BIP[:, bi, :], op=ALU.add),
                           reads=[bSpp, bBIP], writes=[bSPv])
                else:
                    assert halo == [True, False, False, False]
                    mk.dve(lambda e: e.scalar_tensor_tensor(out=SPv[:, 0:128], in0=Sp_ps[:, 0:128], scalar=HMASK[:, 0:1],
                                                            in1=BIP[:, bi, 0:128], op0=ALU.add, op1=ALU.add),
                           reads=[bSpp, bBIP, bC], writes=[bSPv])
                    mk.dve(lambda e: e.tensor_tensor(out=SPv[:, 128:512], in0=Sp_ps[:, 128:512], in1=BIP[:, bi, 128:512], op=ALU.add),
                           reads=[bSpp, bBIP], writes=[bSPv])
                mk.act(lambda e: e.activation(out=PC, in_=SC, func=AF.Exp), reads=[bSC], writes=[bPC])
                mk.act(lambda e: e.activation(out=PP, in_=SPv, func=AF.Exp), reads=[bSPv], writes=[bPP])

            def s45():
                for jj, it in enumerate(items):
                    sl_ = slice(128 * jj, 128 * (jj + 1))
                    mk.pe(lambda e, sl_=sl_, it=it: e.matmul(OD_ps[:, sl_], VD[:, it[3], :], PC[:, sl_], start=True, stop=False),
                          reads=[bVD, bPC], writes=[bODp])
                    mk.pe(lambda e, sl_=sl_, it=it: e.matmul(OD_ps[:, sl_], VD[:, it[4], :], PP[:, sl_], start=False, stop=True),
                          reads=[bVD, bPP], writes=[bODp])
                if d == 1:
                    oap = ACCD[:, 512 * g:512 * (g + 1)]
                    iap = OD_ps[:, :]
                elif d == 4:
                    oap = ACCD[:, 512 * g:512 * (g + 1)].rearrange("p (i r) -> p r i", r=4)
                    iap = OD_ps[:, :].rearrange("p (r i) -> p r i", r=4)
                else:
                    oap = ACCD[:, :].rearrange("p (i r) -> p r i", r=16)[:, 4 * g:4 * g + 4, :]
                    iap = OD_ps[:, :].rearrange("p (r i) -> p r i", r=4)
                if d == 1:
                    mk.dve(lambda e: e.tensor_copy(out=oap, in_=iap), reads=[bODp], writes=[bACCD])
                else:
                    mk.dve(lambda e: e.tensor_tensor(out=oap, in0=oap, in1=iap, op=ALU.add), reads=[bODp, bACCD], writes=[bACCD])
            return (s1, s23, s45)

        c_loads(0)
        for h in range(8):
            z = h % 2
            DVT, bDVT = DVT2[z], bDVT2[z]
            if h + 1 < 8:
                c_loads(h + 1)
            vidx = {}
            tiles = []
            for i in range(15, 32):
                tiles.append((1, (i,), kview(DVT[0:64, :], 1)[:, i, :]))
            for m in range(3, 8):
                for r in range(4):
                    tiles.append((4, (m, r), kview(DVT[0:64, :], 4)[:, m, r, :]))
            for m in range(2):
                for r in range(16):
                    tiles.append((16, (m, r), kview(DVT[0:64, :], 16)[:, m, r, :]))
            for n0 in range(0, len(tiles), 8):
                grp = tiles[n0:n0 + 8]
                for gi, (d, key, ap_) in enumerate(grp):
                    vidx[(d,) + key] = n0 + gi
                    mk.pe(lambda e, gi=gi, ap_=ap_: e.transpose(TPv_ps[:, 64 * gi:64 * (gi + 1)], ap_, IDB[0:64, 0:64]),
                          reads=[bDVT, bC2], writes=[bTPv])
                ng = len(grp)
                mk.dve(lambda e, n0=n0, ng=ng: e.tensor_copy(out=VD[:, n0:n0 + ng, 0:64],
                                                             in_=TPv_ps[:, 0:64 * ng].rearrange("p (t d) -> p t d", d=64)),
                       reads=[bTPv], writes=[bVD])
            batches = [make_batch(h, bi, d, g, vidx) for bi, d in enumerate((1, 4, 16)) for g in range(4)]
            nb = len(batches)
            for i in range(min(2, nb)):
                batches[i][0]()
                batches[i][1]()
            for i in range(nb):
                batches[i][2]()
                if i + 2 < nb:
                    batches[i + 2][0]()
                    batches[i + 2][1]()
            fc, base = 4 + h // 2, 64 * (h % 2)
            mk.act(lambda e: e.activation(out=RCPd[0:64, :], in_=ACCD[64:128, :], func=AF.Ln), reads=[bACCD], writes=[bRCPd])
            mk.act(lambda e: e.activation(out=RCPd[0:64, :], in_=RCPd[0:64, :], func=AF.Exp, scale=-1.0), reads=[bRCPd], writes=[bRCPd])
            mk.dve(lambda e, fc=fc, base=base: e.tensor_tensor(out=OTall[base:base + 64, fc, :], in0=ACCD[0:64, :],
                                                               in1=RCPd[0:64, :], op=ALU.mult),
                   reads=[bACCD, bRCPd], writes=[bOT[fc]])

        mk.barrier()
        ar.off = markD
        GPOST = ar.alloc(3 * 1024, F32).rearrange("p (g n) -> p g n", n=1024); bGP = B("GPOST")
        WOUT = ar.alloc(8 * 1024, BF16).rearrange("p (k n) -> p k n", n=1024); bWOUT = B("WOUT")
        WXQ = ar.alloc(8 * 256, BF16).rearrange("p (k n) -> p k n", n=256); bWXQ = B("WXQ")
        WXO = ar.alloc(2 * 1024, BF16).rearrange("p (k n) -> p k n", n=1024); bWXO = B("WXO")
        KMT = ar.alloc(2 * 256, BF16).rearrange("p (c n) -> p c n", n=256); bKMT = B("KMT")
        VMp = ar.alloc(2 * 4 * 128, BF16).rearrange("p (t h d) -> p t h d", t=2, h=4); bVMp = B("VMp")
        NWC = 6
        WC = [ar.alloc(1024, BF16) for _ in range(NWC)]
        bWC = [B(f"WC{i}") for i in range(NWC)]
        X2 = [ar.alloc(4 * 1024, F32).rearrange("p (t n) -> p t n", n=1024) for _ in range(2)]
        bX2 = [[B(f"X{z}_{t}") for t in range(4)] for z in range(2)]
        Hb2 = [ar.alloc(1024, BF16) for _ in range(2)]; bHb2 = [B("Hb0"), B("Hb1")]
        HTa = ar.alloc(8 * 512, BF16).rearrange("p (k n) -> p k n", n=512); bHTa = B("HTa")
        HT3 = [ar.alloc(8 * 512, BF16).rearrange("p (k n) -> p k n", n=512) for _ in range(2)]; bHT3 = [B("HT3_0"), B("HT3_1")]
        ATraw = ar.alloc(NJ * 512, BF16)
        ATt = ATraw.rearrange("p (j n) -> p j n", n=512); bAT = B("AT")
        WXK = ATraw[:, 0:2048].rearrange("p (k n) -> p k n", n=256); bWXK = B("WXK")
        WXV = ATraw[:, 2048:4096].rearrange("p (k n) -> p k n", n=256); bWXV = B("WXV")
        QM = ar.alloc(2 * 512, BF16).rearrange("p (c n) -> p c n", n=512); bQM = B("QM")
        OMT = ar.alloc(2 * 512, BF16).rearrange("p (c n) -> p c n", n=512); bOMT = B("OMT")
        Tn2 = [ar.alloc(1024, F32) for _ in range(2)]; bTn2 = [B("Tn0"), B("Tn1")]
        SQJ = ar.alloc(1024, BF16); bSQJ = B("SQJ")
        SSd2 = [ar.alloc(4, F32) for _ in range(2)]; bSSd2 = [B("SSd0"), B("SSd1")]
        PM2 = [ar.alloc(512, BF16) for _ in range(2)]; bPM2 = [B("PM0"), B("PM1")]
        SG2 = [ar.alloc(512, F32) for _ in range(2)]; bSG2 = [B("SG0"), B("SG1")]
        RCPm = ar.alloc(512, F32); bRCPm = B("RCPm")

        bB = [B(f"bank{i}") for i in range(8)]
        YB = 4
        TPd_ps = PSB[6].bitcast(BF16); bTPd = bB[6]
        Om_ps, bOm = PSB[6], bB[6]
        G_ps, bG = PSB[7], bB[7]
        Sm_ps, bSm = PSB[7], bB[7]
        G_ps2 = [PSB[0], PSB[2]]; bG2 = [bB[0], bB[2]]
        U_ps2 = [PSB[1], PSB[3]]; bU2 = [bB[1], bB[3]]

        wst = {"i": 0, "c": 0, "n": 0}
        allWb = [DB("Wb", i) for i in range(8 + 24 + 2 + 3 * NJ)]

        def load_wb(src_ap, dst_ap, dst_buf):
            mk.dma(lambda e: e.dma_start(out=dst_ap, in_=src_ap), reads=allWb, writes=[dst_buf])

        mk.dma(lambda e: e.dma_start(out=GPOST, in_=gpost_d.rearrange("g p n -> p g n")), writes=[bGP])
        load_wb(Wout_b.rearrange("(k p) n -> p k n", p=128), WOUT, bWOUT)
        load_wb(Wxq_b.rearrange("(k p) n -> p k n", p=128), WXQ, bWXQ)
        load_wb(Wxk_b.rearrange("(k p) n -> p k n", p=128), WXK, bWXK)
        load_wb(Wxv_b.rearrange("(k p) n -> p k n", p=128), WXV, bWXV)
        load_wb(Wxo_b.rearrange("(k p) n -> p k n", p=128), WXO, bWXO)

        def rstd_stages(st, parts, src_bufs, z):
            SSd, bSSd = SSd2[z], bSSd2[z]

            def s_a():
                mk.dve(lambda e: e.memset(SSd[:, 0:2], 0.0), reads=[], writes=[bSSd])
                off = 0
                for col, iap in enumerate(parts):
                    n = iap.shape[-1]
                    mk.act(lambda e, iap=iap, col=col, n=n, off=off: e.activation(out=SQJ[:, off:off + n], in_=iap, func=AF.Square,
                                                                                 accum_out=SSd[:, col:col + 1]),
                           reads=src_bufs + [bSSd], writes=[bSQJ, bSSd])
                    off += n

            def s_b():
                if len(parts) == 2:
                    mk.dve(lambda e: e.tensor_tensor(out=SSd[:, 0:1], in0=SSd[:, 0:1], in1=SSd[:, 1:2], op=ALU.add),
                           reads=[bSSd], writes=[bSSd])
                mk.act(lambda e: e.activation(out=SSd[:, 2:3], in_=SSd[:, 0:1], func=AF.Ln, bias=EPS, scale=1.0 / 1024),
                       reads=[bSSd], writes=[bSSd])
                mk.act(lambda e: e.activation(out=SSd[:, 3:4], in_=SSd[:, 2:3], func=AF.Exp, scale=-0.5),
                       reads=[bSSd], writes=[bSSd])
            st.append(s_a)
            st.append(s_b)

        def post_norm_stages(st, X, t, gi, xbuf, ba):
            z = wst["n"] % 2
            wst["n"] += 1
            ybufs = [bB[ba], bB[ba + 1]]
            rstd_stages(st, [PSB[ba], PSB[ba + 1]], ybufs, z)
            Tn, bTn = Tn2[z], bTn2[z]

            def s_c():
                for hh in range(2):
                    mk.dve(lambda e, hh=hh: e.scalar_tensor_tensor(out=Tn[:, 512 * hh:512 * (hh + 1)], in0=PSB[ba + hh],
                                                                   scalar=SSd2[z][:, 3:4], in1=GPOST[:, gi, 512 * hh:512 * (hh + 1)],
                                                                   op0=ALU.mult, op1=ALU.mult),
                           reads=[bB[ba + hh], bSSd2[z], bGP], writes=[bTn])

            def s_d():
                mk.dve(lambda e: e.tensor_tensor(out=X[:, t, :], in0=X[:, t, :], in1=Tn, op=ALU.add),
                       reads=[bTn, xbuf], writes=[xbuf])
            st.append(s_c)
            st.append(s_d)

        def pre_norm_stages(st, src_ap, src_bufs, dstT, dst_buf, col0):
            z = wst["n"] % 2
            wst["n"] += 1
            rstd_stages(st, [src_ap], src_bufs, z)
            Hb, bHb = Hb2[z], bHb2[z]

            def s_c():
                mk.dve(lambda e: e.tensor_scalar(out=Hb, in0=src_ap, scalar1=SSd2[z][:, 3:4], scalar2=None, op0=ALU.mult),
                       reads=src_bufs + [bSSd2[z]], writes=[bHb])

            def s_d():
                for kc in range(8):
                    mk.pe(lambda e, kc=kc: e.transpose(TPd_ps[:, 128 * kc:128 * (kc + 1)], Hb[:, 128 * kc:128 * (kc + 1)], IDB),
                          reads=[bHb, bC2], writes=[bTPd])

            def s_e():
                mk.act(lambda e: e.activation(out=dstT[:, :, col0:col0 + 128], in_=TPd_ps.rearrange("p (k n) -> p k n", n=128),
                                              func=AF.Copy), reads=[bTPd], writes=[dst_buf])
            st.append(s_c)
            st.append(s_d)
            st.append(s_e)

        def run_all(st):
            for f in st:
                f()

        X0 = X2[0]
        HMT = HTa
        st0 = []
        for mt in range(2):
            mk.dma(lambda e, mt=mt: e.dma_start(out=X0[:, mt, :], in_=memx[128 * mt:128 * (mt + 1), :]), writes=[bX2[0][mt]])
            pre_norm_stages(st0, X0[:, mt, :], [bX2[0][mt]], HMT, bHTa, 128 * mt)
        run_all(st0)
        for c2 in range(2):
            for kc in range(8):
                mk.pe(lambda e, kc=kc, c2=c2: e.matmul(G_ps[:, 0:256], WXK[:, kc, 128 * c2:128 * (c2 + 1)], HMT[:, kc, 0:256],
                                                        start=(kc == 0), stop=(kc == 7)),
                      reads=[bWXK, bHTa], writes=[bG])
            mk.act(lambda e, c2=c2: e.activation(out=KMT[:, c2, :], in_=G_ps[:, 0:256], func=AF.Copy),
                   reads=[bG], writes=[bKMT])
        mk.dve(lambda e: e.memset(VMp.rearrange("p t h d -> p (t h d)"), 1.0), writes=[bVMp])
        for mt in range(2):
            for kc in range(8):
                mk.pe(lambda e, kc=kc, mt=mt: e.matmul(Om_ps[:, 0:256], HMT[:, kc, 128 * mt:128 * (mt + 1)], WXV[:, kc, :],
                                                        start=(kc == 0), stop=(kc == 7)),
                      reads=[bWXV, bHTa], writes=[bOm])
            mk.act(lambda e, mt=mt: e.activation(out=VMp[:, mt, :, 0:64],
                                                 in_=Om_ps[:, 0:256].rearrange("p (h d) -> p h d", d=64), func=AF.Copy),
                   reads=[bOm], writes=[bVMp])
        mk.dve(lambda e: e.memset(ATraw[:, 0:8], 0.0), reads=[bKMT, bVMp], writes=[bAT, bWXK, bWXV])

        def front_stages(tb, xs):
            st = []
            X, bX = X2[xs], bX2[xs]

            def s_load():
                for t in range(4):
                    gt = 4 * tb + t
                    mk.dma(lambda e, t=t, gt=gt: e.dma_start(out=X[:, t, :], in_=x_own[128 * gt:128 * (gt + 1), :]),
                           writes=[bX[t]])
            st.append(s_load)
            for t in range(4):
                gt = 4 * tb + t
                for hh in range(2):
                    def s_wout(gt=gt, hh=hh):
                        for kc in range(8):
                            mk.pe(lambda e, kc=kc: e.matmul(PSB[YB + hh], OTall[:, kc, 128 * gt:128 * (gt + 1)],
                                                            WOUT[:, kc, 512 * hh:512 * (hh + 1)], start=(kc == 0), stop=(kc == 7)),
                                  reads=[bOT[kc], bWOUT], writes=[bB[YB + hh]])
                    st.append(s_wout)
                post_norm_stages(st, X, t, 0, bX[t], YB)
            for t in range(4):
                pre_norm_stages(st, X[:, t, :], [bX[t]], HTa, bHTa, 128 * t)
            for c2 in range(2):
                def s_q(c2=c2):
                    for kc in range(8):
                        mk.pe(lambda e, kc=kc: e.matmul(G_ps, WXQ[:, kc, 128 * c2:128 * (c2 + 1)], HTa[:, kc, :],
                                                        start=(kc == 0), stop=(kc == 7)),
                              reads=[bWXQ, bHTa], writes=[bG])

                def s_q2(c2=c2):
                    mk.act(lambda e: e.activation(out=QM[:, c2, :], in_=G_ps, func=AF.Copy, scale=0.125),
                           reads=[bG], writes=[bQM])
                st.append(s_q)
                st.append(s_q2)
            for hm in range(4):
                c2, base = hm // 2, 64 * (hm % 2)
                for mt in range(2):
                    def s_qk(c2=c2, base=base, mt=mt):
                        mk.pe(lambda e: e.matmul(Sm_ps, KMT[base:base + 64, c2, 128 * mt:128 * (mt + 1)], QM[base:base + 64, c2, :],
                                                 start=True, stop=True), reads=[bKMT, bQM], writes=[bSm])

                    def s_ex(mt=mt):
                        mk.act(lambda e: e.activation(out=PM2[mt], in_=Sm_ps, func=AF.Exp), reads=[bSm], writes=[bPM2[mt]])

                    def s_pv(mt=mt, hm=hm):
                        mk.pe(lambda e: e.matmul(Om_ps, VMp[:, mt, hm, :], PM2[mt], start=(mt == 0), stop=(mt == 1)),
                              reads=[bVMp, bPM2[mt]], writes=[bOm])
                    st.append(s_qk)
                    st.append(s_ex)
                    st.append(s_pv)

                def s_nrm(c2=c2, base=base):
                    mk.dve(lambda e: e.reciprocal(out=RCPm[0:64, :], in_=Om_ps[64:128, :]), reads=[bOm], writes=[bRCPm])
                    mk.dve(lambda e: e.tensor_tensor(out=OMT[base:base + 64, c2, :], in0=Om_ps[0:64, :],
                                                     in1=RCPm[0:64, :], op=ALU.mult),
                           reads=[bOm, bRCPm], writes=[bOMT])
                st.append(s_nrm)
            for t in range(4):
                for hh in range(2):
                    def s_wxo(t=t, hh=hh):
                        for c2 in range(2):
                            mk.pe(lambda e, c2=c2: e.matmul(PSB[YB + hh], OMT[:, c2, 128 * t:128 * (t + 1)],
                                                            WXO[:, c2, 512 * hh:512 * (hh + 1)], start=(c2 == 0), stop=(c2 == 1)),
                                  reads=[bOMT, bWXO], writes=[bB[YB + hh]])
                    st.append(s_wxo)
                post_norm_stages(st, X, t, 1, bX[t], YB)
            for t in range(4):
                pre_norm_stages(st, X[:, t, :], [bX[t]], HT3[xs], bHT3[xs], 128 * t)
            return st

        out_ops = []

        def ffn(tb, xs, side):
            X, bX = X2[xs], bX2[xs]
            HTd, bHTd = HT3[xs], bHT3[xs]

            def drain(n):
                for _ in range(n):
                    if side:
                        side.pop(0)()
            for j in range(NJ):
                y = j % 2
                wg = wst["c"] % NWC
                wst["c"] += 1
                load_wb(Wg_b[j], WC[wg], bWC[wg])
                wu = wst["c"] % NWC
                wst["c"] += 1
                load_wb(Wu_b[j], WC[wu], bWC[wu])
                for kc in range(8):
                    mk.pe(lambda e, kc=kc, wg=wg, y=y: e.matmul(G_ps2[y], WC[wg][:, 128 * kc:128 * (kc + 1)], HTd[:, kc, :],
                                                                start=(kc == 0), stop=(kc == 7)),
                          reads=[bWC[wg], bHTd], writes=[bG2[y]])
                drain(2)
                for kc in range(8):
                    mk.pe(lambda e, kc=kc, wu=wu, y=y: e.matmul(U_ps2[y], WC[wu][:, 128 * kc:128 * (kc + 1)], HTd[:, kc, :],
                                                                start=(kc == 0), stop=(kc == 7)),
                          reads=[bWC[wu], bHTd], writes=[bU2[y]])
                mk.act(lambda e, y=y: e.activation(out=SG2[y], in_=G_ps2[y], func=AF.Exp, scale=-1.0), reads=[bG2[y]], writes=[bSG2[y]])
                mk.act(lambda e, y=y: e.activation(out=SG2[y], in_=SG2[y], func=AF.Ln, bias=1.0, scale=1.0), reads=[bSG2[y]], writes=[bSG2[y]])
                mk.act(lambda e, y=y: e.activation(out=SG2[y], in_=SG2[y], func=AF.Exp, scale=-1.0), reads=[bSG2[y]], writes=[bSG2[y]])
                mk.dve(lambda e, y=y: e.tensor_tensor(out=SG2[y], in0=SG2[y], in1=G_ps2[y], op=ALU.mult),
                       reads=[bSG2[y], bG2[y]], writes=[bSG2[y]])
                mk.dve(lambda e, j=j, y=y: e.tensor_tensor(out=ATt[:, j, :], in0=SG2[y], in1=U_ps2[y], op=ALU.mult),
                       reads=[bSG2[y], bU2[y]], writes=[bAT])
                drain(2)
            for half in range(2):
                for j in range(NJ):
                    wd = wst["c"] % NWC
                    wst["c"] += 1
                    load_wb(Wd_b[j], WC[wd], bWC[wd])
                    for tt in range(2):
                        t = 2 * half + tt
                        for hh in range(2):
                            mk.pe(lambda e, t=t, tt=tt, j=j, hh=hh, wd=wd: e.matmul(PSB[2 * tt + hh], ATt[:, j, 128 * t:128 * (t + 1)],
                                                                                  WC[wd][:, 512 * hh:512 * (hh + 1)],
                                                                                  start=(j == 0), stop=(j == NJ - 1)),
                                  reads=[bAT, bWC[wd]], writes=[bB[2 * tt + hh]])
                    drain(2 if half == 0 else 1)
                while side:
                    side.pop(0)()
                for tt in range(2):
                    t = 2 * half + tt
                    stf = []
                    post_norm_stages(stf, X, t, 2, bX[t], 2 * tt)
                    run_all(stf)
                    gt = 4 * tb + t
                    out_ops.append(mk.dma(lambda e, t=t, gt=gt: e.dma_start(out=out_d[128 * gt:128 * (gt + 1), :], in_=X[:, t, :]),
                                          reads=[bX[t]], eng="sp"))
            while side:
                side.pop(0)()

        run_all(front_stages(0, 0))
        for tb in range(4):
            side = front_stages(tb + 1, (tb + 1) % 2) if tb + 1 < 4 else []
            ffn(tb, tb % 2, side)

        fin = list(out_ops)
        if debug:
            fin = [o for o in mk.ops if o.is_dma]
        seen = set()
        fin2 = []
        for o in fin:
            if id(o.key) not in seen:
                seen.add(id(o.key))
                fin2.append(o)
        mk.emit(final_ops=fin2)
    return nc


def _t5_bucket(dist):
    n_buckets, max_distance = 32, 2048
    max_exact = n_buckets // 2
    d = np.maximum(dist, 1).astype(np.float32)
    large = max_exact + (np.log(d / max_exact) / np.log(max_distance / max_exact)
                         * (n_buckets - max_exact)).astype(np.int32)
    large = np.minimum(large, n_buckets - 1)
    return np.where(dist < max_exact, dist, large).astype(np.int32)


def _prep_inputs(x, mem, g_mix_pre, w_in, b_f, rel_bias, w_out, g_mix_post, g_xattn_pre, g_mem,
                 w_xq, w_xk, w_xv, w_xo, g_xattn_post, g_ffn_pre, w_gate, w_up, w_down, g_ffn_post):
    f = lambda a: np.ascontiguousarray(np.asarray(a, dtype=np.float32))
    x = f(x)[0]
    mem = f(mem)[0]
    w_in = f(w_in)[0]
    xTn = np.ascontiguousarray(x.T)
    fq, fk, fv = w_in[:, 0:512], w_in[:, 512:1024], w_in[:, 1024:1536]
    gate = w_in[:, 1536:1544]
    dq, dk, dv = w_in[:, 1544:2056], w_in[:, 2056:2568], w_in[:, 2568:3080]
    w_a = np.ascontiguousarray(np.concatenate([fk, fv, fq, dq, dk, dv, gate], axis=1))

    def pk(g):
        return f(g)[0].reshape(8, 128).T
    gpre = np.ascontiguousarray(np.stack([pk(g_mix_pre), pk(g_xattn_pre), pk(g_mem), pk(g_ffn_pre)], 0))
    gpost = np.ascontiguousarray(np.stack([np.broadcast_to(f(g)[0][None, :], (128, 1024))
                                           for g in (g_mix_post, g_xattn_post, g_ffn_post)], 0))
    rb = f(rel_bias)
    p = np.arange(128)[:, None]
    q = np.arange(128)[None, :]
    dbiasc = np.empty((8, 3, 128, 512), np.float32)
    dbiasp = np.empty((8, 3, 128, 128), np.float32)
    for bi, d in enumerate((1, 4, 16)):
        dc = q - p
        dp = q + 128 - p
        bc = _t5_bucket(np.clip(dc, 0, 128) * d)
        bp = _t5_bucket(np.clip(dp, 0, 128) * d)
        for h in range(8):
            tc_ = np.where(dc >= 0, rb[bc, h], np.float32(NEG)).astype(np.float32)
            tp_ = np.where(dp <= 128, rb[bp, h], np.float32(NEG)).astype(np.float32)
            dbiasc[h, bi] = np.tile(tc_, (1, 4))
            dbiasp[h, bi] = tp_
    qq = np.arange(512)[None, :]
    negm = np.stack([np.where(qq >= 128 * j + p, 0.0, NEG).astype(np.float32) for j in range(4)], 0)
    ident = np.eye(128, dtype=np.float32)
    msel = np.zeros((128, 3), np.float32)
    for r in range(3):
        msel[64 + r, r] = 1.0
    common = {
        "memx": mem, "w_a": w_a, "gpre": gpre, "bf": f(b_f)[0].reshape(8, 1), "dbiasc": dbiasc, "dbiasp": dbiasp,
        "negm": negm, "ident": ident, "msel": msel, "w_out": f(w_out)[0], "w_xq": f(w_xq)[0], "w_xk": f(w_xk)[0],
        "w_xv": f(w_xv)[0], "w_xo": f(w_xo)[0], "w_gate": f(w_gate)[0], "w_up": f(w_up)[0], "w_down": f(w_down)[0],
        "gpost": gpost,
    }
    in_maps = []
    for c in range(NCORE):
        roll = (c + 1) * OWN
        p0 = S - roll
        idx = (np.arange(S) + roll) % S
        m = dict(common)
        m["xT"] = np.ascontiguousarray(xTn[:, idx])
        m["x_own"] = np.ascontiguousarray(x[c * OWN:(c + 1) * OWN])
        pos = np.arange(S).reshape(128, 128).T
        m["keymask"] = np.where(pos >= p0, 0.0, NEG).astype(np.float32)
        m["halomask"] = np.full((128, 1), NEG if c == 0 else 0.0, np.float32)
        in_maps.append(m)
    return in_maps


_CACHE = {}


def kernel(**inputs):
    in_maps = _prep_inputs(**inputs)
    if "nc" not in _CACHE:
        _CACHE["nc"] = build_program(debug=False)
    nc = _CACHE["nc"]
    res = run_bass_kernel_spmd(nc, in_maps, core_ids=list(range(NCORE)))
    outs = [np.asarray(r["out"], dtype=np.float32) for r in res.results]
    return np.concatenate(outs, axis=0)[None, :, :]
```

```python
import numpy as np
from contextlib import ExitStack
import concourse.bass as bass
import concourse.mybir as mybir
from concourse.bass_utils import run_bass_kernel_spmd

F32 = mybir.dt.float32
BF16 = mybir.dt.bfloat16
AF = mybir.ActivationFunctionType
ALU = mybir.AluOpType

COMPUTE = ("pe", "act", "dve", "pool")
EPOCH = 16000
NEG = -30000.0
S = 16384
OWN = 2048
NCORE = 8
EPS = 1e-6
DFF = 2816
NJ = DFF // 128


class Buf:
    __slots__ = ("name", "last_writer", "readers", "dma_count", "sem", "inc_amt")

    def __init__(self, name):
        self.name = name
        self.last_writer = None
        self.readers = []
        self.dma_count = 0
        self.sem = None
        self.inc_amt = 16


class Op:
    __slots__ = ("idx", "eng", "fn", "deps", "is_dma", "key", "needs_signal", "sigval", "bar")

    def __init__(self, idx, eng, fn, is_dma, key):
        self.idx = idx
        self.eng = eng
        self.fn = fn
        self.deps = []
        self.is_dma = is_dma
        self.key = key
        self.needs_signal = False
        self.sigval = None
        self.bar = None


class MK:
    def __init__(self, nc):
        self.nc = nc
        self.ops = []
        self.streams = {"pe": [], "act": [], "dve": [], "pool": [], "sp": []}
        self.dma_keys = []
        self.pending_bar = {}

    def buf(self, name):
        return Buf(name)

    def barrier(self):
        last = [self.streams[e][-1] for e in COMPUTE if self.streams[e]]
        last = [o for o in last if not o.is_dma]
        lastc = []
        for e in COMPUTE:
            for o in reversed(self.streams[e]):
                if not o.is_dma:
                    lastc.append(o)
                    break
        for o in lastc:
            o.needs_signal = True
        dstate = [(k, k.dma_count) for k in self.dma_keys if k.dma_count > 0]
        for e in self.streams:
            self.pending_bar[e] = (lastc, dstate)

    def op(self, eng, fn, reads=(), writes=(), dma=False):
        key = None
        if dma:
            for h in writes:
                if not h.name.startswith("dram:"):
                    key = h
                    break
            if key is None:
                for h in reads:
                    if not h.name.startswith("dram:"):
                        key = h
                        break
            if key is None:
                key = reads[0] if reads else writes[0]
            if key not in self.dma_keys:
                self.dma_keys.append(key)
        o = Op(len(self.ops), eng, fn, dma, key)
        if eng in self.pending_bar:
            o.bar = self.pending_bar.pop(eng)
        deps = {}

        def add_dep(d):
            if d is None or d is o:
                return
            if (not d.is_dma) and d.eng == "pe" and eng == "pe" and not dma:
                return
            if d.is_dma:
                deps[d.idx] = (d, d.key.dma_count)
            else:
                deps[d.idx] = (d, None)
            d.needs_signal = True

        for h in reads:
            add_dep(h.last_writer)
        for h in writes:
            add_dep(h.last_writer)
            for r in h.readers:
                add_dep(r)
        if dma:
            key.dma_count += 1
        for h in writes:
            h.last_writer = o
            h.readers = []
        for h in reads:
            if h.last_writer is not o:
                h.readers.append(o)
        o.deps = list(deps.values())
        self.ops.append(o)
        self.streams[eng].append(o)
        return o

    def pe(self, fn, reads=(), writes=()):
        return self.op("pe", fn, reads, writes)

    def act(self, fn, reads=(), writes=()):
        return self.op("act", fn, reads, writes)

    def dve(self, fn, reads=(), writes=()):
        return self.op("dve", fn, reads, writes)

    def pool(self, fn, reads=(), writes=()):
        return self.op("pool", fn, reads, writes)

    def dma(self, fn, reads=(), writes=(), eng="sp"):
        return self.op(eng, fn, reads, writes, dma=True)

    def emit(self, final_ops=()):
        nc = self.nc
        nep = {}
        for e in COMPUTE:
            c = 0
            for o in self.streams[e]:
                if o.is_dma:
                    continue
                if o.needs_signal:
                    c += 1
                    o.sigval = ((c - 1) // EPOCH, (c - 1) % EPOCH + 1)
            nep[e] = max(c - 1, 0) // EPOCH + 1
        with ExitStack() as st:
            esem = {e: [st.enter_context(nc.semaphore(f"s_{e}{j}")) for j in range(nep[e])]
                    for e in COMPUTE}
            for i, k in enumerate(self.dma_keys):
                k.sem = st.enter_context(nc.semaphore(f"d_{i}"))
            block = st.enter_context(nc.Block())
            engobj = {"pe": "tensor", "act": "scalar", "dve": "vector", "pool": "gpsimd",
                      "sp": "sync"}
            fin = [(o.key, o.key.inc_amt * o.key.dma_count) for o in final_ops]

            def make_body(ename):
                stream = self.streams[ename]

                def body(eng):
                    known = {}

                    def wait(sem, val):
                        kid = id(sem)
                        if known.get(kid, 0) >= val:
                            return
                        known[kid] = val
                        eng.wait_ge(sem, val)

                    for o in stream:
                        if o.bar is not None:
                            lastc, dstate = o.bar
                            for d in lastc:
                                wait(esem[d.eng][d.sigval[0]], d.sigval[1])
                            for (k, cnt) in dstate:
                                wait(k.sem, k.inc_amt * cnt)
                        for (d, dv) in o.deps:
                            if d.is_dma:
                                wait(d.key.sem, d.key.inc_amt * dv)
                            else:
                                wait(esem[d.eng][d.sigval[0]], d.sigval[1])
                        ins = o.fn(eng)
                        if o.is_dma:
                            ins.then_inc(o.key.sem, o.key.inc_amt)
                        elif o.needs_signal:
                            ins.then_inc(esem[o.eng][o.sigval[0]], 1)
                    if ename == "sp":
                        for (k, v) in fin:
                            eng.wait_ge(k.sem, v)
                return body

            for ename in ("sp", "pe", "act", "dve", "pool"):
                if not self.streams[ename] and not (ename == "sp" and fin):
                    continue
                getattr(block, engobj[ename])(make_body(ename))


class Arena:
    def __init__(self, t, nfloats):
        self.t = t
        self.n = nfloats
        self.off = 0

    def alloc(self, nelem, dt=F32):
        nb = nelem * (4 if dt == F32 else 2)
        nf = ((nb + 3) // 4 + 15) // 16 * 16
        ap = self.t[:, self.off:self.off + nf]
        self.off += nf
        assert self.off <= self.n, ("SBUF arena overflow", self.off, self.n)
        if dt != F32:
            ap = ap.bitcast(dt)
        return ap[:, 0:nelem]


def build_program(debug=False):
    nc = bass.Bass("TRN2", target_bir_lowering=False)
    mk = MK(nc)

    def din(name, shape, dt=F32):
        return nc.dram_tensor(name, shape, dt, kind="ExternalInput").ap()

    xT = din("xT", [1024, S])
    x_own = din("x_own", [OWN, 1024])
    memx = din("memx", [256, 1024])
    keymask_d = din("keymask", [128, 128])
    halomask_d = din("halomask", [128, 1])
    w_a = din("w_a", [1024, 3080])
    gpre_d = din("gpre", [4, 128, 8])
    bf_d = din("bf", [8, 1])
    dbiasc_d = din("dbiasc", [8, 3, 128, 512])
    dbiasp_d = din("dbiasp", [8, 3, 128, 128])
    negm_d = din("negm", [4, 128, 512])
    ident_d = din("ident", [128, 128])
    msel_d = din("msel", [128, 3])
    w_out_d = din("w_out", [1024, 1024])
    w_xq_d = din("w_xq", [1024, 256])
    w_xk_d = din("w_xk", [1024, 256])
    w_xv_d = din("w_xv", [1024, 256])
    w_xo_d = din("w_xo", [256, 1024])
    w_gate_d = din("w_gate", [1024, DFF])
    w_up_d = din("w_up", [1024, DFF])
    w_down_d = din("w_down", [DFF, 1024])
    gpost_d = din("gpost", [3, 128, 1024])
    out_d = nc.dram_tensor("out", [OWN, 1024], F32, kind="ExternalOutput").ap()

    skind = "ExternalOutput" if debug else "Internal"

    def dscr(name, shape, dt):
        if debug:
            return nc.dram_tensor(name, shape, dt, kind="ExternalOutput").ap()
        return nc.dram_tensor(name, shape, dt).ap()

    KT_s = dscr("KT_s", [8, 64, S], BF16)
    V_s = dscr("V_s", [8, 128, 128 * 128], BF16)
    QT_s = dscr("QT_s", [8, 64, OWN], BF16)
    C_s = dscr("C_s", [8, OWN], F32)
    DQ_s = dscr("DQ_s", [8, 64, OWN], BF16)
    DK_s = dscr("DK_s", [8, 64, 2 * OWN], BF16)
    DV_s = dscr("DV_s", [8, 64, 2 * OWN], BF16)
    Wout_b = dscr("Wout_b", [1024, 1024], BF16)
    Wxq_b = dscr("Wxq_b", [1024, 256], BF16)
    Wxk_b = dscr("Wxk_b", [1024, 256], BF16)
    Wxv_b = dscr("Wxv_b", [1024, 256], BF16)
    Wxo_b = dscr("Wxo_b", [256, 1024], BF16)
    Wg_b = dscr("Wg_b", [NJ, 128, 1024], BF16)
    Wu_b = dscr("Wu_b", [NJ, 128, 1024], BF16)
    Wd_b = dscr("Wd_b", [NJ, 128, 1024], BF16)
    if debug:
        OT_dbg = nc.dram_tensor("OT_dbg", [128, 16 * 1024], BF16, kind="ExternalOutput").ap()
        NC_dbg = nc.dram_tensor("NC_dbg", [128, 1024], F32, kind="ExternalOutput").ap()

    dbufs = {}

    def DB(name, idx=0):
        k = (name, idx)
        if k not in dbufs:
            dbufs[k] = mk.buf(f"dram:{name}{idx}")
        return dbufs[k]

    with ExitStack() as st:
        ARENA_F = 49 * 1024
        arena_t = st.enter_context(nc.sbuf_tensor("arena", [128, ARENA_F], F32))
        ar = Arena(arena_t, ARENA_F)
        PSALL = st.enter_context(nc.psum_tensor("psall", [128, 4096], F32))
        PSB = [PSALL[:, 512 * i:512 * (i + 1)] for i in range(8)]

        def B(name):
            return mk.buf(name)

        OTall = ar.alloc(8 * OWN, BF16).rearrange("p (k n) -> p k n", n=OWN)
        bOT = [B(f"OTall{k}") for k in range(8)]
        IDF = ar.alloc(128, F32)
        IDB = ar.alloc(128, BF16)
        ONESB = ar.alloc(128, BF16)
        MSEL = ar.alloc(3, F32)
        KMASK = ar.alloc(128, F32)
        HMASK = ar.alloc(1, F32)
        GPRE = ar.alloc(32, F32).rearrange("p (g k) -> p g k", k=8)
        NBF = ar.alloc(1, F32)
        ONES8 = ar.alloc(512, F32)
        bC = B("consts")

        mk.dma(lambda e: e.dma_start(out=IDF, in_=ident_d), writes=[bC])
        mk.dma(lambda e: e.dma_start(out=MSEL, in_=msel_d), writes=[bC])
        mk.dma(lambda e: e.dma_start(out=KMASK, in_=keymask_d), writes=[bC])
        mk.dma(lambda e: e.dma_start(out=HMASK, in_=halomask_d), writes=[bC])
        mk.dma(lambda e: e.dma_start(out=GPRE, in_=gpre_d.rearrange("g p k -> p g k")), writes=[bC])
        mk.dma(lambda e: e.dma_start(out=NBF[0:8, :], in_=bf_d), writes=[bC])
        bC2 = B("consts2")
        mk.dve(lambda e: e.tensor_copy(out=IDB, in_=IDF), reads=[bC], writes=[bC2])
        mk.dve(lambda e: e.memset(ONESB, 1.0), writes=[bC2])
        mk.dve(lambda e: e.memset(ONES8, 1.0), writes=[bC2])
        mk.dve(lambda e: e.tensor_scalar(out=NBF[0:8, :], in0=NBF[0:8, :], scalar1=-1.0, scalar2=None,
                                         op0=ALU.mult), reads=[bC], writes=[bC, bC2])
        markD = ar.off
        negc_all = ar.alloc(128 * 8, F32).rearrange("p (t h) -> p t h", h=8)
        bNC = [B(f"negc{b}") for b in range(32)]
        persist_mark = ar.off

        WA = ar.alloc(8 * 3080, BF16).rearrange("p (k c) -> p k c", c=3080)
        bWAc = [B(f"WA{i}") for i in range(7)]
        XTraw = [ar.alloc(8 * 512, F32) for _ in range(2)]
        XT = [a.rearrange("p (k n) -> p k n", n=512) for a in XTraw]
        bXT = [B("XT0"), B("XT1")]
        w_a_v = w_a.rearrange("(k p) c -> p k c", p=128)

        XSQ = ar.alloc(8 * 512, BF16).rearrange("p (k n) -> p k n", n=512)
        bXSQ = B("XSQ")
        HT = [ar.alloc(8 * 512, BF16).rearrange("p (k n) -> p k n", n=512) for _ in range(2)]
        bHT = [B("HT0"), B("HT1")]
        LN = ar.alloc(512, F32)
        bLN = B("LN")
        RSTD = ar.alloc(512, F32)
        bRSTD = B("RSTD")
        EST = [ar.alloc(512, BF16) for _ in range(3)]
        bEST = [B(f"EST{i}") for i in range(3)]
        VT = [ar.alloc(512, BF16) for _ in range(2)]
        bVT = [B("VT0"), B("VT1")]
        VST = ar.alloc(8 * 8 * 128, BF16).rearrange("p (h t d) -> p h t d", h=8, t=8)
        bVST = B("VST")
        GE = ar.alloc(512, F32)
        bGE = B("GE")
        GSP = ar.alloc(512, F32)
        bGSP = B("GSP")
        CS = [ar.alloc(512, F32) for _ in range(2)]
        bCS = [B("CS0"), B("CS1")]
        NCS = ar.alloc(512, F32)
        bNCS = B("NCS")


        SS_ps, bSS = PSB[0], B("SS_ps")
        PR_ps = [PSB[1], PSB[2], PSB[3]]
        bPR = [B("PR0"), B("PR1"), B("PR2")]
        TPb_ps = PSB[4].bitcast(BF16)
        bTPb = B("TPb")
        TPc_ps, bTPc = PSB[5], B("TPc")

        xT_v = xT.rearrange("(k p) n -> p k n", p=128)
        KT_v = KT_s.rearrange("h d n -> (h d) n")
        QT_v = QT_s.rearrange("h d n -> (h d) n")
        DQ_v = DQ_s.rearrange("h d n -> (h d) n")
        DK_v = DK_s.rearrange("h d n -> (h d) n")
        DV_v = DV_s.rearrange("h d n -> (h d) n")
        V_v = V_s.rearrange("h p n -> p h n")
        state = {"pr": 0, "est": 0}

        def project(hs, col0, M, rd_extra=()):
            k = state["pr"] % 3
            state["pr"] += 1
            for kc in range(8):
                mk.pe(lambda e, kc=kc, k=k: e.matmul(PR_ps[k][0:M, :], WA[:, kc, col0:col0 + M], HT[hs][:, kc, :],
                                                       start=(kc == 0), stop=(kc == 7)),
                      reads=bWAc[col0 // 440:(col0 + M - 1) // 440 + 1] + [bHT[hs]], writes=[bPR[k]])
            return k

        def evac_store(k, dst_ap, dst_buf, scale=1.0):
            s = state["est"] % 3
            state["est"] += 1
            mk.act(lambda e: e.activation(out=EST[s], in_=PR_ps[k], func=AF.Copy, scale=scale),
                   reads=[bPR[k]], writes=[bEST[s]])
            mk.dma(lambda e: e.dma_start(out=dst_ap, in_=EST[s]), reads=[bEST[s]], writes=[dst_buf], eng="act")

        NBLK = S // 512

        def xload(b):
            xs = b % 2
            mk.dma(lambda e: e.dma_start(out=XT[xs], in_=xT_v[:, :, 512 * b:512 * (b + 1)]), writes=[bXT[xs]])

        def stats(b):
            xs = b % 2
            hs = b % 2
            mk.pool(lambda e: e.tensor_tensor(out=XSQ, in0=XT[xs], in1=XT[xs], op=ALU.mult),
                    reads=[bXT[xs]], writes=[bXSQ])
            for kc in range(8):
                mk.pe(lambda e, kc=kc: e.matmul(SS_ps, ONESB, XSQ[:, kc, :], start=(kc == 0), stop=(kc == 7)),
                      reads=[bXSQ, bC2], writes=[bSS])
            mk.act(lambda e: e.activation(out=LN, in_=SS_ps, func=AF.Ln, bias=EPS, scale=1.0 / 1024),
                   reads=[bSS], writes=[bLN])
            mk.act(lambda e: e.activation(out=RSTD, in_=LN, func=AF.Exp, scale=-0.5), reads=[bLN], writes=[bRSTD])
            for kc in range(8):
                fn = (lambda e, kc=kc: e.tensor_tensor(out=HT[hs][:, kc, :], in0=XT[xs][:, kc, :], in1=RSTD, op=ALU.mult))
                if kc % 4 == 3:
                    mk.pool(fn, reads=[bXT[xs], bRSTD], writes=[bHT[hs]])
                else:
                    mk.dve(fn, reads=[bXT[xs], bRSTD], writes=[bHT[hs]])

        def cs_transposes(bb):
            cs = bb % 2
            for j in range(4):
                mk.pe(lambda e, j=j: e.transpose(TPc_ps[:, 8 * j:8 * j + 8], CS[cs][0:8, 128 * j:128 * (j + 1)],
                                                 IDF[0:8, 0:8]), reads=[bCS[cs], bC], writes=[bTPc])
            for j in range(4):
                kt = 4 * bb + j
                mk.dve(lambda e, j=j, kt=kt: e.tensor_scalar(out=negc_all[:, kt, :], in0=TPc_ps[:, 8 * j:8 * j + 8],
                                                             scalar1=KMASK[:, kt:kt + 1], scalar2=None, op0=ALU.add),
                       reads=[bTPc, bC], writes=[bNC[bb]])

        mk.pool(lambda e: e.memset(VST.rearrange("p h t d -> p (h t d)"), 1.0), writes=[bVST])
        xload(0)
        xload(1)
        stats(0)
        OTf = OTall.rearrange("p k n -> p (k n)").bitcast(F32)
        WST = [OTf[:, 4096 * i:4096 * i + 8 * 440].rearrange("p (k c) -> p k c", c=440) for i in range(2)]
        bWST = [B("WST0"), B("WST1")]
        for i in range(7):
            s = i % 2
            mk.dma(lambda e, i=i, s=s: e.dma_start(out=WST[s], in_=w_a_v[:, :, 440 * i:440 * (i + 1)]),
                   writes=[bWST[s]])
            for kc in range(8):
                if kc % 2 == 0:
                    mk.dve(lambda e, i=i, s=s, kc=kc: e.tensor_scalar(out=WA[:, kc, 440 * i:440 * (i + 1)], in0=WST[s][:, kc, :],
                                                                      scalar1=GPRE[:, 0, kc:kc + 1], scalar2=None, op0=ALU.mult),
                           reads=[bWST[s], bC], writes=[bWAc[i]])
                else:
                    mk.act(lambda e, i=i, s=s, kc=kc: e.activation(out=WA[:, kc, 440 * i:440 * (i + 1)], in_=WST[s][:, kc, :],
                                                                   func=AF.Copy, scale=GPRE[:, 0, kc:kc + 1]),
                           reads=[bWST[s], bC], writes=[bWAc[i]])
        for b in range(NBLK):
            xs = b % 2
            hs = b % 2
            if b + 1 < NBLK:
                stats(b + 1)
            if b + 2 < NBLK:
                xload(b + 2)
            if b >= 1:
                cs_transposes(b - 1)
            for i in range(4):
                k = project(hs, 128 * i, 128)
                evac_store(k, KT_v[128 * i:128 * (i + 1), 512 * b:512 * (b + 1)], DB("KT", b))
            def v_transposes(i, vs):
                for j in range(4):
                    mk.pe(lambda e, j=j, vs=vs: e.transpose(TPb_ps[:, 128 * j:128 * (j + 1)],
                                                            VT[vs][:, 128 * j:128 * (j + 1)], IDB),
                          reads=[bVT[vs], bC2], writes=[bTPb])
                t0 = 4 * (b % 2)
                mk.dve(lambda e, i=i, t0=t0: e.tensor_copy(
                    out=VST[:, 2 * i:2 * i + 2, t0:t0 + 4, 0:64],
                    in_=TPb_ps[:, 0:512].rearrange("p (t h d) -> p h t d", t=4, h=2)),
                    reads=[bTPb], writes=[bVST])
            for i in range(4):
                k = project(hs, 512 + 128 * i, 128)
                vs = i % 2
                mk.act(lambda e, k=k, vs=vs: e.activation(out=VT[vs], in_=PR_ps[k], func=AF.Copy),
                       reads=[bPR[k]], writes=[bVT[vs]])
                if i >= 1:
                    v_transposes(i - 1, (i - 1) % 2)
            k = project(hs, 3072, 8)
            v_transposes(3, 1)
            if b % 2 == 1:
                sb_ = b // 2
                mk.dma(lambda e, sb_=sb_: e.dma_start(out=V_v[:, :, 1024 * sb_:1024 * (sb_ + 1)],
                                                      in_=VST.rearrange("p h t d -> p h (t d)")),
                       reads=[bVST], writes=[DB("V", sb_)], eng="act")
            mk.act(lambda e, k=k: e.activation(out=GE[0:8, :], in_=PR_ps[k][0:8, :], func=AF.Exp,
                                               bias=NBF[0:8, :], scale=-1.0), reads=[bPR[k], bC2], writes=[bGE])
            mk.act(lambda e: e.activation(out=GSP[0:8, :], in_=GE[0:8, :], func=AF.Ln, bias=1.0, scale=1.0),
                   reads=[bGE], writes=[bGSP])
            cs = b % 2
            if b == 0:
                mk.dve(lambda e: e.tensor_tensor_scan(out=CS[0][0:8, :], data0=ONES8[0:8, :], data1=GSP[0:8, :],
                                                      initial=0.0, op0=ALU.mult, op1=ALU.add),
                       reads=[bGSP, bC2], writes=[bCS[0]])
            else:
                mk.dve(lambda e, cs=cs: e.tensor_tensor_scan(out=CS[cs][0:8, :], data0=ONES8[0:8, :],
                                                             data1=GSP[0:8, :], initial=CS[1 - cs][0:8, 511:512],
                                                             op0=ALU.mult, op1=ALU.add),
                       reads=[bGSP, bC2, bCS[1 - cs]], writes=[bCS[cs]])
            if b >= 28:
                ob = b - 28
                mk.dve(lambda e, cs=cs: e.tensor_scalar(out=NCS[0:8, :], in0=CS[cs][0:8, :], scalar1=-1.0, scalar2=None,
                                                        op0=ALU.mult), reads=[bCS[cs]], writes=[bNCS])
                mk.dma(lambda e, ob=ob: e.dma_start(out=C_s[:, 512 * ob:512 * (ob + 1)], in_=NCS[0:8, :]),
                       reads=[bNCS], writes=[DB("C", ob)], eng="act")
                for i in range(4):
                    k = project(hs, 1024 + 128 * i, 128)
                    evac_store(k, QT_v[128 * i:128 * (i + 1), 512 * ob:512 * (ob + 1)], DB("QT", ob), scale=0.125)
                for i in range(4):
                    k = project(hs, 1536 + 128 * i, 128)
                    evac_store(k, DQ_v[128 * i:128 * (i + 1), 512 * ob:512 * (ob + 1)], DB("DQ", ob), scale=0.125)
            if b >= 24:
                wb = b - 24
                for i in range(4):
                    k = project(hs, 2048 + 128 * i, 128)
                    evac_store(k, DK_v[128 * i:128 * (i + 1), 512 * wb:512 * (wb + 1)], DB("DK", wb))
                for i in range(4):
                    k = project(hs, 2560 + 128 * i, 128)
                    evac_store(k, DV_v[128 * i:128 * (i + 1), 512 * wb:512 * (wb + 1)], DB("DV", wb))

        cs_transposes(NBLK - 1)
        if debug:
            mk.dma(lambda e: e.dma_start(out=NC_dbg, in_=negc_all.rearrange("p t h -> p (t h)")), reads=bNC, eng="act")

        mk.barrier()
        ar.off = persist_mark
        KR = [ar.alloc(4096, BF16) for _ in range(4)]
        bKR = [B(f"KR{i}") for i in range(4)]
        VR = [ar.alloc(32 * 128, BF16).rearrange("p (t d) -> p t d", d=128) for _ in range(4)]
        bVR = [B(f"VR{i}") for i in range(4)]
        QTp2 = [ar.alloc(OWN, BF16) for _ in range(2)]
        bQTp2 = [B("QTp0"), B("QTp1")]
        CR = ar.alloc(OWN, F32)
        bCR = B("CR")
        T0b = ar.alloc(512, BF16); T1b = ar.alloc(512, BF16); T2b = ar.alloc(512, BF16)
        R1 = ar.alloc(512, F32); R2 = ar.alloc(512, F32); A0 = ar.alloc(512, F32); A1 = ar.alloc(512, F32)
        bSPL = B("split")
        NEGMB = ar.alloc(4 * 512, BF16).rearrange("p (j n) -> p j n", n=512)
        NEGMF = ar.alloc(512, F32)
        bNEGMF = B("NEGMF")
        bNEGM = B("NEGM")
        PT = [ar.alloc(1024, BF16) for _ in range(3)]
        bPT = [B(f"PT{i}") for i in range(3)]

        O_ps = [PSB[0], PSB[1], PSB[2], PSB[3]]
        bOp = [B(f"Op{i}") for i in range(4)]
        SP2 = [PSALL[:, 2048:3072], PSALL[:, 3072:4096]]
        bSP2 = [B("SP0"), B("SP1")]

        for j in range(4):
            mk.dma(lambda e, j=j: e.dma_start(out=NEGMF, in_=negm_d[j]), writes=[bNEGMF])
            mk.pool(lambda e, j=j: e.tensor_copy(out=NEGMB[:, j, :], in_=NEGMF), reads=[bNEGMF], writes=[bNEGM])
        ONES_PAIR = 1.0019378662109375
        for i in range(2):
            mk.dve(lambda e, i=i: e.memset(QTp2[i][64:128, :].bitcast(F32), 0.0), writes=[bQTp2[i]])
        for i in range(4):
            if i == 0:
                mk.dve(lambda e, i=i: e.memset(KR[i][64:128, :].bitcast(F32), 0.0), writes=[bKR[i]])
                mk.dve(lambda e, i=i: e.memset(KR[i][64:67, :].bitcast(F32), ONES_PAIR), writes=[bKR[i]])
            else:
                mk.pool(lambda e, i=i: e.memset(KR[i][64:128, :].bitcast(F32), 0.0), writes=[bKR[i]])
                mk.pool(lambda e, i=i: e.memset(KR[i][64:67, :].bitcast(F32), ONES_PAIR), writes=[bKR[i]])

        allKT = [DB("KT", b) for b in range(32)]
        allV = [DB("V", s_) for s_ in range(16)]
        allQT = [DB("QT", o) for o in range(4)]
        allC = [DB("C", o) for o in range(4)]
        WSb = [ar.alloc(1024, F32) for _ in range(3)]
        bWSb = [B(f"WSb{i}") for i in range(3)]
        WBb = [ar.alloc(1024, BF16) for _ in range(3)]
        bWBb = [B(f"WBb{i}") for i in range(3)]
        v3 = lambda a: a.rearrange("p (k c) -> p k c", c=128)
        precast = []
        for kc in range(8):
            precast.append((w_out_d[128 * kc:128 * (kc + 1), :], 1024, None, Wout_b[128 * kc:128 * (kc + 1), :], None))
        for (wd_, wb_, gi_) in ((w_xq_d, Wxq_b, 1), (w_xk_d, Wxk_b, 2), (w_xv_d, Wxv_b, 2)):
            for kc in range(8):
                precast.append((wd_[128 * kc:128 * (kc + 1), :], 256, None, wb_[128 * kc:128 * (kc + 1), :], (gi_, kc)))
        for c2 in range(2):
            precast.append((w_xo_d[128 * c2:128 * (c2 + 1), :], 1024, None, Wxo_b[128 * c2:128 * (c2 + 1), :], None))
        for j in range(NJ):
            precast.append((w_gate_d.rearrange("(k p) c -> p k c", p=128)[:, :, 128 * j:128 * (j + 1)], 1024, v3, Wg_b[j], (3, None)))
            precast.append((w_up_d.rearrange("(k p) c -> p k c", p=128)[:, :, 128 * j:128 * (j + 1)], 1024, v3, Wu_b[j], (3, None)))
            precast.append((w_down_d[128 * j:128 * (j + 1), :], 1024, None, Wd_b[j], None))
        pc_state = {"i": 0}

        def emit_precast(n):
            for _ in range(n):
                i = pc_state["i"]
                if i >= len(precast):
                    return
                pc_state["i"] += 1
                src_ap, ncols, view, dst_ap, gspec = precast[i]
                sl = i % 3
                stg = WSb[sl][:, 0:ncols]
                stg_v = view(stg) if view is not None else stg
                wb = WBb[sl][:, 0:ncols]
                mk.dma(lambda e, stg_v=stg_v, src_ap=src_ap: e.dma_start(out=stg_v, in_=src_ap), writes=[bWSb[sl]])
                if gspec is None:
                    mk.pool(lambda e, wb=wb, stg=stg: e.tensor_copy(out=wb, in_=stg), reads=[bWSb[sl]], writes=[bWBb[sl]])
                elif gspec[1] is not None:
                    gi_, kc_ = gspec
                    mk.pool(lambda e, wb=wb, stg=stg, gi_=gi_, kc_=kc_: e.tensor_scalar(
                        out=wb, in0=stg, scalar1=GPRE[:, gi_, kc_:kc_ + 1], scalar2=None, op0=ALU.mult),
                        reads=[bWSb[sl], bC], writes=[bWBb[sl]])
                else:
                    gi_ = gspec[0]
                    for kc_ in range(8):
                        mk.pool(lambda e, wb=wb, stg=stg, gi_=gi_, kc_=kc_: e.tensor_scalar(
                            out=wb[:, 128 * kc_:128 * (kc_ + 1)], in0=stg[:, 128 * kc_:128 * (kc_ + 1)],
                            scalar1=GPRE[:, gi_, kc_:kc_ + 1], scalar2=None, op0=ALU.mult),
                            reads=[bWSb[sl], bC], writes=[bWBb[sl]])
                mk.dma(lambda e, wb=wb, dst_ap=dst_ap: e.dma_start(out=dst_ap, in_=wb), reads=[bWBb[sl]],
                       writes=[DB("Wb", i)])

        P3 = slice(64, 67)

        def prep_head(h):
            QTp = QTp2[h % 2]
            bQ = bQTp2[h % 2]
            mk.dma(lambda e: e.dma_start(out=QTp[0:64, :], in_=QT_s[h]), reads=allQT, writes=[bQ])
            for r in range(3):
                mk.dma(lambda e, r=r: e.dma_start(out=CR[64 + r:65 + r, :], in_=C_s[h:h + 1, :]),
                       reads=allC, writes=[bCR])
            for q4 in range(4):
                cs_ = slice(512 * q4, 512 * (q4 + 1))
                mk.dve(lambda e, cs_=cs_: e.tensor_copy(out=T0b[P3, :], in_=CR[P3, cs_]), reads=[bCR], writes=[bSPL])
                mk.dve(lambda e, cs_=cs_: e.tensor_tensor(out=R1[P3, :], in0=CR[P3, cs_], in1=T0b[P3, :], op=ALU.subtract),
                       reads=[bCR, bSPL], writes=[bSPL])
                mk.dve(lambda e: e.tensor_copy(out=T1b[P3, :], in_=R1[P3, :]), reads=[bSPL], writes=[bSPL])
                mk.dve(lambda e: e.tensor_tensor(out=R2[P3, :], in0=R1[P3, :], in1=T1b[P3, :], op=ALU.subtract),
                       reads=[bSPL], writes=[bSPL])
                mk.dve(lambda e: e.tensor_copy(out=T2b[P3, :], in_=R2[P3, :]), reads=[bSPL], writes=[bSPL])
                mk.dve(lambda e: e.tensor_scalar(out=A0[P3, :], in0=T0b[P3, :], scalar1=MSEL[P3, 0:1], scalar2=None,
                                                 op0=ALU.mult), reads=[bSPL, bC], writes=[bSPL])
                mk.dve(lambda e: e.scalar_tensor_tensor(out=A1[P3, :], in0=T1b[P3, :], scalar=MSEL[P3, 1:2], in1=A0[P3, :],
                                                        op0=ALU.mult, op1=ALU.add), reads=[bSPL, bC], writes=[bSPL])
                mk.dve(lambda e, cs_=cs_: e.scalar_tensor_tensor(out=QTp[P3, cs_], in0=T2b[P3, :], scalar=MSEL[P3, 2:3],
                                                                 in1=A1[P3, :], op0=ALU.mult, op1=ALU.add),
                       reads=[bSPL, bC], writes=[bQ])

        def load_chunk(h, ci):
            mk.dma(lambda e: e.dma_start(out=KR[ci][0:64, :], in_=KT_s[h][:, 4096 * ci:4096 * (ci + 1)]),
                   reads=allKT, writes=[bKR[ci]])
            mk.dma(lambda e: e.dma_start(out=VR[ci].rearrange("p t d -> p (t d)"), in_=V_s[h][:, 4096 * ci:4096 * (ci + 1)]),
                   reads=allV, writes=[bVR[ci]])

        def last_kt(qb):
            return 115 + 4 * qb

        ust = {"sp": 0, "pt": 0}

        def unit_qk(h, kt, p):
            ci, kl = kt // 32, kt % 32
            QTp, bQ = QTp2[h % 2], bQTp2[h % 2]
            qbs = [qb for qb in (2 * p, 2 * p + 1) if kt <= last_kt(qb)]
            sl = ust["sp"] % 2
            ust["sp"] += 1
            for qb in qbs:
                diag = kt >= 112 + 4 * qb
                col = 512 * (qb % 2)
                mk.pe(lambda e, qb=qb, col=col, diag=diag: e.matmul(SP2[sl][:, col:col + 512], KR[ci][:, 128 * kl:128 * (kl + 1)],
                                                                   QTp[:, 512 * qb:512 * (qb + 1)], start=True, stop=not diag),
                      reads=[bKR[ci], bQ], writes=[bSP2[sl]])
                if diag:
                    j = kt - (112 + 4 * qb)
                    mk.pe(lambda e, col=col, j=j: e.matmul(SP2[sl][:, col:col + 512], IDB, NEGMB[:, j, :], start=False, stop=True),
                          reads=[bC2, bNEGM], writes=[bSP2[sl]])
            return (h, kt, p, qbs, sl)

        def unit_exp(u):
            h, kt, p, qbs, sl = u
            c0 = 512 * (qbs[0] % 2)
            c1 = 512 * (qbs[-1] % 2) + 512
            ps = ust["pt"] % 3
            ust["pt"] += 1
            mk.act(lambda e: e.activation(out=PT[ps][:, c0:c1], in_=SP2[sl][:, c0:c1], func=AF.Exp,
                                          bias=negc_all[:, kt, h:h + 1], scale=1.0),
                   reads=[bSP2[sl], bNC[kt // 4]], writes=[bPT[ps]])
            return ps

        def unit_pv(u, ps):
            h, kt, p, qbs, sl = u
            ci, kl = kt // 32, kt % 32
            for qb in qbs:
                col = 512 * (qb % 2)
                mk.pe(lambda e, qb=qb, col=col: e.matmul(O_ps[qb], VR[ci][:, kl, :], PT[ps][:, col:col + 512],
                                                         start=(kt == 0), stop=(kt == last_kt(qb))),
                      reads=[bVR[ci], bPT[ps]], writes=[bOp[qb]])

        RCPf = [ar.alloc(512, F32) for _ in range(2)]
        bRCPf = [B("RCPf0"), B("RCPf1")]

        def epilogue(h):
            fc, base = h // 2, 64 * (h % 2)
            for qb in range(4):
                r = qb % 2
                mk.dve(lambda e, qb=qb, r=r: e.reciprocal(out=RCPf[r][0:64, :], in_=O_ps[qb][64:128, :]),
                       reads=[bOp[qb]], writes=[bRCPf[r]])
                mk.dve(lambda e, qb=qb, r=r: e.tensor_tensor(out=OTall[base:base + 64, fc, 512 * qb:512 * (qb + 1)],
                                                             in0=O_ps[qb][0:64, :], in1=RCPf[r][0:64, :], op=ALU.mult),
                       reads=[bOp[qb], bRCPf[r]], writes=[bOT[fc]])

        prep_head(0)
        for ci in range(4):
            load_chunk(0, ci)
        per_head_pc = (len(precast) + 7) // 8
        for h in range(8):
            for ci in range(4):
                ulist = [(kt, p) for kt in range(32 * ci, 32 * ci + 32) for p in range(2) if kt <= last_kt(2 * p + 1)]
                units = {}
                n = len(ulist)
                for i in range(min(2, n)):
                    units[i] = unit_qk(h, ulist[i][0], ulist[i][1])
                for i in range(n):
                    ps = unit_exp(units[i])
                    if i + 2 < n:
                        units[i + 2] = unit_qk(h, ulist[i + 2][0], ulist[i + 2][1])
                    unit_pv(units[i], ps)
                if h + 1 < 8:
                    load_chunk(h + 1, ci)
                    if ci == 0:
                        prep_head(h + 1)
                if ci == 1:
                    emit_precast(per_head_pc)
            epilogue(h)

        mk.barrier()
        ar.off = persist_mark
        DQT2 = [ar.alloc(OWN, BF16) for _ in range(2)]; bDQT2 = [B("DQT0"), B("DQT1")]
        DKT2 = [ar.alloc(2 * OWN, BF16) for _ in range(2)]; bDKT2 = [B("DKT0"), B("DKT1")]
        DVT2 = [ar.alloc(2 * OWN, BF16) for _ in range(2)]; bDVT2 = [B("DVT0"), B("DVT1")]
        BIC2 = [ar.alloc(3 * 512, F32).rearrange("p (b n) -> p b n", n=512) for _ in range(2)]; bBIC2 = [B("BIC0"), B("BIC1")]
        BIP2 = [ar.alloc(3 * 512, F32).rearrange("p (b n) -> p b n", n=512) for _ in range(2)]; bBIP2 = [B("BIP0"), B("BIP1")]
        VD = ar.alloc(69 * 128, BF16).rearrange("p (t d) -> p t d", d=128)
        bVD = B("VD")
        SC2 = [ar.alloc(512, F32) for _ in range(2)]; bSC2 = [B("SC0"), B("SC1")]
        SPv2 = [ar.alloc(512, F32) for _ in range(2)]; bSPv2 = [B("SPv0"), B("SPv1")]
        PC2 = [ar.alloc(512, BF16) for _ in range(2)]; bPC2 = [B("PC0"), B("PC1")]
        PP2 = [ar.alloc(512, BF16) for _ in range(2)]; bPP2 = [B("PP0"), B("PP1")]
        ACCD = ar.alloc(OWN, F32); bACCD = B("ACCD")
        RCPd = ar.alloc(OWN, F32); bRCPd = B("RCPd")
        Sc_ps2 = [PSB[0], PSB[1]]; bScp2 = [B("Sc_ps0"), B("Sc_ps1")]
        Sp_ps2 = [PSB[2], PSB[3]]; bSpp2 = [B("Sp_ps0"), B("Sp_ps1")]
        OD_ps2 = [PSB[4], PSB[5]]; bODp2 = [B("OD_ps0"), B("OD_ps1")]
        TPv_ps = PSB[6].bitcast(BF16); bTPv = B("TPv")
        cst = {"n": 0}

        mk.dve(lambda e: e.memset(VD.rearrange("p t d -> p (t d)"), 1.0), writes=[bVD])
        allDQ = [DB("DQ", o) for o in range(4)]
        allDK = [DB("DK", o) for o in range(8)]
        allDV = [DB("DV", o) for o in range(8)]

        def kview(T, d):
            if d == 1:
                return T.rearrange("p (t i) -> p t i", i=128)
            if d == 4:
                return T.rearrange("p (m i r) -> p m r i", m=8, i=128, r=4)
            return T.rearrange("p (m i r) -> p m r i", m=2, i=128, r=16)

        def qview(T, d):
            if d == 1:
                return T.rearrange("p (t i) -> p t i", i=128)
            if d == 4:
                return T.rearrange("p (m i r) -> p m r i", m=4, i=128, r=4)
            return T.rearrange("p (i r) -> p r i", i=128, r=16)

        def c_loads(h):
            z = h % 2
            mk.dma(lambda e: e.dma_start(out=DQT2[z][0:64, :], in_=DQ_s[h]), reads=allDQ, writes=[bDQT2[z]])
            mk.dma(lambda e: e.dma_start(out=DKT2[z][0:64, :], in_=DK_s[h]), reads=allDK, writes=[bDKT2[z]])
            mk.dma(lambda e: e.dma_start(out=DVT2[z][0:64, :], in_=DV_s[h]), reads=allDV, writes=[bDVT2[z]])
            mk.dma(lambda e: e.dma_start(out=BIC2[z], in_=dbiasc_d[h].rearrange("b p n -> p b n")), writes=[bBIC2[z]])
            for rep in range(4):
                mk.dma(lambda e, rep=rep: e.dma_start(out=BIP2[z][:, :, 128 * rep:128 * (rep + 1)],
                                                      in_=dbiasp_d[h].rearrange("b p n -> p b n")), writes=[bBIP2[z]])

        def make_batch(h, bi, d, g, vidx):
            z = h % 2
            DQT, DKT, BIC, BIP = DQT2[z], DKT2[z], BIC2[z], BIP2[z]
            bDQT, bDKT, bBIC, bBIP = bDQT2[z], bDKT2[z], bBIC2[z], bBIP2[z]
            KV = kview(DKT[0:64, :], d)
            QV = qview(DQT[0:64, :], d)
            y = cst["n"] % 2
            cst["n"] += 1
            Sc_ps, bScp, Sp_ps, bSpp, OD_ps, bODp = Sc_ps2[y], bScp2[y], Sp_ps2[y], bSpp2[y], OD_ps2[y], bODp2[y]
            SC, bSC, SPv, bSPv, PC, bPC, PP, bPP = SC2[y], bSC2[y], SPv2[y], bSPv2[y], PC2[y], bPC2[y], PP2[y], bPP2[y]
            items = []
            for jj in range(4):
                if d == 1:
                    j = 4 * g + jj
                    items.append((QV[:, j, :], KV[:, 16 + j, :], KV[:, 15 + j, :], vidx[(1, 16 + j)], vidx[(1, 15 + j)], j == 0))
                elif d == 4:
                    m, r = g, jj
                    items.append((QV[:, m, r, :], KV[:, 4 + m, r, :], KV[:, 3 + m, r, :], vidx[(4, 4 + m, r)],
                                  vidx[(4, 3 + m, r)], m == 0))
                else:
                    r = 4 * g + jj
                    items.append((QV[:, r, :], KV[:, 1, r, :], KV[:, 0, r, :], vidx[(16, 1, r)], vidx[(16, 0, r)], True))

            def s1():
                for jj, it in enumerate(items):
                    mk.pe(lambda e, jj=jj, it=it: e.matmul(Sc_ps[:, 128 * jj:128 * (jj + 1)], it[1], it[0], start=True, stop=True),
                          reads=[bDKT, bDQT], writes=[bScp])
                for jj, it in enumerate(items):
                    mk.pe(lambda e, jj=jj, it=it: e.matmul(Sp_ps[:, 128 * jj:128 * (jj + 1)], it[2], it[0], start=True, stop=True),
                          reads=[bDKT, bDQT], writes=[bSpp])

            def s23():
                mk.dve(lambda e: e.tensor_tensor(out=SC, in0=Sc_ps, in1=BIC[:, bi, :], op=ALU.add),
                       reads=[bScp, bBIC], writes=[bSC])
                halo = [it[5] for it in items]
                if all(halo):
                    mk.dve(lambda e: e.scalar_tensor_tensor(out=SPv, in0=Sp_ps, scalar=HMASK[:, 0:1], in1=BIP[:, bi, :],
                                                            op0=ALU.add, op1=ALU.add), reads=[bSpp, bBIP, bC], writes=[bSPv])
                elif not any(halo):
                    mk.dve(lambda e: e.tensor_tensor(out=SPv, in0=Sp_ps, in1=BIP[:, bi, :], op=ALU.add),
                           reads=[bSpp, bBIP], writes=[bSPv])
                else:
                    assert halo == [True, False, False, False]
                    mk.dve(lambda e: e.scalar_tensor_tensor(out=SPv[:, 0:128], in0=Sp_ps[:, 0:128], scalar=HMASK[:, 0:1],
                                                            in1=BIP[:, bi, 0:128], op0=ALU.add, op1=ALU.add),
                           reads=[bSpp, bBIP, bC], writes=[bSPv])
                    mk.dve(lambda e: e.tensor_tensor(out=SPv[:, 128:512], in0=Sp_ps[:, 128:512], in1=BIP[:, bi, 128:512], op=ALU.add),
                           reads=[bSpp, bBIP], writes=[bSPv])
                mk.act(lambda e: e.activation(out=PC, in_=SC, func=AF.Exp), reads=[bSC], writes=[bPC])
                mk.act(lambda e: e.activation(out=PP, in_=SPv, func=AF.Exp), reads=[bSPv], writes=[bPP])

            def s45():
                for jj, it in enumerate(items):
                    sl_ = slice(128 * jj, 128 * (jj + 1))
                    mk.pe(lambda e, sl_=sl_, it=it: e.matmul(OD_ps[:, sl_], VD[:, it[3], :], PC[:, sl_], start=True, stop=False),
                          reads=[bVD, bPC], writes=[bODp])
                    mk.pe(lambda e, sl_=sl_, it=it: e.matmul(OD_ps[:, sl_], VD[:, it[4], :], PP[:, sl_], start=False, stop=True),
                          reads=[bVD, bPP], writes=[bODp])
                if d == 1:
                    oap = ACCD[:, 512 * g:512 * (g + 1)]
                    iap = OD_ps[:, :]
                elif d == 4:
                    oap = ACCD[:, 512 * g:512 * (g + 1)].rearrange("p (i r) -> p r i", r=4)
                    iap = OD_ps[:, :].rearrange("p (r i) -> p r i", r=4)
                else:
                    oap = ACCD[:, :].rearrange("p (i r) -> p r i", r=16)[:, 4 * g:4 * g + 4, :]
                    iap = OD_ps[:, :].rearrange("p (r i) -> p r i", r=4)
                if d == 1:
                    mk.dve(lambda e: e.tensor_copy(out=oap, in_=iap), reads=[bODp], writes=[bACCD])
                else:
                    mk.dve(lambda e: e.tensor_tensor(out=oap, in0=oap, in1=iap, op=ALU.add), reads=[bODp, bACCD], writes=[bACCD])
            return (s1, s23, s45)

        c_loads(0)
        for h in range(8):
            z = h % 2
            DVT, bDVT = DVT2[z], bDVT2[z]
            if h + 1 < 8:
                c_loads(h + 1)
            vidx = {}
            tiles = []
            for i in range(15, 32):
                tiles.append((1, (i,), kview(DVT[0:64, :], 1)[:, i, :]))
            for m in range(3, 8):
                for r in range(4):
                    tiles.append((4, (m, r), kview(DVT[0:64, :], 4)[:, m, r, :]))
            for m in range(2):
                for r in range(16):
                    tiles.append((16, (m, r), kview(DVT[0:64, :], 16)[:, m, r, :]))
            for n0 in range(0, len(tiles), 8):
                grp = tiles[n0:n0 + 8]
                for gi, (d, key, ap_) in enumerate(grp):
                    vidx[(d,) + key] = n0 + gi
                    mk.pe(lambda e, gi=gi, ap_=ap_: e.transpose(TPv_ps[:, 64 * gi:64 * (gi + 1)], ap_, IDB[0:64, 0:64]),
                          reads=[bDVT, bC2], writes=[bTPv])
                ng = len(grp)
                mk.dve(lambda e, n0=n0, ng=ng: e.tensor_copy(out=VD[:, n0:n0 + ng, 0:64],
                                                             in_=TPv_ps[:, 0:64 * ng].rearrange("p (t d) -> p t d", d=64)),
                       reads=[bTPv], writes=[bVD])
            batches = [make_batch(h, bi, d, g, vidx) for bi, d in enumerate((1, 4, 16)) for g in range(4)]
            nb = len(batches)
            for i in range(min(2, nb)):
                batches[i][0]()
                batches[i][1]()
            for i in range(nb):
                batches[i][2]()
                if i + 2 < nb:
                    batches[i + 2][0]()
                    batches[i + 2][1]()
            fc, base = 4 + h // 2, 64 * (h % 2)
            mk.act(lambda e: e.activation(out=RCPd[0:64, :], in_=ACCD[64:128, :], func=AF.Ln), reads=[bACCD], writes=[bRCPd])
            mk.act(lambda e: e.activation(out=RCPd[0:64, :], in_=RCPd[0:64, :], func=AF.Exp, scale=-1.0), reads=[bRCPd], writes=[bRCPd])
            mk.dve(lambda e, fc=fc, base=base: e.tensor_tensor(out=OTall[base:base + 64, fc, :], in0=ACCD[0:64, :],
                                                               in1=RCPd[0:64, :], op=ALU.mult),
                   reads=[bACCD, bRCPd], writes=[bOT[fc]])

        mk.barrier()
        ar.off = markD
        GPOST = ar.alloc(3 * 1024, F32).rearrange("p (g n) -> p g n", n=1024); bGP = B("GPOST")
        WOUT = ar.alloc(8 * 1024, BF16).rearrange("p (k n) -> p k n", n=1024); bWOUT = B("WOUT")
        WXQ = ar.alloc(8 * 256, BF16).rearrange("p (k n) -> p k n", n=256); bWXQ = B("WXQ")
        WXO = ar.alloc(2 * 1024, BF16).rearrange("p (k n) -> p k n", n=1024); bWXO = B("WXO")
        KMT = ar.alloc(2 * 256, BF16).rearrange("p (c n) -> p c n", n=256); bKMT = B("KMT")
        VMp = ar.alloc(2 * 4 * 128, BF16).rearrange("p (t h d) -> p t h d", t=2, h=4); bVMp = B("VMp")
        NWC = 6
        WC = [ar.alloc(1024, BF16) for _ in range(NWC)]
        bWC = [B(f"WC{i}") for i in range(NWC)]
        X2 = [ar.alloc(4 * 1024, F32).rearrange("p (t n) -> p t n", n=1024) for _ in range(2)]
        bX2 = [[B(f"X{z}_{t}") for t in range(4)] for z in range(2)]
        Hb2 = [ar.alloc(1024, BF16) for _ in range(2)]; bHb2 = [B("Hb0"), B("Hb1")]
        HTa = ar.alloc(8 * 512, BF16).rearrange("p (k n) -> p k n", n=512); bHTa = B("HTa")
        HT3 = [ar.alloc(8 * 512, BF16).rearrange("p (k n) -> p k n", n=512) for _ in range(2)]; bHT3 = [B("HT3_0"), B("HT3_1")]
        ATraw = ar.alloc(NJ * 512, BF16)
        ATt = ATraw.rearrange("p (j n) -> p j n", n=512); bAT = B("AT")
        WXK = ATraw[:, 0:2048].rearrange("p (k n) -> p k n", n=256); bWXK = B("WXK")
        WXV = ATraw[:, 2048:4096].rearrange("p (k n) -> p k n", n=256); bWXV = B("WXV")
        QM = ar.alloc(2 * 512, BF16).rearrange("p (c n) -> p c n", n=512); bQM = B("QM")
        OMT = ar.alloc(2 * 512, BF16).rearrange("p (c n) -> p c n", n=512); bOMT = B("OMT")
        Tn2 = [ar.alloc(1024, F32) for _ in range(2)]; bTn2 = [B("Tn0"), B("Tn1")]
        SQJ = ar.alloc(1024, BF16); bSQJ = B("SQJ")
        SSd2 = [ar.alloc(4, F32) for _ in range(2)]; bSSd2 = [B("SSd0"), B("SSd1")]
        PM2 = [ar.alloc(512, BF16) for _ in range(2)]; bPM2 = [B("PM0"), B("PM1")]
        SG2 = [ar.alloc(512, F32) for _ in range(2)]; bSG2 = [B("SG0"), B("SG1")]
        RCPm = ar.alloc(512, F32); bRCPm = B("RCPm")

        bB = [B(f"bank{i}") for i in range(8)]
        YB = 4
        TPd_ps = PSB[6].bitcast(BF16); bTPd = bB[6]
        Om_ps, bOm = PSB[6], bB[6]
        G_ps, bG = PSB[7], bB[7]
        Sm_ps, bSm = PSB[7], bB[7]
        G_ps2 = [PSB[0], PSB[2]]; bG2 = [bB[0], bB[2]]
        U_ps2 = [PSB[1], PSB[3]]; bU2 = [bB[1], bB[3]]

        wst = {"i": 0, "c": 0, "n": 0}
        allWb = [DB("Wb", i) for i in range(8 + 24 + 2 + 3 * NJ)]

        def load_wb(src_ap, dst_ap, dst_buf):
            mk.dma(lambda e: e.dma_start(out=dst_ap, in_=src_ap), reads=allWb, writes=[dst_buf])

        mk.dma(lambda e: e.dma_start(out=GPOST, in_=gpost_d.rearrange("g p n -> p g n")), writes=[bGP])
        load_wb(Wout_b.rearrange("(k p) n -> p k n", p=128), WOUT, bWOUT)
        load_wb(Wxq_b.rearrange("(k p) n -> p k n", p=128), WXQ, bWXQ)
        load_wb(Wxk_b.rearrange("(k p) n -> p k n", p=128), WXK, bWXK)
        load_wb(Wxv_b.rearrange("(k p) n -> p k n", p=128), WXV, bWXV)
        load_wb(Wxo_b.rearrange("(k p) n -> p k n", p=128), WXO, bWXO)

        def rstd_stages(st, parts, src_bufs, z):
            SSd, bSSd = SSd2[z], bSSd2[z]

            def s_a():
                mk.dve(lambda e: e.memset(SSd[:, 0:2], 0.0), reads=[], writes=[bSSd])
                off = 0
                for col, iap in enumerate(parts):
                    n = iap.shape[-1]
                    mk.act(lambda e, iap=iap, col=col, n=n, off=off: e.activation(out=SQJ[:, off:off + n], in_=iap, func=AF.Square,
                                                                                 accum_out=SSd[:, col:col + 1]),
                           reads=src_bufs + [bSSd], writes=[bSQJ, bSSd])
                    off += n

            def s_b():
                if len(parts) == 2:
                    mk.dve(lambda e: e.tensor_tensor(out=SSd[:, 0:1], in0=SSd[:, 0:1], in1=SSd[:, 1:2], op=ALU.add),
                           reads=[bSSd], writes=[bSSd])
                mk.act(lambda e: e.activation(out=SSd[:, 2:3], in_=SSd[:, 0:1], func=AF.Ln, bias=EPS, scale=1.0 / 1024),
                       reads=[bSSd], writes=[bSSd])
                mk.act(lambda e: e.activation(out=SSd[:, 3:4], in_=SSd[:, 2:3], func=AF.Exp, scale=-0.5),
                       reads=[bSSd], writes=[bSSd])
            st.append(s_a)
            st.append(s_b)

        def post_norm_stages(st, X, t, gi, xbuf, ba):
            z = wst["n"] % 2
            wst["n"] += 1
            ybufs = [bB[ba], bB[ba + 1]]
            rstd_stages(st, [PSB[ba], PSB[ba + 1]], ybufs, z)
            Tn, bTn = Tn2[z], bTn2[z]

            def s_c():
                for hh in range(2):
                    mk.dve(lambda e, hh=hh: e.scalar_tensor_tensor(out=Tn[:, 512 * hh:512 * (hh + 1)], in0=PSB[ba + hh],
                                                                   scalar=SSd2[z][:, 3:4], in1=GPOST[:, gi, 512 * hh:512 * (hh + 1)],
                                                                   op0=ALU.mult, op1=ALU.mult),
                           reads=[bB[ba + hh], bSSd2[z], bGP], writes=[bTn])

            def s_d():
                mk.dve(lambda e: e.tensor_tensor(out=X[:, t, :], in0=X[:, t, :], in1=Tn, op=ALU.add),
                       reads=[bTn, xbuf], writes=[xbuf])
            st.append(s_c)
            st.append(s_d)

        def pre_norm_stages(st, src_ap, src_bufs, dstT, dst_buf, col0):
            z = wst["n"] % 2
            wst["n"] += 1
            rstd_stages(st, [src_ap], src_bufs, z)
            Hb, bHb = Hb2[z], bHb2[z]

            def s_c():
                mk.dve(lambda e: e.tensor_scalar(out=Hb, in0=src_ap, scalar1=SSd2[z][:, 3:4], scalar2=None, op0=ALU.mult),
                       reads=src_bufs + [bSSd2[z]], writes=[bHb])

            def s_d():
                for kc in range(8):
                    mk.pe(lambda e, kc=kc: e.transpose(TPd_ps[:, 128 * kc:128 * (kc + 1)], Hb[:, 128 * kc:128 * (kc + 1)], IDB),
                          reads=[bHb, bC2], writes=[bTPd])

            def s_e():
                mk.act(lambda e: e.activation(out=dstT[:, :, col0:col0 + 128], in_=TPd_ps.rearrange("p (k n) -> p k n", n=128),
                                              func=AF.Copy), reads=[bTPd], writes=[dst_buf])
            st.append(s_c)
            st.append(s_d)
            st.append(s_e)

        def run_all(st):
            for f in st:
                f()

        X0 = X2[0]
        HMT = HTa
        st0 = []
        for mt in range(2):
            mk.dma(lambda e, mt=mt: e.dma_start(out=X0[:, mt, :], in_=memx[128 * mt:128 * (mt + 1), :]), writes=[bX2[0][mt]])
            pre_norm_stages(st0, X0[:, mt, :], [bX2[0][mt]], HMT, bHTa, 128 * mt)
        run_all(st0)
        for c2 in range(2):
            for kc in range(8):
                mk.pe(lambda e, kc=kc, c2=c2: e.matmul(G_ps[:, 0:256], WXK[:, kc, 128 * c2:128 * (c2 + 1)], HMT[:, kc, 0:256],
                                                        start=(kc == 0), stop=(kc == 7)),
                      reads=[bWXK, bHTa], writes=[bG])
            mk.act(lambda e, c2=c2: e.activation(out=KMT[:, c2, :], in_=G_ps[:, 0:256], func=AF.Copy),
                   reads=[bG], writes=[bKMT])
        mk.dve(lambda e: e.memset(VMp.rearrange("p t h d -> p (t h d)"), 1.0), writes=[bVMp])
        for mt in range(2):
            for kc in range(8):
                mk.pe(lambda e, kc=kc, mt=mt: e.matmul(Om_ps[:, 0:256], HMT[:, kc, 128 * mt:128 * (mt + 1)], WXV[:, kc, :],
                                                        start=(kc == 0), stop=(kc == 7)),
                      reads=[bWXV, bHTa], writes=[bOm])
            mk.act(lambda e, mt=mt: e.activation(out=VMp[:, mt, :, 0:64],
                                                 in_=Om_ps[:, 0:256].rearrange("p (h d) -> p h d", d=64), func=AF.Copy),
                   reads=[bOm], writes=[bVMp])
        mk.dve(lambda e: e.memset(ATraw[:, 0:8], 0.0), reads=[bKMT, bVMp], writes=[bAT, bWXK, bWXV])

        def front_stages(tb, xs):
            st = []
            X, bX = X2[xs], bX2[xs]

            def s_load():
                for t in range(4):
                    gt = 4 * tb + t
                    mk.dma(lambda e, t=t, gt=gt: e.dma_start(out=X[:, t, :], in_=x_own[128 * gt:128 * (gt + 1), :]),
                           writes=[bX[t]])
            st.append(s_load)
            for t in range(4):
                gt = 4 * tb + t
                for hh in range(2):
                    def s_wout(gt=gt, hh=hh):
                        for kc in range(8):
                            mk.pe(lambda e, kc=kc: e.matmul(PSB[YB + hh], OTall[:, kc, 128 * gt:128 * (gt + 1)],
                                                            WOUT[:, kc, 512 * hh:512 * (hh + 1)], start=(kc == 0), stop=(kc == 7)),
                                  reads=[bOT[kc], bWOUT], writes=[bB[YB + hh]])
                    st.append(s_wout)
                post_norm_stages(st, X, t, 0, bX[t], YB)
            for t in range(4):
                pre_norm_stages(st, X[:, t, :], [bX[t]], HTa, bHTa, 128 * t)
            for c2 in range(2):
                def s_q(c2=c2):
                    for kc in range(8):
                        mk.pe(lambda e, kc=kc: e.matmul(G_ps, WXQ[:, kc, 128 * c2:128 * (c2 + 1)], HTa[:, kc, :],
                                                        start=(kc == 0), stop=(kc == 7)),
                              reads=[bWXQ, bHTa], writes=[bG])

                def s_q2(c2=c2):
                    mk.act(lambda e: e.activation(out=QM[:, c2, :], in_=G_ps, func=AF.Copy, scale=0.125),
                           reads=[bG], writes=[bQM])
                st.append(s_q)
                st.append(s_q2)
            for hm in range(4):
                c2, base = hm // 2, 64 * (hm % 2)
                for mt in range(2):
                    def s_qk(c2=c2, base=base, mt=mt):
                        mk.pe(lambda e: e.matmul(Sm_ps, KMT[base:base + 64, c2, 128 * mt:128 * (mt + 1)], QM[base:base + 64, c2, :],
                                                 start=True, stop=True), reads=[bKMT, bQM], writes=[bSm])

                    def s_ex(mt=mt):
                        mk.act(lambda e: e.activation(out=PM2[mt], in_=Sm_ps, func=AF.Exp), reads=[bSm], writes=[bPM2[mt]])

                    def s_pv(mt=mt, hm=hm):
                        mk.pe(lambda e: e.matmul(Om_ps, VMp[:, mt, hm, :], PM2[mt], start=(mt == 0), stop=(mt == 1)),
                              reads=[bVMp, bPM2[mt]], writes=[bOm])
                    st.append(s_qk)
                    st.append(s_ex)
                    st.append(s_pv)

                def s_nrm(c2=c2, base=base):
                    mk.act(lambda e: e.activation(out=RCPm[0:64, :], in_=Om_ps[64:128, :], func=AF.Ln), reads=[bOm], writes=[bRCPm])
                    mk.act(lambda e: e.activation(out=RCPm[0:64, :], in_=RCPm[0:64, :], func=AF.Exp, scale=-1.0),
                           reads=[bRCPm], writes=[bRCPm])
                    mk.dve(lambda e: e.tensor_tensor(out=OMT[base:base + 64, c2, :], in0=Om_ps[0:64, :],
                                                     in1=RCPm[0:64, :], op=ALU.mult),
                           reads=[bOm, bRCPm], writes=[bOMT])
                st.append(s_nrm)
            for t in range(4):
                for hh in range(2):
                    def s_wxo(t=t, hh=hh):
                        for c2 in range(2):
                            mk.pe(lambda e, c2=c2: e.matmul(PSB[YB + hh], OMT[:, c2, 128 * t:128 * (t + 1)],
                                                            WXO[:, c2, 512 * hh:512 * (hh + 1)], start=(c2 == 0), stop=(c2 == 1)),
                                  reads=[bOMT, bWXO], writes=[bB[YB + hh]])
                    st.append(s_wxo)
                post_norm_stages(st, X, t, 1, bX[t], YB)
            for t in range(4):
                pre_norm_stages(st, X[:, t, :], [bX[t]], HT3[xs], bHT3[xs], 128 * t)
            return st

        out_ops = []

        def ffn(tb, xs, side):
            X, bX = X2[xs], bX2[xs]
            HTd, bHTd = HT3[xs], bHT3[xs]

            def drain(n):
                for _ in range(n):
                    if side:
                        side.pop(0)()
            for j in range(NJ):
                y = j % 2
                wg = wst["c"] % NWC
                wst["c"] += 1
                load_wb(Wg_b[j], WC[wg], bWC[wg])
                wu = wst["c"] % NWC
                wst["c"] += 1
                load_wb(Wu_b[j], WC[wu], bWC[wu])
                for kc in range(8):
                    mk.pe(lambda e, kc=kc, wg=wg, y=y: e.matmul(G_ps2[y], WC[wg][:, 128 * kc:128 * (kc + 1)], HTd[:, kc, :],
                                                                start=(kc == 0), stop=(kc == 7)),
                          reads=[bWC[wg], bHTd], writes=[bG2[y]])
                    if kc % 4 == 3:
                        drain(1)
                for kc in range(8):
                    mk.pe(lambda e, kc=kc, wu=wu, y=y: e.matmul(U_ps2[y], WC[wu][:, 128 * kc:128 * (kc + 1)], HTd[:, kc, :],
                                                                start=(kc == 0), stop=(kc == 7)),
                          reads=[bWC[wu], bHTd], writes=[bU2[y]])
                    if kc % 4 == 3:
                        drain(1)
                mk.act(lambda e, y=y: e.activation(out=SG2[y], in_=G_ps2[y], func=AF.Exp, scale=-1.0), reads=[bG2[y]], writes=[bSG2[y]])
                mk.act(lambda e, y=y: e.activation(out=SG2[y], in_=SG2[y], func=AF.Ln, bias=1.0, scale=1.0), reads=[bSG2[y]], writes=[bSG2[y]])
                mk.act(lambda e, y=y: e.activation(out=SG2[y], in_=SG2[y], func=AF.Exp, scale=-1.0), reads=[bSG2[y]], writes=[bSG2[y]])
                mk.dve(lambda e, y=y: e.tensor_tensor(out=SG2[y], in0=SG2[y], in1=G_ps2[y], op=ALU.mult),
                       reads=[bSG2[y], bG2[y]], writes=[bSG2[y]])
                mk.dve(lambda e, j=j, y=y: e.tensor_tensor(out=ATt[:, j, :], in0=SG2[y], in1=U_ps2[y], op=ALU.mult),
                       reads=[bSG2[y], bU2[y]], writes=[bAT])
                drain(1)
            for half in range(2):
                for j in range(NJ):
                    wd = wst["c"] % NWC
                    wst["c"] += 1
                    load_wb(Wd_b[j], WC[wd], bWC[wd])
                    for tt in range(2):
                        t = 2 * half + tt
                        for hh in range(2):
                            mk.pe(lambda e, t=t, tt=tt, j=j, hh=hh, wd=wd: e.matmul(PSB[2 * tt + hh], ATt[:, j, 128 * t:128 * (t + 1)],
                                                                                  WC[wd][:, 512 * hh:512 * (hh + 1)],
                                                                                  start=(j == 0), stop=(j == NJ - 1)),
                                  reads=[bAT, bWC[wd]], writes=[bB[2 * tt + hh]])
                    drain(1)
                while side:
                    side.pop(0)()
                for tt in range(2):
                    t = 2 * half + tt
                    stf = []
                    post_norm_stages(stf, X, t, 2, bX[t], 2 * tt)
                    run_all(stf)
                    gt = 4 * tb + t
                    out_ops.append(mk.dma(lambda e, t=t, gt=gt: e.dma_start(out=out_d[128 * gt:128 * (gt + 1), :], in_=X[:, t, :]),
                                          reads=[bX[t]], eng="sp"))
            while side:
                side.pop(0)()

        run_all(front_stages(0, 0))
        for tb in range(4):
            side = front_stages(tb + 1, (tb + 1) % 2) if tb + 1 < 4 else []
            ffn(tb, tb % 2, side)

        fin = list(out_ops)
        if debug:
            fin = [o for o in mk.ops if o.is_dma]
        seen = set()
        fin2 = []
        for o in fin:
            if id(o.key) not in seen:
                seen.add(id(o.key))
                fin2.append(o)
        mk.emit(final_ops=fin2)
    return nc


def _t5_bucket(dist):
    n_buckets, max_distance = 32, 2048
    max_exact = n_buckets // 2
    d = np.maximum(dist, 1).astype(np.float32)
    large = max_exact + (np.log(d / max_exact) / np.log(max_distance / max_exact)
                         * (n_buckets - max_exact)).astype(np.int32)
    large = np.minimum(large, n_buckets - 1)
    return np.where(dist < max_exact, dist, large).astype(np.int32)


def _prep_inputs(x, mem, g_mix_pre, w_in, b_f, rel_bias, w_out, g_mix_post, g_xattn_pre, g_mem,
                 w_xq, w_xk, w_xv, w_xo, g_xattn_post, g_ffn_pre, w_gate, w_up, w_down, g_ffn_post):
    f = lambda a: np.ascontiguousarray(np.asarray(a, dtype=np.float32))
    x = f(x)[0]
    mem = f(mem)[0]
    w_in = f(w_in)[0]
    xTn = np.ascontiguousarray(x.T)
    fq, fk, fv = w_in[:, 0:512], w_in[:, 512:1024], w_in[:, 1024:1536]
    gate = w_in[:, 1536:1544]
    dq, dk, dv = w_in[:, 1544:2056], w_in[:, 2056:2568], w_in[:, 2568:3080]
    w_a = np.ascontiguousarray(np.concatenate([fk, fv, fq, dq, dk, dv, gate], axis=1))

    def pk(g):
        return f(g)[0].reshape(8, 128).T
    gpre = np.ascontiguousarray(np.stack([pk(g_mix_pre), pk(g_xattn_pre), pk(g_mem), pk(g_ffn_pre)], 0))
    gpost = np.ascontiguousarray(np.stack([np.broadcast_to(f(g)[0][None, :], (128, 1024))
                                           for g in (g_mix_post, g_xattn_post, g_ffn_post)], 0))
    rb = f(rel_bias)
    p = np.arange(128)[:, None]
    q = np.arange(128)[None, :]
    dbiasc = np.empty((8, 3, 128, 512), np.float32)
    dbiasp = np.empty((8, 3, 128, 128), np.float32)
    for bi, d in enumerate((1, 4, 16)):
        dc = q - p
        dp = q + 128 - p
        bc = _t5_bucket(np.clip(dc, 0, 128) * d)
        bp = _t5_bucket(np.clip(dp, 0, 128) * d)
        for h in range(8):
            tc_ = np.where(dc >= 0, rb[bc, h], np.float32(NEG)).astype(np.float32)
            tp_ = np.where(dp <= 128, rb[bp, h], np.float32(NEG)).astype(np.float32)
            dbiasc[h, bi] = np.tile(tc_, (1, 4))
            dbiasp[h, bi] = tp_
    qq = np.arange(512)[None, :]
    negm = np.stack([np.where(qq >= 128 * j + p, 0.0, NEG).astype(np.float32) for j in range(4)], 0)
    ident = np.eye(128, dtype=np.float32)
    msel = np.zeros((128, 3), np.float32)
    for r in range(3):
        msel[64 + r, r] = 1.0
    common = {
        "memx": mem, "w_a": w_a, "gpre": gpre, "bf": f(b_f)[0].reshape(8, 1), "dbiasc": dbiasc, "dbiasp": dbiasp,
        "negm": negm, "ident": ident, "msel": msel, "w_out": f(w_out)[0], "w_xq": f(w_xq)[0], "w_xk": f(w_xk)[0],
        "w_xv": f(w_xv)[0], "w_xo": f(w_xo)[0], "w_gate": f(w_gate)[0], "w_up": f(w_up)[0], "w_down": f(w_down)[0],
        "gpost": gpost,
    }
    in_maps = []
    for c in range(NCORE):
        roll = (c + 1) * OWN
        p0 = S - roll
        idx = (np.arange(S) + roll) % S
        m = dict(common)
        m["xT"] = np.ascontiguousarray(xTn[:, idx])
        m["x_own"] = np.ascontiguousarray(x[c * OWN:(c + 1) * OWN])
        pos = np.arange(S).reshape(128, 128).T
        m["keymask"] = np.where(pos >= p0, 0.0, NEG).astype(np.float32)
        m["halomask"] = np.full((128, 1), NEG if c == 0 else 0.0, np.float32)
        in_maps.append(m)
    return in_maps


_CACHE = {}


def kernel(**inputs):
    in_maps = _prep_inputs(**inputs)
    if "nc" not in _CACHE:
        _CACHE["nc"] = build_program(debug=False)
    nc = _CACHE["nc"]
    res = run_bass_kernel_spmd(nc, in_maps, core_ids=list(range(NCORE)))
    outs = [np.asarray(r["out"], dtype=np.float32) for r in res.results]
    return np.concatenate(outs, axis=0)[None, :, :]
```
